# Optimizing a Trainium2 kernel written in Bass

```python
import jax, jax.numpy as jnp
from jax import lax
import numpy as np


D_MODEL = 1024
BATCH = 8
SEQ = 2048
DEPTH = 1
DEC_BATCH = 128
DEC_SEQ = 4
PAST_LEN = 16384
PAGE_SIZE = 128

D_MIX = D_MODEL
D_A = D_MIX // 2
HEAD_A = 64
H_A = D_A // HEAD_A
D_R = D_MIX - D_A
H_R = 4
HEAD_R = D_R // H_R
LORA_W = 64
LORA_A = 64
LORA_G = 128
D_FF = 2816
RET_CHUNK = 128
ROPE_BASE = 10000.0
EPS = 1e-6
GN_EPS_A = 64e-5
GN_EPS_R = 1e-5
N_SHIFT = 3 * D_A + LORA_W + LORA_A + LORA_G
N_COLS = N_SHIFT + 4 * D_R

kernel_name = 'hymba_rwkv7_retnet_macaron_step'


def rms_norm(x, g):
    xf = x.astype(jnp.float32)
    y = xf * lax.rsqrt(jnp.mean(xf * xf, axis=-1, keepdims=True) + EPS)
    return (y * g.astype(jnp.float32)).astype(x.dtype)


def swiglu(h, wg, wu, wd):
    return (jax.nn.silu(h @ wg) * (h @ wu)) @ wd


def head_norm(y, eps):
    mu = jnp.mean(y, axis=-1, keepdims=True)
    var = jnp.mean(jnp.square(y - mu), axis=-1, keepdims=True)
    yn = (y - mu) * lax.rsqrt(var + eps)
    return yn.reshape(y.shape[0], y.shape[1], -1)


def rope(x, pos):
    half = x.shape[-1] // 2
    inv = ROPE_BASE ** (-jnp.arange(half, dtype=jnp.float32) / half)
    ang = pos.astype(jnp.float32)[:, None] * inv[None, :]
    cos = jnp.cos(ang)[None, :, None, :]
    sin = jnp.sin(ang)[None, :, None, :]
    x1, x2 = x[..., :half], x[..., half:]
    return jnp.concatenate([x1 * cos - x2 * sin, x1 * sin + x2 * cos], axis=-1)


def rwkv7_group(mixed, s0, w0, w2, a0, a2, g2, k_k, k_a, r_k, lnx_w, lnx_b):
    B, T, _ = mixed.shape
    f = mixed.astype(jnp.float32)
    o1, o2, o3 = D_A, 2 * D_A, 3 * D_A
    o4, o5 = o3 + LORA_W, o3 + LORA_W + LORA_A
    r, k, v = f[..., :o1], f[..., o1:o2], f[..., o2:o3]
    wd, ad, gd = f[..., o3:o4], f[..., o4:o5], f[..., o5:]
    w = -jax.nn.softplus(-(w0 + jnp.tanh(wd) @ w2)) - 0.5
    decay = jnp.exp(-jnp.exp(w))
    a = jax.nn.sigmoid(a0 + ad @ a2)
    g = jax.nn.sigmoid(gd) @ g2
    hd = lambda t: t.reshape(B, T, H_A, HEAD_A)
    kk = hd(k * k_k)
    kk = kk / jnp.maximum(jnp.sqrt(jnp.sum(kk * kk, axis=-1, keepdims=True)), 1e-12)
    k = k * (1.0 + (a - 1.0) * k_a)
    rh, kh, vh = hd(r), hd(k), hd(v)
    a_vec = -kk
    b_vec = kk * hd(a)
    xs = tuple(t.transpose(1, 0, 2, 3) for t in (rh, hd(decay), kh, vh, a_vec, b_vec))

    def step(S, inp):
        r_t, w_t, k_t, v_t, a_t, b_t = inp
        sa = jnp.einsum('bhij,bhj->bhi', S, a_t)
        S = S * w_t[:, :, None, :] + sa[..., None] * b_t[:, :, None, :] + v_t[..., None] * k_t[:, :, None, :]
        y = jnp.einsum('bhij,bhj->bhi', S, r_t)
        return S, y

    S, ys = lax.scan(step, s0.astype(jnp.float32), xs)
    ys = ys.transpose(1, 0, 2, 3)
    y = head_norm(ys, GN_EPS_A) * lnx_w + lnx_b
    bonus = (jnp.sum(rh * kh * r_k, axis=-1, keepdims=True) * vh).reshape(B, T, D_A)
    y = (y + bonus) * g
    return y.astype(mixed.dtype), S


def retention_chunked(q, k, v, s0):
    B, T, H, D = q.shape
    C = min(RET_CHUNK, T)
    n = T // C
    lg = jnp.log1p(-jnp.exp2(-5.0 - jnp.arange(H, dtype=jnp.float32)))
    idx = jnp.arange(C, dtype=jnp.float32)
    diff = idx[:, None] - idx[None, :]
    dmask = jnp.where(diff >= 0, jnp.exp(lg[:, None, None] * jnp.maximum(diff, 0.0)), 0.0)
    q_dec = jnp.exp(lg[:, None] * (idx + 1.0))
    k_dec = jnp.exp(lg[:, None] * (C - 1.0 - idx))
    c_dec = jnp.exp(lg * C)

    def to_chunks(t):
        return t.reshape(B, n, C, H, D).transpose(1, 0, 3, 2, 4)

    def step(S, inp):
        qc, kc, vc = inp
        inner = jnp.einsum('bhid,bhjd->bhij', qc, kc) * dmask
        y = jnp.einsum('bhij,bhje->bhie', inner, vc) + jnp.einsum('bhid,bhde->bhie', qc * q_dec[..., None], S)
        S = S * c_dec[:, None, None] + jnp.einsum('bhjd,bhje->bhde', kc * k_dec[..., None], vc)
        return S, y

    S, ys = lax.scan(step, s0.astype(jnp.float32), (to_chunks(q), to_chunks(k), to_chunks(v)))
    y = ys.transpose(1, 0, 3, 2, 4).reshape(B, T, H, D)
    return y, S


def retention_group(pr, s0, pos, gn_w):
    B, T, _ = pr.shape
    f = pr.astype(jnp.float32)
    hd = lambda t: t.reshape(B, T, H_R, HEAD_R)
    q = rope(hd(f[..., :D_R]), pos)
    k = rope(hd(f[..., D_R:2 * D_R]), pos) * (HEAD_R ** -0.5)
    v = hd(f[..., 2 * D_R:3 * D_R])
    g = f[..., 3 * D_R:]
    y, S = retention_chunked(q, k, v, s0)
    y = head_norm(y, GN_EPS_R) * gn_w
    y = jax.nn.silu(g) * y
    return y.astype(pr.dtype), S


def hybrid_layer(x, prev_h, wkv0, ret0, pos, p):
    (norm_g, f1g, f1u, f1d, w_in, mu, w0, w2, a0, a2, g2, k_k, k_a, r_k,
     lnx_w, lnx_b, gn_w, w_out, f2g, f2u, f2d) = p
    x = x + 0.5 * rms_norm(swiglu(rms_norm(x, norm_g[0]), f1g, f1u, f1d), norm_g[1])
    h = rms_norm(x, norm_g[2])
    h_ext = jnp.concatenate([prev_h[:, None, :].astype(h.dtype), h], axis=1)
    ps = h_ext @ w_in[:, :N_SHIFT]
    cur, prv = ps[:, 1:], ps[:, :-1]
    mixed = cur + (prv - cur) * mu
    ya, wkv_new = rwkv7_group(mixed, wkv0, w0, w2, a0, a2, g2, k_k, k_a, r_k, lnx_w, lnx_b)
    pr = h @ w_in[:, N_SHIFT:]
    yr, ret_new = retention_group(pr, ret0, pos, gn_w)
    mix = jnp.concatenate([ya, yr], axis=-1) @ w_out
    x = x + rms_norm(mix, norm_g[3])
    x = x + 0.5 * rms_norm(swiglu(rms_norm(x, norm_g[4]), f2g, f2u, f2d), norm_g[5])
    return x, h[:, -1], wkv_new, ret_new


def setup_inputs(seed: int = 0) -> dict:
    key = jax.random.key(seed)
    ks = jax.random.split(key, 26)
    nrm = lambda k, shape, s: s * jax.random.normal(k, shape, jnp.float32)
    return {
        'x_prompt': nrm(ks[0], (BATCH, SEQ, D_MODEL), 1.0),
        'x_sample': nrm(ks[1], (DEC_BATCH, DEC_SEQ, D_MODEL), 1.0),
        'state_shift': nrm(ks[2], (DEPTH, DEC_BATCH, D_MODEL), 1.0),
        'state_wkv': nrm(ks[3], (DEPTH, DEC_BATCH, H_A, HEAD_A, HEAD_A), 0.3),
        'state_ret': nrm(ks[4], (DEPTH, DEC_BATCH, H_R, HEAD_R, HEAD_R), 1.0),
        'norm_g': 1.0 + nrm(ks[5], (DEPTH, 6, D_MODEL), 0.05),
        'ffn1_wg': nrm(ks[6], (DEPTH, D_MODEL, D_FF), D_MODEL ** -0.5),
        'ffn1_wu': nrm(ks[7], (DEPTH, D_MODEL, D_FF), D_MODEL ** -0.5),
        'ffn1_wd': nrm(ks[8], (DEPTH, D_FF, D_MODEL), D_FF ** -0.5),
        'w_in': nrm(ks[9], (DEPTH, D_MODEL, N_COLS), D_MODEL ** -0.5),
        'mu_shift': jax.random.uniform(ks[10], (DEPTH, N_SHIFT), jnp.float32),
        'w0': jnp.linspace(-5.0, 0.5, D_A, dtype=jnp.float32)[None, :] + nrm(ks[11], (DEPTH, D_A), 0.1),
        'w2': nrm(ks[12], (DEPTH, LORA_W, D_A), LORA_W ** -0.5),
        'a0': nrm(ks[13], (DEPTH, D_A), 0.1),
        'a2': nrm(ks[14], (DEPTH, LORA_A, D_A), LORA_A ** -0.5),
        'g2': nrm(ks[15], (DEPTH, LORA_G, D_A), LORA_G ** -0.5),
        'k_k': 0.85 + nrm(ks[16], (DEPTH, D_A), 0.05),
        'k_a': 1.0 + nrm(ks[17], (DEPTH, D_A), 0.05),
        'r_k': nrm(ks[18], (DEPTH, H_A, HEAD_A), 0.1),
        'lnx_w': 1.0 + nrm(ks[19], (DEPTH, D_A), 0.05),
        'lnx_b': nrm(ks[20], (DEPTH, D_A), 0.02),
        'ret_gn_w': 1.0 + nrm(ks[21], (DEPTH, D_R), 0.05),
        'w_out': nrm(ks[22], (DEPTH, D_MIX, D_MODEL), D_MIX ** -0.5),
        'ffn2_wg': nrm(ks[23], (DEPTH, D_MODEL, D_FF), D_MODEL ** -0.5),
        'ffn2_wu': nrm(ks[24], (DEPTH, D_MODEL, D_FF), D_MODEL ** -0.5),
        'ffn2_wd': nrm(ks[25], (DEPTH, D_FF, D_MODEL), D_FF ** -0.5),
    }


def reference(x_prompt, x_sample, state_shift, state_wkv, state_ret, norm_g, ffn1_wg, ffn1_wu, ffn1_wd,
              w_in, mu_shift, w0, w2, a0, a2, g2, k_k, k_a, r_k, lnx_w, lnx_b, ret_gn_w, w_out,
              ffn2_wg, ffn2_wu, ffn2_wd):
    Bp, Tp, _ = x_prompt.shape
    Ts = x_sample.shape[1]
    pos_p = jnp.arange(Tp, dtype=jnp.int32)
    pos_s = PAST_LEN + jnp.arange(Ts, dtype=jnp.int32)
    yp, ys = x_prompt, x_sample
    sh_p, wk_p, rt_p, sh_s, wk_s, rt_s = [], [], [], [], [], []
    for l in range(DEPTH):
        p = (norm_g[l], ffn1_wg[l], ffn1_wu[l], ffn1_wd[l], w_in[l], mu_shift[l], w0[l], w2[l], a0[l],
             a2[l], g2[l], k_k[l], k_a[l], r_k[l], lnx_w[l], lnx_b[l], ret_gn_w[l], w_out[l],
             ffn2_wg[l], ffn2_wu[l], ffn2_wd[l])
        zero_shift = jnp.zeros((Bp, D_MODEL), x_prompt.dtype)
        zero_wkv = jnp.zeros((Bp, H_A, HEAD_A, HEAD_A), jnp.float32)
        zero_ret = jnp.zeros((Bp, H_R, HEAD_R, HEAD_R), jnp.float32)
        yp, a1, b1, c1 = hybrid_layer(yp, zero_shift, zero_wkv, zero_ret, pos_p, p)
        ys, a2_, b2, c2 = hybrid_layer(ys, state_shift[l], state_wkv[l], state_ret[l], pos_s, p)
        sh_p.append(a1); wk_p.append(b1); rt_p.append(c1)
        sh_s.append(a2_); wk_s.append(b2); rt_s.append(c2)
    shift_prompt = jnp.stack(sh_p)
    wkv_prompt = jnp.stack(wk_p)
    ret_prompt = jnp.stack(rt_p)
    shift_sample = jnp.stack(sh_s)
    wkv_sample = jnp.stack(wk_s)
    ret_sample = jnp.stack(rt_s)
    return (yp, ys, shift_prompt, wkv_prompt, ret_prompt, shift_sample, wkv_sample, ret_sample)
```

```python
import contextlib
import numpy as np
import concourse.bass as bass
import concourse.mybir as mybir
from concourse.bass_utils import run_bass_kernel_spmd

F32 = mybir.dt.float32
BF16 = mybir.dt.bfloat16
ALU = mybir.AluOpType
AF = mybir.ActivationFunctionType
AX = mybir.AxisListType

SAME_ENG_SYNC = True
SCHED = True
HOIST = False
HOIST_F2 = False
SCHED_XL = 0.85
MM_OVH = 0.08
SCHED_TAGS = ("M1", "M2", "F")
SAME_ENG_MIN_DIST = 0
ANNOTATE = False
D = 1024
DFF = 2816
NCH = 22
NRING = 4
EPS = 1e-6
C0 = float(np.exp(-0.5))


class Prog:
    ENGS = ("pe", "act", "dve", "pool", "sp")

    def __init__(self, nc):
        self.nc = nc
        self.ops = []
        self.writers = {}
        self.readers = {}
        self.last = {}
        self.dma_pending = []
        self.bar_nop = {}
        self.last_q = {}
        self.seg_start = 0

    @staticmethod
    def _key(a):
        if isinstance(a, (str, tuple)):
            return a
        if "DRam" in type(a.tensor).__name__:
            return None
        return a.name

    def op(self, eng, fn, r=(), w=(), dma=None, extra=(), cost=0.3, lat=0.0, tbl=None):
        idx = len(self.ops)
        rk = [k for k in (self._key(a) for a in r) if k is not None]
        wk = [k for k in (self._key(a) for a in w) if k is not None]
        deps = set(extra)
        for k in rk:
            deps.update(self.writers.get(k, ()))
            if isinstance(k, str) and k.startswith("ps"):
                deps.update(j for j in self.readers.get(k, ()) if self.ops[j]["eng"] != eng)
        for k in wk:
            deps.update(self.writers.get(k, ()))
            deps.update(self.readers.get(k, ()))
        if SCHED:
            b = self.bar_nop.get(eng)
            if b is not None:
                deps.add(b)
        self.ops.append(dict(eng=eng, fn=fn, deps=deps, dma=dma, tag=getattr(self, "tag", ""), cost=cost, lat=lat, tbl=tbl))
        if eng in ("sp", "pool"):
            self.last_q[eng] = idx
        for k in rk:
            if k in wk:
                continue
            lst = self.readers.setdefault(k, [])
            if dma is None and not SCHED:
                lst[:] = [j for j in lst if not (self.ops[j]["eng"] == eng and self.ops[j]["dma"] is None)]
            lst.append(idx)
        for k in wk:
            self.writers[k] = [idx]
            self.readers[k] = []
        if dma is None:
            self.last[eng] = idx
        else:
            self.dma_pending.append(idx)
        return idx

    def barrier(self):
        if getattr(self, "_bar_at", -1) == len(self.ops):
            return
        lasts = dict(self.last)
        pend = list(self.dma_pending)
        self.dma_pending = []
        if SCHED:
            allprev = list(range(self.seg_start, len(self.ops)))
        for e in self.ENGS:
            ex = [v for (q, v) in lasts.items()] + pend
            if SCHED:
                ex = allprev
            i_ = self.op(e, lambda eng: eng.nop(), extra=ex, cost=0.05)
            self.ops[i_]["bar"] = True
            self.bar_nop[e] = i_
        self._bar_at = len(self.ops)
        self.seg_start = len(self.ops)

    def schedule(self):
        ops = self.ops
        n = len(ops)
        succ = [[] for _ in range(n)]
        indeg = [0] * n
        lastE = {}
        sdeps = []
        for i, o in enumerate(ops):
            o["deps"] = set(j for j in o["deps"] if j != i)
            sd = set(o["deps"])
            if o["eng"] in ("sp", "pool") or not any(o["tag"].startswith(p_) for p_ in SCHED_TAGS):
                if o["eng"] in lastE:
                    sd.add(lastE[o["eng"]])
            lastE[o["eng"]] = i
            sdeps.append(sd)
            indeg[i] = len(sd)
            for j in sd:
                succ[j].append(i)
        XL = SCHED_XL
        bl = [0.0] * n
        for i in range(n - 1, -1, -1):
            o = ops[i]
            m_ = 0.0
            for s in succ[i]:
                x = bl[s] + (XL if ops[s]["eng"] != o["eng"] or o["dma"] is not None else 0.05)
                if x > m_:
                    m_ = x
            bl[i] = o["cost"] + o["lat"] + m_
        finish = [0.0] * n
        ready_t = [0.0] * n
        free = {e: 0.0 for e in self.ENGS}
        rdy = {e: [] for e in self.ENGS}
        for i in range(n):
            if indeg[i] == 0:
                rdy[ops[i]["eng"]].append(i)
        order = {e: [] for e in self.ENGS}
        done = 0
        cur_tbl = [None]
        while done < n:
            best = None
            for e in self.ENGS:
                lst = rdy[e]
                if not lst:
                    continue
                tmin = min(ready_t[i] for i in lst)
                t_e = max(free[e], tmin)
                pick = None
                pk = None
                for i in lst:
                    if ready_t[i] <= t_e + 1e-9:
                        tb = ops[i]["tbl"]
                        same = 1 if (e != "act" or tb is None or tb == cur_tbl[0]) else 0
                        key = (same, bl[i], -i)
                        if pick is None or key > pk:
                            pick, pk = i, key
                if best is None or (t_e, pick) < (best[0], best[1]):
                    best = (t_e, pick, e)
            st, i, e = best
            rdy[e].remove(i)
            o = ops[i]
            if e == "act" and o["tbl"] is not None and o["tbl"] != cur_tbl[0]:
                cur_tbl[0] = o["tbl"]
                st += 1.3
            free[e] = st + o["cost"]
            finish[i] = st + o["cost"] + o["lat"]
            order[e].append(i)
            done += 1
            for s in succ[i]:
                indeg[s] -= 1
                if i in ops[s]["deps"]:
                    x = finish[i] + (XL if ops[s]["eng"] != e or o["dma"] is not None else 0.05)
                else:
                    x = st + o["cost"]
                if x > ready_t[s]:
                    ready_t[s] = x
                if indeg[s] == 0:
                    rdy[ops[s]["eng"]].append(s)
        self.sim_time = max(finish) if n else 0.0
        return order

    def emit(self):
        nc = self.nc
        ops = self.ops
        sched_order = self.schedule() if SCHED else None

        pos = {}
        per = {e: [] for e in self.ENGS}
        if sched_order is not None:
            per = sched_order
        else:
            for i, o in enumerate(ops):
                per[o["eng"]].append(i)
        for e in self.ENGS:
            for p_, i in enumerate(per[e]):
                pos[i] = p_
                ops[i]["idx"] = i
        if SCHED:
            for o in ops:
                if o.get("bar"):
                    keep = {}
                    nd = set()
                    for j in o["deps"]:
                        pj = ops[j]
                        if pj["dma"] is not None:
                            nd.add(j)
                        elif pj["eng"] not in keep or pos[j] > pos[keep[pj["eng"]]]:
                            keep[pj["eng"]] = j
                    o["deps"] = nd | set(keep.values())

        def elide(pj, o):
            if not (pj["dma"] is None and o["dma"] is None and pj["eng"] == o["eng"]):
                return False
            if pj["eng"] == "pe" or not SAME_ENG_SYNC:
                return True
            return SAME_ENG_MIN_DIST > 0 and (pos[o["idx"]] - pos[pj["idx"]]) >= SAME_ENG_MIN_DIST

        needed = [False] * len(ops)
        for i, o in enumerate(ops):
            for j in o["deps"]:
                if not elide(ops[j], o):
                    needed[j] = True
        engsem, dmasem, dmacnt, final_dma = {}, {}, {}, {}
        cnt = {e: 0 for e in self.ENGS}
        val = [0] * len(ops)
        semof = [None] * len(ops)
        num_seq = [i for e in self.ENGS for i in per[e]]
        for i in num_seq:
            o = ops[i]
            if o["dma"] is not None:
                g = o["dma"]
                if g not in dmasem:
                    dmasem[g] = nc.alloc_semaphore(name="d%d" % len(dmasem))
                    dmacnt[g] = 0
                dmacnt[g] += 16
                val[i] = dmacnt[g]
                semof[i] = dmasem[g]
                final_dma[g] = dmacnt[g]
            elif needed[i]:
                e = o["eng"]
                if e not in engsem:
                    engsem[e] = nc.alloc_semaphore(name="e_" + e)
                cnt[e] += 1
                val[i] = cnt[e]
                semof[i] = engsem[e]
        self.n_sems = len(dmasem) + len(engsem)

        def run(engname, eng):
            waited = {}
            for i in per[engname]:
                o = ops[i]
                best = {}
                for j in o["deps"]:
                    if semof[j] is None or elide(ops[j], o):
                        continue
                    s = semof[j]
                    if val[j] > best.get(id(s), (0, None))[0]:
                        best[id(s)] = (val[j], s)
                for sid, (v, s) in best.items():
                    if waited.get(sid, 0) >= v:
                        continue
                    waited[sid] = v
                    eng.wait_ge(s, v)
                ins = o["fn"](eng)
                if ANNOTATE and o["tag"]:
                    ins.annotate(o["tag"])
                if o["dma"] is not None:
                    ins.then_inc(semof[i], 16)
                elif semof[i] is not None:
                    ins.then_inc(semof[i], 1)
            if engname == "sp":
                for g, v in final_dma.items():
                    eng.wait_ge(dmasem[g], v)

        with nc.Block() as block:
            @block.tensor
            def _(e):
                run("pe", e)

            @block.scalar
            def _(e):
                run("act", e)

            @block.vector
            def _(e):
                run("dve", e)

            @block.gpsimd
            def _(e):
                run("pool", e)

            @block.sync
            def _(e):
                run("sp", e)


def _fs(ap):
    n = 1
    for s in ap.shape[1:]:
        n *= s
    return n


def _ec(ap):
    return _fs(ap) / 900.0 + 0.15


def _eca(ap):
    return _fs(ap) / 1050.0 + 0.11


class K:
    def __init__(self, nc):
        self.nc = nc
        self.p = Prog(nc)
        self._stack = [[]]
        self.ps_banks = []
        self.ps_i = 0

    def sb(self, name, shape, dt):
        self._uid = getattr(self, "_uid", 0) + 1
        g = self.nc.sbuf_tensor("%s_%d" % (name, self._uid), list(shape), dt)
        t = g.__enter__()
        self._stack[-1].append(g)
        return t

    def psum(self, name, shape, dt):
        g = self.nc.psum_tensor(name, list(shape), dt)
        t = g.__enter__()
        self._stack[-1].append(g)
        return t

    @contextlib.contextmanager
    def scope(self):
        self.p.barrier()
        self._stack.append([])
        try:
            yield
        finally:
            self.p.barrier()
            for g in reversed(self._stack.pop()):
                g.__exit__(None, None, None)

    def bank(self, pool=None):
        if pool is None:
            b = self.ps_banks[self.ps_i % len(self.ps_banks)]
            self.ps_i += 1
            return b
        self._pi = getattr(self, "_pi", [0, 0])
        b = self.ps_banks[pool * 4 + self._pi[pool] % 4]
        self._pi[pool] += 1
        return b

    def mm(self, out, lhsT, rhs, start=True, stop=True, tp=None, rk=None):
        kw = {}
        if tp is not None:
            kw["tile_position"] = tp
        self.p.op("pe", lambda e: e.matmul(out, lhsT, rhs, start=start, stop=stop, **kw),
                  r=[lhsT, rhs] if rk is None else rk, w=[out], cost=max(_fs(rhs), 64) / 2300.0 + MM_OVH, lat=0.15)

    def tr(self, out, in_, ident):
        self.p.op("pe", lambda e: e.transpose(out, in_, ident), r=[in_, ident], w=[out], cost=0.1, lat=0.15)

    def act(self, out, in_, func, bias=None, scale=None, accum=None):
        kw = {}
        if bias is not None:
            kw["bias"] = bias
        if scale is not None:
            kw["scale"] = scale
        if accum is not None:
            kw["accum_out"] = accum
        rr = [in_] + [a for a in (bias, scale) if not isinstance(a, (int, float, type(None)))]
        ww = [out] + ([accum] if accum is not None else [])
        tbl = {AF.Sigmoid: "sig", AF.Tanh: "sig", AF.Exp: "exp", AF.Ln: "exp", AF.Silu: "silu"}.get(func)
        self.p.op("act", lambda e: e.activation(out, in_, func, **kw), r=rr, w=ww, cost=_ec(out), tbl=tbl)

    def tt(self, eng, out, in0, in1, op, wk=None):
        self.p.op(eng, lambda e: e.tensor_tensor(out, in0, in1, op), r=[in0, in1], w=[out] if wk is None else wk, cost=_ec(out))

    def ts(self, eng, out, in0, s1, s2, op0, op1=None):
        rr = [in0] + [a for a in (s1, s2) if not isinstance(a, (int, float, type(None)))]
        kw = {}
        if op1 is not None:
            kw["op1"] = op1
        self.p.op(eng, lambda e: e.tensor_scalar(out, in0, s1, s2, op0, **kw), r=rr, w=[out], cost=_ec(out))

    def stt(self, eng, out, in0, scalar, in1, op0, op1):
        rr = [in0, in1] + ([] if isinstance(scalar, (int, float)) else [scalar])
        self.p.op(eng, lambda e: e.scalar_tensor_tensor(out, in0, scalar, in1, op0, op1), r=rr, w=[out], cost=_ec(out))

    def copy(self, eng, out, in_):
        if eng == "act":
            self.p.op("act", lambda e: e.copy(out, in_), r=[in_], w=[out], cost=_ec(out))
        else:
            self.p.op(eng, lambda e: e.tensor_copy(out, in_), r=[in_], w=[out], cost=_ec(out))

    def recip(self, out, in_):
        self.p.op("dve", lambda e: e.reciprocal(out, in_), r=[in_], w=[out], cost=5 * _ec(out))

    def red(self, out, in_):
        self.p.op("dve", lambda e: e.tensor_reduce(out, in_, AX.X, ALU.add), r=[in_], w=[out], cost=_ec(in_))

    def memset(self, eng, ap, v):
        self.p.op(eng, lambda e: e.memset(ap, v), r=[], w=[ap], cost=_ec(ap))

    def dma(self, q, out, in_, grp):
        self.p.op(q, lambda e: e.dma_start(out=out, in_=in_), r=[in_], w=[out], dma=grp, cost=0.3,
                  lat=2.0 + _fs(out) * 128 * 4 / 150e3)


def _pack_consts():
    P = 128
    items = {}
    p = np.arange(P)
    col = np.arange(128)
    items["ident"] = np.eye(P, dtype=np.float32)
    s = (p % 64)[:, None]
    t = (col % 64)[None, :]
    items["mA_p"] = np.where(col[None, :] < 64, s < t, s <= t).astype(np.float32)
    items["mN_p"] = (np.arange(64)[None, :] < s).astype(np.float32)
    sb_, tb_ = (p // 4)[:, None], ((col % 64) // 4)[None, :]
    s4, t4 = (p % 4)[:, None], ((col % 64) % 4)[None, :]
    mA_s = np.where(col[None, :] < 64, s4 < t4, s4 <= t4) & (sb_ == tb_) & (p[:, None] < 64)
    items["mA_s"] = mA_s.astype(np.float32)
    c64 = np.arange(64)
    items["mN_s"] = (((c64 % 4)[None, :] < s4) & ((c64 // 4)[None, :] == sb_) & (p[:, None] < 64)).astype(np.float32)
    S_, T_ = p[:, None], col[None, :]
    same = (S_ // 64) == (T_ // 64)
    items["triI_p"] = (same & (S_ <= T_)).astype(np.float32)
    items["triX_p"] = (same & (S_ < T_)).astype(np.float32)
    items["triR_p"] = (same & (S_ > T_)).astype(np.float32)
    same = ((S_ // 4) == (T_ // 4)) & (S_ < 64) & (T_ < 64)
    items["triI_s"] = (same & (S_ <= T_)).astype(np.float32)
    items["triX_s"] = (same & (S_ < T_)).astype(np.float32)
    items["triR_s"] = (same & (S_ > T_)).astype(np.float32)
    items["sel_p"] = ((p // 64)[:, None] == np.arange(2)[None, :]).astype(np.float32)
    items["sel_s"] = (((p // 4)[:, None] == np.arange(16)[None, :]) & (p[:, None] < 64)).astype(np.float32)
    cm = ((c64 // 4)[None, :] == np.arange(16)[:, None]).astype(np.float32)
    items["colmask"] = np.broadcast_to(cm.reshape(1, 16 * 64), (P, 16 * 64)).copy()
    lg = np.log1p(-np.exp2(-5.0 - np.arange(4, dtype=np.float32))).astype(np.float32)
    scale = np.float32(128.0 ** -0.5)
    i_ = np.arange(128, dtype=np.float32)
    diff = i_[None, :] - i_[:, None]
    dm = np.zeros((P, 4, 128), np.float32)
    for h in range(4):
        dm[:, h, :] = np.where(diff >= 0, np.exp(lg[h] * np.maximum(diff, 0.0)), 0.0) * scale
    items["dm_p"] = dm.reshape(P, 512)
    dms = np.zeros((P, 4, 64), np.float32)
    jj, ii = np.arange(64)[:, None], np.arange(64)[None, :]
    d4 = (ii % 4 - jj % 4).astype(np.float32)
    okm = (jj // 4 == ii // 4) & (d4 >= 0)
    for h in range(4):
        dms[:64, h, :] = np.where(okm, np.exp(lg[h] * np.maximum(d4, 0.0)), 0.0) * scale
    items["dm_s"] = dms.reshape(P, 256)
    qd = np.zeros((P, 4, 128), np.float32)
    qs = np.zeros((P, 4, 64), np.float32)
    kd = np.zeros((P, 4), np.float32)
    ks = np.zeros((P, 4), np.float32)
    cdp = np.zeros((P, 4), np.float32)
    cds = np.zeros((P, 4), np.float32)
    for h in range(4):
        qd[:, h, :] = np.exp(lg[h] * (i_ + 1.0))[None, :]
        qs[:, h, :] = np.exp(lg[h] * ((np.arange(64) % 4).astype(np.float32) + 1.0))[None, :]
        kd[:, h] = np.exp(lg[h] * (127.0 - i_)) * scale
        ks[:, h] = np.exp(lg[h] * (3.0 - (p % 4).astype(np.float32))) * scale
        cdp[:, h] = np.exp(lg[h] * 128.0)
        cds[:, h] = np.exp(lg[h] * 4.0)
    items["qdec_p"] = qd.reshape(P, 512)
    items["qdec_s"] = qs.reshape(P, 256)
    items["kdec_p"] = kd
    items["kdec_s"] = ks
    items["cdec_p"] = cdp
    items["cdec_s"] = cds
    offs = {}
    o = 0
    for k_, v in items.items():
        offs[k_] = (o, v.shape[1])
        o += v.shape[1]
    pack = np.concatenate([items[k_] for k_ in items], axis=1).astype(np.float32)
    return pack, offs


def _rope_tables():
    half = 64
    inv = (np.float32(10000.0) ** (-np.arange(half, dtype=np.float32) / np.float32(half))).astype(np.float32)
    pos_p = np.arange(2048, dtype=np.float32)
    ang = (pos_p[:, None] * inv[None, :]).astype(np.float32)
    cs_p = np.concatenate([np.cos(ang), np.sin(ang)], axis=1).astype(np.float32)
    pos_s = (16384 + (np.arange(64) % 4)).astype(np.float32)
    ang = (pos_s[:, None] * inv[None, :]).astype(np.float32)
    cs_s = np.concatenate([np.cos(ang), np.sin(ang)], axis=1).astype(np.float32)
    return cs_p, cs_s


CPACK, COFF = _pack_consts()
NCP = CPACK.shape[1]


class _Stop(Exception):
    pass


class Builder:
    def __init__(self, debug=None):
        nc = bass.Bass("TRN2", target_bir_lowering=False)
        self.nc = nc
        self.k = K(nc)
        self.debug = debug or {}
        self.dbg_outs = {}

        def di(name, shape):
            return nc.dram_tensor(name, list(shape), F32, kind="ExternalInput").ap()

        def do(name, shape):
            return nc.dram_tensor(name, list(shape), F32, kind="ExternalOutput").ap()

        self.xp = di("xp", [2048, D])
        self.xs = di("xs", [64, D])
        self.sshift = di("sshift", [16, D])
        self.swkv = di("swkv", [16, 64, 512])
        self.sret = di("sret", [16, 128, 512])
        self.gTd = di("gT", [128, 48])
        self.normg = di("normg", [6, D])
        self.fw = {1: (di("f1gu", [NCH, 128, 2048]), None, di("f1d", [DFF, D])),
                   2: (di("f2gu", [NCH, 128, 2048]), None, di("f2d", [DFF, D]))}
        self.win_n = (512, 512, 512, 256, 512, 512, 512, 512)
        self.win = [di("win%d" % g, [128, 8 * n_]) for g, n_ in enumerate(self.win_n)]
        self.w_out = di("w_out", [D, D])
        self.mu = di("mu", [1, 1792])
        self.muTd = di("muT", [128, 2])
        self.vec = {n: di(n, [1, 512]) for n in ("w0", "a0", "k_k", "k_a", "r_k", "lnx_w", "lnx_b", "gn_w")}
        self.w2 = di("w2", [64, 512])
        self.a2 = di("a2", [64, 512])
        self.g2 = di("g2", [128, 512])
        self.cpack = di("cpack", [128, NCP])
        self.csp = di("cs_p", [2048, 128])
        self.css = di("cs_s", [64, 128])
        self.yp = do("yp", [2048, D])
        self.ys = do("ys", [64, D])
        self.shp = do("shp", [1, D])
        self.wkp = do("wkp", [64, 512])
        self.rtp = do("rtp", [128, 512])
        self.shs = do("shs", [16, D])
        self.wks = do("wks", [16, 64, 512])
        self.rts = do("rts", [16, 128, 512])

    def dbg(self, name, ap, shape):
        if name not in self.debug:
            return
        o = self.nc.dram_tensor("dbg_" + name, list(shape), F32, kind="ExternalOutput").ap()
        t = self.k.sb("dbgt_" + name, list(shape), F32)
        self.k.copy("dve", t[:], ap)
        self.k.dma("sp", o, t[:], "dbg_" + name)
        self.dbg_outs[name] = "dbg_" + name

    def c(self, name):
        o, n = COFF[name]
        return self.cb[:, o:o + n]

    def build(self):
        k = self.k
        for i in range(8):
            k.ps_banks.append(k.psum("ps%d" % i, [128, 512], F32))
        self.cb = k.sb("cb", [128, NCP], BF16)
        self.identf = k.sb("identf", [128, 128], F32)
        self.kcd = k.sb("kcd", [128, 16], F32)
        self.gT = k.sb("gTs", [128, 48], F32)
        self.muT = k.sb("muTs", [128, 2], F32)
        self.Hst = k.sb("Hst", [128, 4, 64], F32)
        self.Hb = k.sb("Hb", [128, 4, 64], BF16)
        self.Sst = k.sb("Sst", [128, 4, 128], F32)
        self.Sb = k.sb("Sb", [128, 4, 128], BF16)
        self.hprev = k.sb("hprev", [128, 8], BF16)
        self.xn = k.sb("xn", [128, D], BF16)
        self.junk = k.sb("junk", [128, D], BF16)
        self.ss = k.sb("ss", [128, 8], F32)
        self.identb = self.c("ident")
        with k.scope():
            st = k.sb("cstage", [128, NCP], F32)
            k.dma("sp", st[:], self.cpack, "c0")
            k.dma("sp", self.gT[:], self.gTd, "c1")
            k.dma("sp", self.muT[:], self.muTd, "c2")
            k.copy("dve", self.cb[:], st[:])
            o, n = COFF["ident"]
            k.copy("act", self.identf[:], st[:, o:o + n])
            o, _ = COFF["kdec_p"]
            k.copy("act", self.kcd[:], st[:, o:o + 16])
        k.memset("dve", self.Hst[:], 0.0)
        k.memset("dve", self.Hb[:], 0.0)
        k.memset("dve", self.Sst[:], 0.0)
        k.memset("dve", self.Sb[:], 0.0)
        k.memset("dve", self.hprev[:], 0.0)
        self.xs_t = k.sb("xs_t", [128, D], F32)
        blocks = self.debug.get("blocks", ["s", "p0", "p1"])
        stages = self.debug.get("stages", ("f1", "m1", "m2", "f2"))
        k.dma("sp", self.xs_t[:64, :], self.xs, "xs_l")
        stile = (self.xs_t, 64)
        for bi in range(2):
            if ("p%d" % bi) not in blocks:
                continue
            with k.scope():
                pt = [(k.sb("x%d" % i, [128, D], F32), 128) for i in range(8)]
                t_f1 = pt + ([stile] if (bi == 0 and "s" in blocks) else [])
                t_f2 = pt + ([stile] if (bi == 1 and "s" in blocks) else [])
                ncmax = 1024 + 64
                hT = k.sb("hT", [128, 8, 1 + ncmax], BF16)
                yaT = k.sb("yaT", [128, 4, 1024], BF16)
                for i in range(8):
                    r0 = (bi * 8 + i) * 128
                    k.dma("sp", pt[i][0][:, :], self.xp[r0:r0 + 128, :], "xl%d" % i)
                self.m_pre_done = False
                self.f2_pre_done = 0
                if "f1" in stages:
                    hoist = (2, 8) if ("m1" in stages and HOIST) else None
                    self.ffn(1, t_f1, hT, 0, 1, hoist=hoist)
                    self.m_pre_done = hoist is not None
                if "m1" in stages:
                    self.mixer1(pt, hT, yaT, "p", bi == 1, 1024, bi)
                if "m2" in stages:
                    self.mixer2(pt, hT, yaT, "p", bi == 1, 1024, bi)
                def store(i, bi=bi, pt=pt):
                    if i < 8:
                        r0 = (bi * 8 + i) * 128
                        k.dma("sp", self.yp[r0:r0 + 128, :], pt[i][0][:, :], "xs%d" % i)
                if "f2" in stages:
                    self.ffn(2, t_f2, hT, 4, 5, pre_done=self.f2_pre_done, after_tile=store)
                else:
                    for i in range(8):
                        store(i)
            if bi == 0 and "s" in blocks:
                with k.scope():
                    hT = k.sb("hTs_blk", [128, 8, 1 + 64], BF16)
                    yaT = k.sb("yaTs_blk", [128, 4, 64], BF16)
                    if "m1" in stages:
                        self.mixer1([stile], hT, yaT, "s", False, 64, 0)
                    if "m2" in stages:
                        self.mixer2([stile], hT, yaT, "s", False, 64, 0)
        if "s" in blocks:
            if "p1" not in blocks:
                with k.scope():
                    hT = k.sb("hTs_blk2", [128, 8, 1 + 64], BF16)
                    if "p0" not in blocks:
                        self.ffn(1, [stile], hT, 0, 1)
                        yaT = k.sb("yaTs_blk2", [128, 4, 64], BF16)
                        self.mixer1([stile], hT, yaT, "s", False, 64, 0)
                        self.mixer2([stile], hT, yaT, "s", False, 64, 0)
                    self.ffn(2, [stile], hT, 4, 5)
            k.dma("sp", self.ys, self.xs_t[:64, :], "xs_s")
        k.p.emit()
        return self.nc

    def rstd_from(self, nt, src, dst, n):
        k = self.k
        k.act(dst, src, AF.Ln, scale=1.0 / n, bias=EPS)
        k.act(dst, dst, AF.Exp, scale=-0.5)

    def prenorm(self, xt, nt, gidx, dst, sample_dst=False, wk=None, pool=None):
        k = self.k
        k.act(self.junk[:nt, :], xt[:nt, :], AF.Square, accum=self.ss[:nt, 0:1])
        self.rstd_from(nt, self.ss[:nt, 0:1], self.ss[:nt, 1:2], D)
        k.ts("dve", self.xn[:nt, :], xt[:nt, :], self.ss[:nt, 1:2], None, ALU.mult)
        bk = k.bank(pool)
        psb = bk[:].bitcast(BF16)
        for kk in range(8):
            k.tr(psb[:, kk * 128:kk * 128 + nt], self.xn[:nt, kk * 128:(kk + 1) * 128], self.identb[:nt, :nt])
        src = psb.rearrange("p (k t) -> p k t", k=8)[:, :, :nt]
        gs = self.gT[:, gidx * 8:(gidx + 1) * 8]
        if sample_dst:
            src = src.rearrange("p k (b t) -> p k b t", t=4)
            g = gs.unsqueeze(2).unsqueeze(3).to_broadcast([128, 8, 16, 4])
        else:
            g = gs.unsqueeze(2).to_broadcast([128, 8, nt])
        k.tt("dve", dst, src, g, ALU.mult, wk=wk)

    def postnorm_residual(self, Y, xt, nt, gpb, factor, tY):
        k = self.k
        ss = self.ss
        k.act(self.junk[:nt, 0:512], Y[0][:nt, :], AF.Square, accum=ss[:nt, 2:3])
        k.act(self.junk[:nt, 512:1024], Y[1][:nt, :], AF.Square, accum=ss[:nt, 3:4])
        k.tt("dve", ss[:nt, 4:5], ss[:nt, 2:3], ss[:nt, 3:4], ALU.add)
        self.rstd_from(nt, ss[:nt, 4:5], ss[:nt, 5:6], D)
        for j in range(2):
            t = tY[j]
            k.stt("dve", t[:nt, :], Y[j][:nt, :], ss[:nt, 5:6], gpb[:nt, j * 512:(j + 1) * 512], ALU.mult, ALU.mult)
            k.stt("dve", xt[:nt, j * 512:(j + 1) * 512], t[:nt, :], float(factor), xt[:nt, j * 512:(j + 1) * 512],
                  ALU.mult, ALU.add)

    def ffn(self, which, tiles, hT, gpre, gpost, pre_done=0, hoist=None, after_tile=None):
        k = self.k
        wg, wu, wd = self.fw[which]
        ncols = sum(t[1] for t in tiles)
        k.p.tag = "F%d" % which
        with k.scope():
            aT = k.sb("aT", [128, NCH, ncols], BF16)
            wdt = k.sb("wdt", [128, NCH, D], BF16)
            ring = [k.sb("wr%d" % i, [128, 2, 8, 128], BF16) for i in range(NRING)]
            gpb = k.sb("gpb", [128, D], F32)
            tE = [k.sb("tE%d" % i, [128, 512], F32) for i in range(2)]
            tY = [k.sb("tY%d" % i, [128, 512], F32) for i in range(2)]
            k.dma("sp", gpb[:], self.normg[gpost:gpost + 1, :].partition_broadcast(128), "gpb")
            wdv = wd.rearrange("(c p) d -> p c d", p=128)
            col = 1
            hkey = lambda c_: ("hTg", hT[:].name, (c_ - 1) // 512)
            for i, (xt, nt) in enumerate(tiles):
                if i >= pre_done:
                    self.prenorm(xt, nt, gpre, hT[:, :, col:col + nt], wk=[hkey(col)])
                col += nt
            groups = [(c0, min(512, ncols - c0)) for c0 in range(0, ncols, 512)]
            for c in range(NCH):
                slot = ring[c % NRING]
                k.dma("pool", slot[:].rearrange("p a k f -> p (a k f)"), wg[c], "wr%d" % (c % NRING))
                if c == 2:
                    k.dma("pool", wdt[:, 0:11, :], wdv[:, 0:11, :], "wd0")
                if c == 5:
                    k.dma("pool", wdt[:, 11:22, :], wdv[:, 11:22, :], "wd1")
                for gi, (c0, n) in enumerate(groups):
                    G = k.bank()
                    U = k.bank()
                    for kk in range(8):
                        k.mm(G[:, :n], slot[:, 0, kk, :], hT[:, kk, 1 + c0:1 + c0 + n], start=kk == 0, stop=kk == 7,
                             rk=[slot[:, 0, kk, :], hkey(1 + c0)])
                    for kk in range(8):
                        k.mm(U[:, :n], slot[:, 1, kk, :], hT[:, kk, 1 + c0:1 + c0 + n], start=kk == 0, stop=kk == 7,
                             rk=[slot[:, 0, kk, :], hkey(1 + c0)])
                    e = tE[gi % 2]
                    k.act(e[:, :n], G[:, :n], AF.Silu)
                    k.tt("dve", aT[:, c, c0:c0 + n], U[:, :n], e[:, :n], ALU.mult)
            col = 0
            pend = None

            def do_hoist(p_):
                xt0, nt0, c0_ = p_
                self.prenorm(xt0, nt0, hoist[0], hT[:, :, 1 + c0_:1 + c0_ + nt0], wk=[hkey(1 + c0_), hT[:]])
            for i, (xt, nt) in enumerate(tiles):
                Y = [k.bank(), k.bank()]
                for j in range(2):
                    for c in range(NCH):
                        k.mm(Y[j][:nt, :], aT[:, c, col:col + nt], wdt[:, c, j * 512:(j + 1) * 512],
                             start=c == 0, stop=c == NCH - 1)
                if pend is not None:
                    do_hoist(pend)
                    pend = None
                self.postnorm_residual(Y, xt, nt, gpb, 0.5, tY)
                if after_tile is not None:
                    after_tile(i)
                if hoist is not None and i < hoist[1]:
                    pend = (xt, nt, col)
                col += nt
            if pend is not None:
                do_hoist(pend)

    def stop(self, n):
        if self.debug.get("m1lvl", 99) <= n:
            raise _Stop()

    def mixer1(self, *a):
        try:
            self._mixer1(*a)
        except _Stop:
            pass

    def _mixer1(self, tiles, hT, yaT, kind, last, ncols, bi):
        k = self.k
        smp = kind == "s"
        sfx = "_s" if smp else "_p"
        ns = 16 if smp else 2
        nsteps = 2 if smp else 6
        k.p.tag = "M1%s_pre" % sfx
        with k.scope():
            Wsg = []
            for g_ in range(4):
                t_ = k.sb("Ws%d" % g_, [128, 8, self.win_n[g_]], BF16)
                k.dma("pool", t_[:].rearrange("p k n -> p (k n)"), self.win[g_], "wsg%d" % g_)
                Wsg.append(t_)
            mu_b = k.sb("mu_b", [128, 1536], F32)
            k.dma("sp", mu_b[:], self.mu[0:1, 0:1536].partition_broadcast(128), "mub")
            bc = {}
            for n in ("w0", "a0", "k_k", "k_a", "r_k", "lnx_w", "lnx_b"):
                bc[n] = k.sb("bc_" + n, [128, 512], F32)
                k.dma("sp", bc[n][:], self.vec[n].partition_broadcast(128), "bc_" + n)
            w2a2 = k.sb("w2a2", [128, 512], BF16)
            k.dma("pool", w2a2[0:64, :], self.w2, "w2a2")
            k.dma("pool", w2a2[64:128, :], self.a2, "w2a2")
            g2b = k.sb("g2b", [128, 512], BF16)
            k.dma("pool", g2b[:], self.g2, "g2b")
            scr = k.sb("scr", [128, D], F32)
            hfull = scr
            f32t = lambda n, w=512: k.sb(n, [128, w], F32)
            b16t = lambda n, w=512: k.sb(n, [128, w], BF16)
            rkv = f32t("rkv", 1536)
            g2f = rkv[:, 0:D]
            k.dma("sp", g2f, self.normg[2:3, :].partition_broadcast(128), "g2f")
            dT = k.sb("dT", [128, 8, 128], BF16)
            lor = f32t("lor", 128)
            lwa = b16t("lwa", 128)
            lg = b16t("lg", 128)
            sg = f32t("sg")
            sghi = b16t("sghi")
            sglo = b16t("sglo")
            alr = f32t("alr")
            gg2 = [f32t("gg0"), f32t("gg1")]
            E1 = f32t("E1")
            dtmp = E1
            E2 = f32t("E2")
            kkt = sg
            bvec = kkt
            t1 = f32t("t1")
            kkn = f32t("kkn")
            kmod = f32t("kmod")
            st8 = k.sb("st8", [128, 64], F32)
            Rt2 = [b16t("Rt0"), b16t("Rt1")]
            At2 = [b16t("At0"), b16t("At1")]
            Bt2 = [b16t("Bt0"), b16t("Bt1")]
            Kt2 = [b16t("Kt0"), b16t("Kt1")]
            Bh2 = [b16t("Bh0"), b16t("Bh1")]
            Kh2 = [b16t("Kh0"), b16t("Kh1")]
            Vb2 = [b16t("Vb0"), b16t("Vb1")]
            ART_2 = [k.sb("ART_%d" % i, [128, 4, 2, 2, 64], BF16) for i in range(2)]
            BT = k.sb("BT", [128, 4, 128], BF16)
            KT = k.sb("KT", [128, 4, 128], BF16)
            A1_2 = [k.sb("A1_%d" % i, [128, 8, 128], BF16) for i in range(2)]
            A2_2 = [k.sb("A2_%d" % i, [128, 8, 128], BF16) for i in range(2)]
            Pt = [k.sb("Pt%d" % i, [128, 8, 64], BF16) for i in range(3)]
            PTt = [k.sb("PTt%d" % i, [128, 8, 64], BF16) for i in range(2)]
            W2s_2 = [k.sb("W2s_%d" % i, [128, 8, 64], F32) for i in range(2)]
            Xb = k.sb("Xb", [128, 8, 128], BF16)
            W1T = k.sb("W1T", [128, 4, 128], BF16)
            Ub = k.sb("Ub", [128, 8, 64], BF16)
            gC2 = [k.sb("gC%d" % i, [128, 4, 16], F32) for i in range(2)]
            stA2 = [k.sb("stA%d" % i, [128, 8], F32) for i in range(2)]
            ysb = scr[:, 0:512]
            yc = scr[:, 512:1024]
            ya = b16t("ya")
            mA = self.c("mA" + sfx)
            mN = self.c("mN" + sfx)
            triI, triX, triR = self.c("triI" + sfx), self.c("triX" + sfx), self.c("triR" + sfx)
            sel = self.c("sel" + sfx)
            if smp:
                hTs = k.sb("hTs", [128, 8, 16, 5], BF16)
                hTc = k.sb("hTc", [128, 8, 64], BF16)
                hTp = k.sb("hTp", [128, 8, 64], BF16)
                shs_t = k.sb("shs_t", [16, D], F32)
                shs_b = k.sb("shs_b", [16, D], BF16)
                Hs = [k.sb("Hs%d" % b, [128, 4, 64], F32) for b in range(16)]
                Hsb = k.sb("Hsb", [128, 16, 4, 64], BF16)
                Snat = [k.sb("Snat%d" % i, [64, 512], F32) for i in range(2)]
                W1Tm = k.sb("W1Tm", [128, 4, 16, 64], BF16)
                RTm = k.sb("RTm", [128, 4, 16, 64], BF16)
                Bhm = [Rt2[1], At2[1]]
                Khm = [Bt2[1], Kt2[1]]
                colmask = self.c("colmask").rearrange("p (b t) -> p b t", b=16)
                k.dma("sp", shs_t[:], self.sshift, "shs_t")
                k.copy("dve", shs_b[:], shs_t[:])
                bk = k.bank()
                psb = bk[:].bitcast(BF16)
                for kk in range(8):
                    k.tr(psb[:, kk * 16:(kk + 1) * 16], shs_b[:16, kk * 128:(kk + 1) * 128], self.identb[:16, :16])
                k.copy("dve", hTs[:, :, :, 0], psb[:, 0:128].rearrange("p (k b) -> p k b", k=8))

            def state_gen():
                for b in range(16):
                    k.p.tag = "M1_s_state"
                    sn = Snat[b % 2]
                    k.dma("sp", sn[:, :], self.swkv[b],
                          "snat%d" % (b % 2))
                    bk = k.bank(1)
                    for hp in range(4):
                        k.tr(bk[:, hp * 64:(hp + 1) * 64], sn[:64, hp * 128:(hp + 1) * 128], self.identf[:64, :64])
                    k.copy("act", Hs[b][:], bk[:, 0:256].rearrange("p (h v) -> p h v", h=4))
                    k.copy("dve", Hsb[:, b, :, :], Hs[b][:])
                    yield "s"

            self.stop(1)
            if not smp:
                k.copy("dve", hT[:, :, 0], self.hprev[:])
            col = 1
            for i, (xt_, nt) in enumerate(tiles):
                if smp:
                    self.prenorm(xt_, nt, 2, hTs[:, :, :, 1:5], sample_dst=True)
                    k.stt("dve", hfull[:nt, :], xt_[:nt, :], self.ss[:nt, 1:2], g2f[:nt, :], ALU.mult, ALU.mult)
                    for b in range(16):
                        k.dma("sp", self.shs[b:b + 1, :], hfull[4 * b + 3:4 * b + 4, :], "shs_o")
                    k.copy("dve", hTc[:].rearrange("p k (b t) -> p k b t", t=4), hTs[:, :, :, 1:5])
                    k.copy("dve", hTp[:].rearrange("p k (b t) -> p k b t", t=4), hTs[:, :, :, 0:4])
                    k.copy("dve", hT[:, :, 1:65], hTc[:])
                else:
                    if not (getattr(self, "m_pre_done", False) and not (last and i == len(tiles) - 1)):
                        self.prenorm(xt_, nt, 2, hT[:, :, col:col + nt])
                    if last and i == len(tiles) - 1:
                        k.stt("dve", hfull[:nt, :], xt_[:nt, :], self.ss[:nt, 1:2], g2f[:nt, :], ALU.mult, ALU.mult)
                        k.dma("sp", self.shp, hfull[127:128, :], "shp_o")
                col += nt
            if not smp:
                k.copy("dve", self.hprev[:], hT[:, :, ncols])

            self.stop(2)
            def tile_gen(ti, xt_, nt, col):
                k.p.tag = "M1%s_t%d" % (sfx, ti)
                pb_ = ti % 2
                Rt, At, Bt, Kt, Bh, Kh, Vb = Rt2[pb_], At2[pb_], Bt2[pb_], Kt2[pb_], Bh2[pb_], Kh2[pb_], Vb2[pb_]
                gg, gC, stA = gg2[pb_], gC2[pb_], stA2[pb_]
                A1, A2, ART, W2s = A1_2[pb_], A2_2[pb_], ART_2[pb_], W2s_2[pb_]
                if smp:
                    cur_ap = lambda kk: hTc[:, kk, :]
                    k.tt("dve", dT[:, :, :nt], hTp[:, :, :], hTc[:, :, :], ALU.subtract)
                else:
                    cur_ap = lambda kk, col=col: hT[:, kk, col:col + nt]
                    k.tt("dve", dT[:, :, :nt], hT[:, :, col - 1:col - 1 + nt], hT[:, :, col:col + nt], ALU.subtract)
                prv_ap = lambda kk: dT[:, kk, :nt]
                for gi in range(3):
                    cs_ = slice(gi * 512, (gi + 1) * 512)
                    cu, pv = k.bank(0), k.bank(0)
                    for kk in range(8):
                        k.mm(cu[:nt, :], cur_ap(kk), Wsg[gi][:, kk, :], start=kk == 0, stop=kk == 7)
                    for kk in range(8):
                        k.mm(pv[:nt, :], prv_ap(kk), Wsg[gi][:, kk, :], start=kk == 0, stop=kk == 7)
                    k.tt("dve", dtmp[:nt, :], pv[:nt, :], mu_b[:nt, cs_], ALU.mult)
                    k.tt("dve", rkv[:nt, cs_], cu[:nt, :], dtmp[:nt, :], ALU.add)
                    yield "a"
                    k.p.tag = "M1%s_t%d" % (sfx, ti)
                for fc in range(2):
                    cs_ = slice(1536 + fc * 128, 1536 + (fc + 1) * 128)
                    cu, pv = k.bank(0), k.bank(0)
                    for kk in range(8):
                        k.mm(cu[:, :nt], Wsg[3][:, kk, fc * 128:(fc + 1) * 128], cur_ap(kk), start=kk == 0, stop=kk == 7)
                    for kk in range(8):
                        k.mm(pv[:, :nt], Wsg[3][:, kk, fc * 128:(fc + 1) * 128], prv_ap(kk), start=kk == 0, stop=kk == 7)
                    k.copy("act", lor[:, :nt], cu[:, :nt])
                    k.stt("dve", lor[:, :nt], pv[:, :nt], self.muT[:, fc:fc + 1], lor[:, :nt], ALU.mult, ALU.add)
                    if fc == 0:
                        k.act(lwa[0:64, :nt], lor[0:64, :nt], AF.Tanh)
                        k.copy("act", lwa[64:128, :nt], lor[64:128, :nt])
                    else:
                        k.act(lg[:, :nt], lor[:, :nt], AF.Sigmoid)
                    yield "a"
                    k.p.tag = "M1%s_t%d" % (sfx, ti)
                def sigm(dst, ps, bias_t):
                    k.tt("dve", dst[:nt, :], ps[:nt, :], bias_t[:nt, :], ALU.add)
                    k.act(dst[:nt, :], dst[:nt, :], AF.Sigmoid)
                Lw = k.bank(0)
                k.mm(Lw[:nt, :], lwa[0:64, :nt], w2a2[0:64, :])
                sigm(sg, Lw, bc["w0"])
                La = k.bank(0)
                k.mm(La[:nt, :], lwa[64:128, :nt], w2a2[64:128, :], tp=(64, 0))
                sigm(alr, La, bc["a0"])
                Gp = k.bank(0)
                k.mm(Gp[:nt, :], lg[:, :nt], g2b[:, :])
                k.copy("act", gg[:nt, :], Gp[:nt, :])
                k.copy("act", sghi[:nt, :], sg[:nt, :])
                k.tt("dve", sglo[:nt, :], sg[:nt, :], sghi[:nt, :], ALU.subtract)
                r_ = rkv[:nt, 0:512]
                k_ = rkv[:nt, 512:1024]
                v_ = rkv[:nt, 1024:1536]
                h8 = lambda ap: ap.rearrange("p (h j) -> p h j", h=8)
                bc8 = lambda ap: ap.unsqueeze(2).to_broadcast([nt, 8, 64])
                yield "a"
                k.p.tag = "M1%s_t%d" % (sfx, ti)
                k.tt("dve", kkt[:nt, :], k_, bc["k_k"][:nt, :], ALU.mult)
                k.tt("dve", t1[:nt, :], kkt[:nt, :], kkt[:nt, :], ALU.mult)
                k.red(st8[:nt, 0:8], h8(t1[:nt, :]))
                k.ts("dve", st8[:nt, 0:8], st8[:nt, 0:8], 1e-24, None, ALU.max)
                k.act(st8[:nt, 8:16], st8[:nt, 0:8], AF.Ln)
                k.act(st8[:nt, 8:16], st8[:nt, 8:16], AF.Exp, scale=-0.5)
                k.tt("dve", h8(kkn[:nt, :]), h8(kkt[:nt, :]), bc8(st8[:nt, 8:16]), ALU.mult)
                k.stt("dve", t1[:nt, :], alr[:nt, :], -1.0, bc["k_a"][:nt, :], ALU.add, ALU.mult)
                k.ts("dve", t1[:nt, :], t1[:nt, :], 1.0, None, ALU.add)
                k.tt("dve", kmod[:nt, :], k_, t1[:nt, :], ALU.mult)
                k.tt("dve", bvec[:nt, :], kkn[:nt, :], alr[:nt, :], ALU.mult)
                k.tt("dve", t1[:nt, :], r_, kmod[:nt, :], ALU.mult)
                k.tt("dve", t1[:nt, :], t1[:nt, :], bc["r_k"][:nt, :], ALU.mult)
                k.red(stA[:nt, 0:8], h8(t1[:nt, :]))
                k.copy("act", Vb[:nt, :], v_)
                yield "a"
                k.p.tag = "M1%s_t%d" % (sfx, ti)

                def cums(tri):
                    pb = k.bank(0)
                    k.mm(pb[:nt, :], tri[:nt, :nt], sghi[:nt, :], start=True, stop=False)
                    k.mm(pb[:nt, :], tri[:nt, :nt], sglo[:nt, :], start=False, stop=True)
                    return pb
                csI = cums(triI)
                k.act(E1[:nt, :], csI[:nt, :], AF.Exp, scale=-C0)
                k.tt("dve", Rt[:nt, :], r_, E1[:nt, :], ALU.mult)
                k.act(E2[:nt, :], csI[:nt, :], AF.Exp, scale=C0)
                k.tt("dve", Bt[:nt, :], bvec[:nt, :], E2[:nt, :], ALU.mult)
                k.tt("dve", Kt[:nt, :], kmod[:nt, :], E2[:nt, :], ALU.mult)
                csX = cums(triX)
                k.act(E1[:nt, :], csX[:nt, :], AF.Exp, scale=-C0)
                k.stt("dve", At[:nt, :], kkn[:nt, :], -1.0, E1[:nt, :], ALU.mult, ALU.mult)
                csR = cums(triR)
                k.act(E2[:nt, :], csR[:nt, :], AF.Exp, scale=-C0)
                k.tt("dve", Bh[:nt, :], bvec[:nt, :], E2[:nt, :], ALU.mult)
                k.tt("dve", Kh[:nt, :], kmod[:nt, :], E2[:nt, :], ALU.mult)
                gcp = k.bank(0)
                for hp in range(4):
                    k.mm(gcp[:, hp * 16:hp * 16 + ns], sghi[:nt, hp * 128:(hp + 1) * 128], sel[:nt, :ns], start=True, stop=False)
                    k.mm(gcp[:, hp * 16:hp * 16 + ns], sglo[:nt, hp * 128:(hp + 1) * 128], sel[:nt, :ns], start=False, stop=True)
                k.act(gC[:, :, :ns], gcp[:, 0:64].rearrange("p (h s) -> p h s", h=4)[:, :, :ns], AF.Exp, scale=-C0)
                yield "A_done"
                self.stop(3)
                k.p.tag = "M1%s_t%dB" % (sfx, ti)
                nch = 1 if smp else 2
                for (src, which) in ((At, 0), (Rt, 1), (Bt, 2), (Kt, 3)):
                    bk = k.bank(1)
                    psb = bk[:].bitcast(BF16)
                    for hp in range(4):
                        k.tr(psb[:, hp * 128:hp * 128 + nt], src[:nt, hp * 128:(hp + 1) * 128], self.identb[:nt, :nt])
                    pv4 = psb[:, 0:512].rearrange("p (h t) -> p h t", h=4)
                    if which < 2:
                        k.copy("act" if which == 0 else "dve", ART[:, :, 0:nch, which, :],
                               pv4[:, :, :nt].rearrange("p h (c t) -> p h c t", c=nch))
                    else:
                        k.copy("act" if which == 2 else "dve", (BT if which == 2 else KT)[:, :, :nt], pv4[:, :, :nt])
                self.stop(3.2)
                mA4 = mA[:nt, :].unsqueeze(1).to_broadcast([nt, 4, 128])
                hp2 = lambda t_: t_.rearrange("p (hp par) t -> p hp par t", par=2)
                for (LT, dstA) in ((BT, A1), (KT, A2)):
                    oo = [k.bank(1), k.bank(1)]
                    for c2 in range(nch):
                        rows = c2 * 64
                        for hp in range(4):
                            for par in range(2):
                                fp = par * 64
                                rhsAR = ART[fp:fp + 64, hp, c2, :, :].rearrange("p a t -> p (a t)")
                                k.mm(oo[par][rows:rows + 64, hp * 128:(hp + 1) * 128], LT[fp:fp + 64, hp, rows:rows + 64],
                                     rhsAR, tp=(fp, rows))
                    for par in range(2):
                        k.tt("dve", hp2(dstA[:nt, :, :])[:, :, par, :], oo[par][:nt, :].rearrange("p (h t) -> p h t", h=4), mA4,
                             ALU.mult)
                oN = [k.bank(1), k.bank(1)]
                for c2 in range(nch):
                    rows = c2 * 64
                    for hp in range(4):
                        for par in range(2):
                            fp = par * 64
                            k.mm(oN[par][rows:rows + 64, hp * 64:(hp + 1) * 64], ART[fp:fp + 64, hp, c2, 0, :],
                                 BT[fp:fp + 64, hp, rows:rows + 64], tp=(fp, rows))
                for par in range(2):
                    k.tt("dve", hp2(Pt[0][:nt, :, :])[:, :, par, :], oN[par][:nt, 0:256].rearrange("p (h t) -> p h t", h=4),
                         mN[:nt, :].unsqueeze(1).to_broadcast([nt, 4, 64]), ALU.mult)
                self.stop(3.6)
                Xps = [k.bank(1), k.bank(1)]
                pnb, ptnb = k.bank(1), k.bank(1)
                seen = set()
                for h in range(8):
                    for c2 in range(nch):
                        rows = c2 * 64
                        bkx = Xps[h // 4]
                        s0 = (h % 4) * 128
                        first = (h // 4, c2) not in seen
                        seen.add((h // 4, c2))
                        k.mm(bkx[rows:rows + 64, s0:s0 + 64], self.identb[rows:rows + 64, rows:rows + 64],
                             At[rows:rows + 64, h * 64:(h + 1) * 64], start=first, stop=True, tp=(rows, rows))
                        k.mm(bkx[rows:rows + 64, s0 + 64:s0 + 128], A2[rows:rows + 64, h, 0:64],
                             Vb[rows:rows + 64, h * 64:(h + 1) * 64], start=False, stop=True, tp=(rows, rows))
                for q in range(2):
                    k.copy("act" if q == 0 else "dve", Xb[:nt, q * 4:(q + 1) * 4, :],
                           Xps[q][:nt, :].rearrange("p (h t) -> p h t", h=4))
                self.stop(4)
                k.p.tag = "M1%s_t%dD" % (sfx, ti)
                Pc = Pt[0]
                PTc = None
                for step in range(nsteps):
                    lastst = step == nsteps - 1
                    if not lastst:
                        for h in range(8):
                            for c2 in range(nch):
                                rows = c2 * 64
                                lhsPT = A1[rows:rows + 64, h, 0:64] if PTc is None else PTc[rows:rows + 64, h, :]
                                k.mm(pnb[rows:rows + 64, h * 64:(h + 1) * 64], lhsPT, Pc[rows:rows + 64, h, :], tp=(rows, rows))
                        for h in range(8):
                            for c2 in range(nch):
                                rows = c2 * 64
                                rhsPT = A1[rows:rows + 64, h, 0:64] if PTc is None else PTc[rows:rows + 64, h, :]
                                k.mm(ptnb[rows:rows + 64, h * 64:(h + 1) * 64], Pc[rows:rows + 64, h, :], rhsPT, tp=(rows, rows))
                    for h in range(8):
                        for c2 in range(nch):
                            rows = c2 * 64
                            lhs = A1[rows:rows + 64, h, 0:64] if PTc is None else PTc[rows:rows + 64, h, :]
                            k.mm(Xps[h // 4][rows:rows + 64, (h % 4) * 128:(h % 4 + 1) * 128], lhs, Xb[rows:rows + 64, h, :],
                                 start=False, stop=True, tp=(rows, rows))
                    if not lastst:
                        Pn = Pt[1 + step % 2]
                        PTn = PTt[step % 2]
                        k.copy("act", Pn[:nt, :, :], pnb[:nt, :].rearrange("p (h t) -> p h t", h=8))
                        k.copy("act", PTn[:nt, :, :], ptnb[:nt, :].rearrange("p (h t) -> p h t", h=8))
                        Pc, PTc = Pn, PTn
                    k.copy("act", Xb[:nt, 0:4, :], Xps[0][:nt, :].rearrange("p (h t) -> p h t", h=4))
                    k.copy("dve", Xb[:nt, 4:8, :], Xps[1][:nt, :].rearrange("p (h t) -> p h t", h=4))
                    yield "d"
                    k.p.tag = "M1%s_t%dD" % (sfx, ti)
                self.stop(4.9)
                for q in range(2):
                    k.copy("act", W2s[:nt, q * 4:(q + 1) * 4, :],
                           Xps[q][:nt, :].rearrange("p (h t) -> p h t", h=4)[:, :, 64:128])
                self.stop(4.95)
                yield "D_done"
                k.p.tag = "M1%s_t%dC" % (sfx, ti)
                bk = k.bank(0)
                psb = bk[:].bitcast(BF16)
                k.copy("dve", ya[:nt, :].rearrange("p (h j) -> p h j", h=8), Xb[:nt, :, 0:64])
                for hp in range(4):
                    k.tr(psb[:, hp * 128:hp * 128 + nt], ya[:nt, hp * 128:(hp + 1) * 128], self.identb[:nt, :nt])
                k.copy("act", W1T[:, :, :nt], psb[:, 0:512].rearrange("p (h t) -> p h t", h=4)[:, :, :nt])

                self.stop(5)
                yield "c"
                k.p.tag = "M1%s_t%dC" % (sfx, ti)
                hpv = lambda t_: t_.rearrange("p (hp par) v -> p hp par v", par=2)

                def evac_y(r0, r1, Yp, YA):
                    k.copy("act", ysb[r0:r1, :], Yp[r0:r1, :])
                    for par in range(2):
                        yv = hpv(ysb[r0:r1, :].rearrange("p (h v) -> p h v", h=8))[:, :, par, :]
                        k.tt("dve", yv, yv, YA[par][r0:r1, 0:256].rearrange("p (h v) -> p h v", h=4), ALU.add)
                if not smp:
                    for c2 in range(2):
                        rows = c2 * 64
                        Up = [k.bank(0), k.bank(0)]
                        YA = [k.bank(0), k.bank(0)]
                        for hp in range(4):
                            for par in range(2):
                                fp = par * 64
                                k.mm(Up[par][rows:rows + 64, hp * 64:(hp + 1) * 64], W1T[fp:fp + 64, hp, rows:rows + 64],
                                     self.Hb[fp:fp + 64, hp, :], tp=(fp, rows))
                        for hp in range(4):
                            for par in range(2):
                                fp = par * 64
                                k.mm(YA[par][rows:rows + 64, hp * 64:(hp + 1) * 64], ART[fp:fp + 64, hp, c2, 1, :],
                                     self.Hb[fp:fp + 64, hp, :], tp=(fp, rows))
                        for par in range(2):
                            k.tt("dve", hpv(Ub[rows:rows + 64, :, :])[:, :, par, :],
                                 Up[par][rows:rows + 64, 0:256].rearrange("p (h v) -> p h v", h=4),
                                 hpv(W2s[rows:rows + 64, :, :])[:, :, par, :], ALU.add)
                        Hn = k.bank(0)
                        for h in range(8):
                            hp, fp = h // 2, (h % 2) * 64
                            ho = Hn[fp:fp + 64, hp * 64:(hp + 1) * 64]
                            k.mm(ho, Bh[rows:rows + 64, h * 64:(h + 1) * 64], Ub[rows:rows + 64, h, :], start=True, stop=False, tp=(rows, fp))
                            k.mm(ho, Kh[rows:rows + 64, h * 64:(h + 1) * 64], Vb[rows:rows + 64, h * 64:(h + 1) * 64], start=False,
                                 stop=True, tp=(rows, fp))
                        k.tt("dve", self.Hst[:], self.Hst[:], gC[:, :, c2:c2 + 1].to_broadcast([128, 4, 64]), ALU.mult)
                        k.tt("dve", self.Hst[:], self.Hst[:], Hn[:, 0:256].rearrange("p (h v) -> p h v", h=4), ALU.add)
                        k.copy("act", self.Hb[:], self.Hst[:])
                        Yp = k.bank(0)
                        for h in range(8):
                            yo = Yp[rows:rows + 64, h * 64:(h + 1) * 64]
                            k.mm(yo, A1[rows:rows + 64, h, 64:128], Ub[rows:rows + 64, h, :], start=True, stop=False, tp=(rows, rows))
                            k.mm(yo, A2[rows:rows + 64, h, 64:128], Vb[rows:rows + 64, h * 64:(h + 1) * 64], start=False, stop=True,
                                 tp=(rows, rows))
                        evac_y(rows, rows + 64, Yp, YA)
                        if c2 == 0:
                            yield "c"
                            k.p.tag = "M1%s_t%dC" % (sfx, ti)
                else:
                    k.tt("dve", W1Tm[:], W1T[:, :, 0:64].unsqueeze(2).to_broadcast([128, 4, 16, 64]),
                         colmask.unsqueeze(1).to_broadcast([128, 4, 16, 64]), ALU.mult)
                    k.tt("dve", RTm[:], ART[:, :, 0, 1, :].unsqueeze(2).to_broadcast([128, 4, 16, 64]),
                         colmask.unsqueeze(1).to_broadcast([128, 4, 16, 64]), ALU.mult)
                    Up = [k.bank(0), k.bank(0)]
                    YA = [k.bank(0), k.bank(0)]
                    for par in range(2):
                        fp = par * 64
                        for hp in range(4):
                            for b in range(16):
                                k.mm(Up[par][0:64, hp * 64:(hp + 1) * 64], W1Tm[fp:fp + 64, hp, b, :], Hsb[fp:fp + 64, b, hp, :],
                                     start=b == 0, stop=b == 15, tp=(fp, 0))
                        for hp in range(4):
                            for b in range(16):
                                k.mm(YA[par][0:64, hp * 64:(hp + 1) * 64], RTm[fp:fp + 64, hp, b, :], Hsb[fp:fp + 64, b, hp, :],
                                     start=b == 0, stop=b == 15, tp=(fp, 0))
                    for par in range(2):
                        k.tt("dve", hpv(Ub[0:64, :, :])[:, :, par, :], Up[par][0:64, 0:256].rearrange("p (h v) -> p h v", h=4),
                             hpv(W2s[0:64, :, :])[:, :, par, :], ALU.add)
                    Yp = k.bank(0)
                    for h in range(8):
                        yo = Yp[0:64, h * 64:(h + 1) * 64]
                        k.mm(yo, A1[0:64, h, 64:128], Ub[0:64, h, :], start=True, stop=False)
                        k.mm(yo, A2[0:64, h, 64:128], Vb[0:64, h * 64:(h + 1) * 64], start=False, stop=True)
                if smp:
                    evac_y(0, 64, Yp, YA)
                    for b in range(16):
                        bm, km = Bhm[b % 2], Khm[b % 2]
                        k.ts("dve", bm[0:64, :], Bh[0:64, :], sel[0:64, b:b + 1], None, ALU.mult)
                        k.ts("dve", km[0:64, :], Kh[0:64, :], sel[0:64, b:b + 1], None, ALU.mult)
                        Hn = k.bank(0)
                        for h in range(8):
                            hp, fp = h // 2, (h % 2) * 64
                            ho = Hn[fp:fp + 64, hp * 64:(hp + 1) * 64]
                            k.mm(ho, bm[0:64, h * 64:(h + 1) * 64], Ub[0:64, h, :], start=True, stop=False, tp=(0, fp))
                            k.mm(ho, km[0:64, h * 64:(h + 1) * 64], Vb[0:64, h * 64:(h + 1) * 64], start=False, stop=True, tp=(0, fp))
                        k.tt("dve", Hs[b][:], Hs[b][:], gC[:, :, b:b + 1].to_broadcast([128, 4, 64]), ALU.mult)
                        k.tt("dve", Hs[b][:], Hs[b][:], Hn[:, 0:256].rearrange("p (h v) -> p h v", h=4), ALU.add)
                        bk = k.bank(0)
                        for hp in range(4):
                            k.tr(bk[0:64, hp * 128:(hp + 1) * 128], Hs[b][:, hp, :], self.identf[:, :])
                        sn = Snat[b % 2]
                        k.copy("act", sn[:, :], bk[0:64, :])
                        k.dma("sp", self.wks[b], sn[:, :], "snat%d" % (b % 2))

                self.stop(6)
                yield "C_done"
                k.p.tag = "M1%s_t%dO" % (sfx, ti)
                k.red(st8[:nt, 24:32], h8(ysb[:nt, :]))
                k.ts("dve", st8[:nt, 24:32], st8[:nt, 24:32], -1.0 / 64, None, ALU.mult)
                k.tt("dve", h8(yc[:nt, :]), h8(ysb[:nt, :]), bc8(st8[:nt, 24:32]), ALU.add)
                k.tt("dve", ysb[:nt, :], yc[:nt, :], yc[:nt, :], ALU.mult)
                k.red(st8[:nt, 32:40], h8(ysb[:nt, :]))
                k.act(st8[:nt, 40:48], st8[:nt, 32:40], AF.Ln, scale=1.0 / 64, bias=64e-5)
                k.act(st8[:nt, 40:48], st8[:nt, 40:48], AF.Exp, scale=-0.5)
                yield "o"
                k.p.tag = "M1%s_t%dO" % (sfx, ti)
                k.tt("dve", h8(yc[:nt, :]), h8(yc[:nt, :]), bc8(st8[:nt, 40:48]), ALU.mult)
                k.tt("dve", yc[:nt, :], yc[:nt, :], bc["lnx_w"][:nt, :], ALU.mult)
                k.tt("dve", yc[:nt, :], yc[:nt, :], bc["lnx_b"][:nt, :], ALU.add)
                k.tt("dve", h8(ysb[:nt, :]), h8(Vb[:nt, :]), bc8(stA[:nt, 0:8]), ALU.mult)
                k.tt("dve", yc[:nt, :], yc[:nt, :], ysb[:nt, :], ALU.add)
                k.tt("dve", ya[:nt, :], yc[:nt, :], gg[:nt, :], ALU.mult)
                yield "o"
                k.p.tag = "M1%s_t%dO" % (sfx, ti)
                bk = k.bank(0)
                psb = bk[:].bitcast(BF16)
                for m in range(4):
                    k.tr(psb[:, m * 128:m * 128 + nt], ya[:nt, m * 128:(m + 1) * 128], self.identb[:nt, :nt])
                k.copy("act", yaT[:, :, col - 1:col - 1 + nt], psb[:, 0:512].rearrange("p (h t) -> p h t", h=4)[:, :, :nt])

            def adv(g, until):
                for tok in g:
                    if tok in until:
                        return tok
                return None
            gens = []
            col = 1
            for ti, (xt_, nt) in enumerate(tiles):
                gens.append(tile_gen(ti, xt_, nt, col))
                col += nt
            if smp:
                sg_ = state_gen()
                a_done = False
                s_done = False
                while not (a_done and s_done):
                    if not s_done:
                        for _ in range(2):
                            if adv(sg_, ("s",)) is None:
                                s_done = True
                                break
                    if not a_done:
                        if adv(gens[0], ("a", "A_done")) == "A_done":
                            a_done = True
            else:
                adv(gens[0], ("A_done",))
            pending_out = None
            for ti in range(len(gens)):
                g = gens[ti]
                nx = gens[ti + 1] if ti + 1 < len(gens) else None
                nx_done = nx is None
                nd = 0
                while True:
                    tok = adv(g, ("d", "D_done", "C_done", "c"))
                    if tok is None:
                        break
                    if tok == "c":
                        continue
                    if tok == "d":
                        nd += 1
                        if pending_out is not None and nd >= 2:
                            if adv(pending_out, ("o",)) is None:
                                pending_out = None
                        for _rep in range(1):
                            if not nx_done:
                                if adv(nx, ("a", "A_done")) == "A_done":
                                    nx_done = True
                    if tok == "D_done":
                        if pending_out is not None:
                            adv(pending_out, ())
                            pending_out = None
                    if tok == "C_done":
                        pending_out = g
                        break
                if not nx_done:
                    adv(nx, ("A_done",))
            if pending_out is not None:
                adv(pending_out, ())
            if (not smp) and last:
                bk = k.bank(1)
                for hp in range(4):
                    k.tr(bk[0:64, hp * 128:(hp + 1) * 128], self.Hst[:, hp, :], self.identf[:, :])
                k.copy("act", scr[0:64, 0:512], bk[0:64, :])
                k.dma("sp", self.wkp, scr[0:64, 0:512], "wkp_o")

    def mixer2(self, tiles, hT, yaT, kind, last, ncols, bi):
        k = self.k
        smp = kind == "s"
        sfx = "_s" if smp else "_p"
        k.p.tag = "M2%s_pre" % sfx
        hoist_f2 = (not smp) and HOIST_F2 and ("f2" in self.debug.get("stages", ("f1", "m1", "m2", "f2")))
        if hoist_f2:
            self.f2_pre_done = len(tiles)
        with k.scope():
            Wrg = []
            for g_ in range(4):
                t_ = k.sb("Wr%d" % g_, [128, 8, 512], BF16)
                k.dma("pool", t_[:].rearrange("p k n -> p (k n)"), self.win[4 + g_], "wrg%d" % g_)
                Wrg.append(t_)
            Wo = k.sb("Wo", [128, 8, D], BF16)
            k.dma("pool", Wo[:], self.w_out.rearrange("(k p) n -> p k n", p=128), "wo")
            gpb = k.sb("gpb2", [128, D], F32)
            k.dma("sp", gpb[:], self.normg[3:4, :].partition_broadcast(128), "gpb2")
            gnw = k.sb("gnw", [128, 512], F32)
            k.dma("sp", gnw[:], self.vec["gn_w"].partition_broadcast(128), "gnw")
            nbuf = 1 if smp else 2
            cst = [k.sb("cst%d" % i, [128, 128], F32) for i in range(2)]
            B2 = []
            for pb in range(nbuf):
                d = {}
                d["qkvg"] = k.sb("qkvg%d" % pb, [128, 2048], F32)
                d["ta"] = k.sb("ta%d" % pb, [128, 256], F32)
                d["tb"] = k.sb("tb%d" % pb, [128, 256], F32)
                for n in ("qrot", "krot", "kd", "vb", "yr"):
                    d[n] = k.sb("%s%d" % (n, pb), [128, 512], BF16)
                for n in ("inm", "qT", "kT", "qdT", "yrT"):
                    d[n] = k.sb("%s%d" % (n, pb), [128, 4, 128], BF16)
                for n in ("ysb", "yc", "eg"):
                    d[n] = k.sb("%s2_%d" % (n, pb), [128, 512], F32)
                d["st4"] = k.sb("st4_%d" % pb, [128, 32], F32)
                d["tY"] = [k.sb("tY2%d_%d" % (i, pb), [128, 512], F32) for i in range(2)]
                B2.append(d)
            dm = self.c("dm" + sfx)
            qdec = self.c("qdec" + sfx)
            kdec = self.kcd[:, 4:8] if smp else self.kcd[:, 0:4]
            cdec = self.kcd[:, 12:16] if smp else self.kcd[:, 8:12]
            sel = self.c("sel_s")
            if smp:
                Ss = [k.sb("Ss%d" % b, [128, 4, 128], F32) for b in range(16)]
                Ssb = k.sb("Ssb", [128, 16, 4, 128], BF16)
                qdTm = k.sb("qdTm", [128, 4, 16, 64], BF16)
                kdm = [k.sb("kdm%d" % i, [64, 512], BF16) for i in range(2)]
                colmask = self.c("colmask").rearrange("p (b t) -> p b t", b=16)
                for b in range(16):
                    k.dma("sp", Ss[b][:].rearrange("p h e -> p (h e)"), self.sret[b], "ssld%d" % b)
                    k.copy("act" if b % 2 == 0 else "dve", Ssb[:, b, :, :], Ss[b][:])
            def tile_gen(ti, xt_, nt, col):
                k.p.tag = "M2%s_t%d" % (sfx, ti)
                pl = ti % 2
                d = B2[ti % nbuf]
                qkvg, ta, tb, qrot, krot, kd, vb, yr = (d[n] for n in ("qkvg", "ta", "tb", "qrot", "krot", "kd", "vb", "yr"))
                inm, qT, kT, qdT, yrT = (d[n] for n in ("inm", "qT", "kT", "qdT", "yrT"))
                ysb, yc, eg, st4, tY = d["ysb"], d["yc"], d["eg"], d["st4"], d["tY"]
                ct = cst[ti % 2]
                if smp:
                    k.dma("sp", ct[:nt, :], self.css, "cst%d" % (ti % 2))
                else:
                    r0 = (bi * 8 + ti) * 128
                    k.dma("sp", ct[:nt, :], self.csp[r0:r0 + 128, :], "cst%d" % (ti % 2))
                cosb = ct[:nt, 0:64].unsqueeze(1).to_broadcast([nt, 4, 64])
                sinb = ct[:nt, 64:128].unsqueeze(1).to_broadcast([nt, 4, 64])
                tav = ta[:nt, :].rearrange("p (h d) -> p h d", h=4)
                tbv = tb[:nt, :].rearrange("p (h d) -> p h d", h=4)
                h4 = lambda ap: ap.rearrange("p (h e) -> p h e", h=4)
                bc4 = lambda ap: ap.unsqueeze(2).to_broadcast([nt, 4, 128])

                def rope(off, dst):
                    xv = qkvg[:nt, off:off + 512].rearrange("p (h a d) -> p h a d", h=4, a=2)
                    dv = dst[:nt, :].rearrange("p (h a d) -> p h a d", h=4, a=2)
                    k.tt("dve", tav, xv[:, :, 0, :], cosb, ALU.mult)
                    k.tt("dve", tbv, xv[:, :, 1, :], sinb, ALU.mult)
                    k.tt("dve", dv[:, :, 0, :], tav, tbv, ALU.subtract)
                    k.tt("dve", tav, xv[:, :, 0, :], sinb, ALU.mult)
                    k.tt("dve", tbv, xv[:, :, 1, :], cosb, ALU.mult)
                    k.tt("dve", dv[:, :, 1, :], tav, tbv, ALU.add)

                def transp(src, dstT):
                    bk = k.bank(pl)
                    psb = bk[:].bitcast(BF16)
                    for h in range(4):
                        k.tr(psb[:, h * 128:h * 128 + nt], src[:nt, h * 128:(h + 1) * 128], self.identb[:nt, :nt])
                    k.copy("act", dstT[:, :, :nt], psb[:, 0:512].rearrange("p (h t) -> p h t", h=4)[:, :, :nt])
                for gi in range(4):
                    bk = k.bank(pl)
                    for kk in range(8):
                        k.mm(bk[:nt, :], hT[:, kk, col:col + nt], Wrg[gi][:, kk, :], start=kk == 0, stop=kk == 7,
                             rk=[("hTt", ti), Wrg[gi][:]])
                    k.copy("act", qkvg[:nt, gi * 512:(gi + 1) * 512], bk[:nt, :])
                    if gi == 1:
                        rope(0, qrot)
                    if gi == 2:
                        rope(512, krot)
                        k.tt("dve", h4(kd[:nt, :]), h4(krot[:nt, :]), bc4(kdec[:nt, :]), ALU.mult)
                        transp(qrot, qT)
                    if gi == 3:
                        k.copy("act", vb[:nt, :], qkvg[:nt, 1024:1536])
                        transp(krot, kT)
                    yield "y"
                    k.p.tag = "M2%s_t%d" % (sfx, ti)

                yield "y"
                k.p.tag = "M2%s_t%d" % (sfx, ti)
                qdv = qdec.rearrange("p (h t) -> p h t", h=4)
                k.tt("dve", qdT[:, :, :nt], qT[:, :, :nt], qdv, ALU.mult)
                bk = k.bank(pl)
                for h in range(4):
                    k.mm(bk[:nt, h * 128:h * 128 + nt], kT[:, h, :nt], qT[:, h, :nt])
                k.tt("dve", inm[:nt, :, :nt], bk[:nt, :].rearrange("p (h t) -> p h t", h=4)[:, :, :nt],
                     dm[:nt, :].rearrange("p (h t) -> p h t", h=4), ALU.mult)
                yield "S0"
                k.p.tag = "M2%s_t%d" % (sfx, ti)
                Yb = k.bank(pl)
                if smp:
                    k.tt("dve", qdTm[:], qdT[:, :, 0:64].unsqueeze(2).to_broadcast([128, 4, 16, 64]),
                         colmask.unsqueeze(1).to_broadcast([128, 4, 16, 64]), ALU.mult)
                for h in range(4):
                    yo = Yb[:nt, h * 128:(h + 1) * 128]
                    k.mm(yo, inm[:nt, h, :nt], vb[:nt, h * 128:(h + 1) * 128], start=True, stop=False)
                    if smp:
                        for b in range(16):
                            k.mm(yo, qdTm[:, h, b, :], Ssb[:, b, h, :], start=False, stop=b == 15)
                    else:
                        k.mm(yo, qdT[:, h, :nt], self.Sb[:, h, :], start=False, stop=True)
                k.copy("act", ysb[:nt, :], Yb[:nt, :])
                cdb = cdec.unsqueeze(2).to_broadcast([128, 4, 128])
                if smp:
                    for b in range(16):
                        km = kdm[b % 2]
                        k.ts("dve", km[:, :], kd[0:64, :], sel[0:64, b:b + 1], None, ALU.mult)
                        Sn = k.bank(pl)
                        for h in range(4):
                            k.mm(Sn[:, h * 128:(h + 1) * 128], km[0:64, h * 128:(h + 1) * 128], vb[0:64, h * 128:(h + 1) * 128])
                        k.tt("dve", Ss[b][:], Ss[b][:], cdb, ALU.mult)
                        k.tt("dve", Ss[b][:], Ss[b][:], Sn[:, :].rearrange("p (h e) -> p h e", h=4), ALU.add)
                        k.dma("sp", self.rts[b], Ss[b][:].rearrange("p h e -> p (h e)"), "rts_o%d" % (b % 4))
                else:
                    Sn = k.bank(pl)
                    for h in range(4):
                        k.mm(Sn[:, h * 128:(h + 1) * 128], kd[:nt, h * 128:(h + 1) * 128], vb[:nt, h * 128:(h + 1) * 128])
                    k.tt("dve", self.Sst[:], self.Sst[:], cdb, ALU.mult)
                    k.tt("dve", self.Sst[:], self.Sst[:], Sn[:, :].rearrange("p (h e) -> p h e", h=4), ALU.add)
                    k.copy("act", self.Sb[:], self.Sst[:])
                yield "S1"
                k.p.tag = "M2%s_t%d" % (sfx, ti)
                k.red(st4[:nt, 0:4], h4(ysb[:nt, :]))
                k.ts("dve", st4[:nt, 0:4], st4[:nt, 0:4], -1.0 / 128, None, ALU.mult)
                k.tt("dve", h4(yc[:nt, :]), h4(ysb[:nt, :]), bc4(st4[:nt, 0:4]), ALU.add)
                k.tt("dve", ysb[:nt, :], yc[:nt, :], yc[:nt, :], ALU.mult)
                k.red(st4[:nt, 4:8], h4(ysb[:nt, :]))
                k.act(st4[:nt, 8:12], st4[:nt, 4:8], AF.Ln, scale=1.0 / 128, bias=1e-5)
                k.act(st4[:nt, 8:12], st4[:nt, 8:12], AF.Exp, scale=-0.5)
                k.tt("dve", h4(yc[:nt, :]), h4(yc[:nt, :]), bc4(st4[:nt, 8:12]), ALU.mult)
                k.tt("dve", yc[:nt, :], yc[:nt, :], gnw[:nt, :], ALU.mult)

                yield "y"
                k.p.tag = "M2%s_t%d" % (sfx, ti)
                g_ = qkvg[:nt, 1536:2048]
                k.act(eg[:nt, :], g_, AF.Silu)
                k.tt("dve", yr[:nt, :], eg[:nt, :], yc[:nt, :], ALU.mult)
                bk = k.bank(pl)
                psb = bk[:].bitcast(BF16)
                for m in range(4):
                    k.tr(psb[:, m * 128:m * 128 + nt], yr[:nt, m * 128:(m + 1) * 128], self.identb[:nt, :nt])
                k.copy("act", yrT[:, :, :nt], psb[:, 0:512].rearrange("p (h t) -> p h t", h=4)[:, :, :nt])

                yield "y"
                k.p.tag = "M2%s_t%d" % (sfx, ti)
                Y = [k.bank(pl), k.bank(pl)]
                for j in range(2):
                    for m in range(8):
                        lhs = yaT[:, m, col - 1:col - 1 + nt] if m < 4 else yrT[:, m - 4, :nt]
                        k.mm(Y[j][:nt, :], lhs, Wo[:, m, j * 512:(j + 1) * 512], start=m == 0, stop=m == 7)
                self.postnorm_residual(Y, xt_, nt, gpb, 1.0, tY)
                if hoist_f2:
                    self.prenorm(xt_, nt, 4, hT[:, :, col:col + nt], wk=[("hTt", ti)], pool=pl)

            def adv(g, until):
                for tok in g:
                    if tok in until:
                        return tok
                return None
            gens = []
            col = 1
            for ti, (xt_, nt) in enumerate(tiles):
                gens.append(tile_gen(ti, xt_, nt, col))
                col += nt
            adv(gens[0], ("S0",))
            for ti in range(len(gens)):
                cur = gens[ti]
                nxt = gens[ti + 1] if ti + 1 < len(gens) else None
                adv(cur, ("S1",))
                cur_done = False
                nxt_ready = nxt is None
                while not (cur_done and nxt_ready):
                    if not nxt_ready:
                        if adv(nxt, ("y", "S0")) == "S0":
                            nxt_ready = True
                    if not cur_done:
                        if adv(cur, ("y",)) is None:
                            cur_done = True
            if (not smp) and last:
                k.dma("sp", self.rtp, self.Sst[:].rearrange("p h e -> p (h e)"), "rtp_o")


_CACHE = {}


def _get_nc(debug=None):
    key = repr(sorted((debug or {}).items()))
    if key not in _CACHE:
        b = Builder(debug)
        b.build()
        _CACHE[key] = b
    return _CACHE[key]


def _in_maps(inp):
    f = lambda a: np.ascontiguousarray(np.asarray(a, dtype=np.float32))
    ct = lambda a: f(f(a)[0].reshape(8, 128, NCH, 128).transpose(2, 1, 0, 3).reshape(NCH, 128, 1024))
    gu = lambda a, b: f(np.stack([ct(a), ct(b)], axis=2).reshape(NCH, 128, 2048))
    cs_p, cs_s = _rope_tables()
    ng = f(inp["norm_g"])[0]
    shared = {
        "gT": f(ng.reshape(6, 8, 128).transpose(2, 0, 1).reshape(128, 48)),
        "normg": ng,
        "f1gu": gu(inp["ffn1_wg"], inp["ffn1_wu"]), "f1d": f(inp["ffn1_wd"])[0],
        "f2gu": gu(inp["ffn2_wg"], inp["ffn2_wu"]), "f2d": f(inp["ffn2_wd"])[0],
        "w_out": f(inp["w_out"])[0],
        "mu": f(inp["mu_shift"]),
        "muT": f(f(inp["mu_shift"])[0, 1536:1792].reshape(2, 128).T),
        "w0": f(inp["w0"]), "a0": f(inp["a0"]), "k_k": f(inp["k_k"]), "k_a": f(inp["k_a"]),
        "r_k": f(inp["r_k"]).reshape(1, 512), "lnx_w": f(inp["lnx_w"]), "lnx_b": f(inp["lnx_b"]),
        "gn_w": f(inp["ret_gn_w"]),
        "w2": f(inp["w2"])[0], "a2": f(inp["a2"])[0], "g2": f(inp["g2"])[0],
        "cpack": CPACK, "cs_p": cs_p, "cs_s": cs_s,
    }
    wi = f(inp["w_in"])[0]
    a_ = 0
    for g_, n_ in enumerate((512, 512, 512, 256, 512, 512, 512, 512)):
        shared["win%d" % g_] = f(wi[:, a_:a_ + n_].reshape(8, 128, n_).transpose(1, 0, 2).reshape(128, 8 * n_))
        a_ += n_
    xp = f(inp["x_prompt"])
    xs = f(inp["x_sample"])
    ssh = f(inp["state_shift"])[0]
    swk = f(inp["state_wkv"])[0]
    srt = f(inp["state_ret"])[0]
    maps = []
    for c in range(8):
        m = dict(shared)
        m["xp"] = xp[c]
        m["xs"] = f(xs[16 * c:16 * (c + 1)].reshape(64, D))
        m["sshift"] = f(ssh[16 * c:16 * (c + 1)])
        m["swkv"] = f(swk[16 * c:16 * (c + 1)].transpose(0, 2, 1, 3).reshape(16, 64, 512))
        m["sret"] = f(srt[16 * c:16 * (c + 1)].transpose(0, 2, 1, 3).reshape(16, 128, 512))
        maps.append(m)
    return maps


def kernel(**inputs):
    b = _get_nc()
    maps = _in_maps(inputs)
    res = run_bass_kernel_spmd(b.nc, maps, core_ids=list(range(8)))
    R = res.results
    yp = np.stack([R[c]["yp"] for c in range(8)]).astype(np.float32)
    ys = np.concatenate([R[c]["ys"].reshape(16, 4, D) for c in range(8)]).astype(np.float32)
    shp = np.stack([R[c]["shp"].reshape(D) for c in range(8)])[None].astype(np.float32)
    wkp = np.stack([R[c]["wkp"].reshape(64, 8, 64).transpose(1, 0, 2) for c in range(8)])[None].astype(np.float32)
    rtp = np.stack([R[c]["rtp"].reshape(128, 4, 128).transpose(1, 0, 2) for c in range(8)])[None].astype(np.float32)
    shs = np.concatenate([R[c]["shs"] for c in range(8)])[None].astype(np.float32)
    wks = np.concatenate([R[c]["wks"].reshape(16, 64, 8, 64).transpose(0, 2, 1, 3) for c in range(8)])[None].astype(np.float32)
    rts = np.concatenate([R[c]["rts"].reshape(16, 128, 4, 128).transpose(0, 2, 1, 3) for c in range(8)])[None].astype(np.float32)
    return (yp, ys, shp, wkp, rtp, shs, wks, rts)
```

```python
import contextlib
import numpy as np
import concourse.bass as bass
import concourse.mybir as mybir
from concourse.bass_utils import run_bass_kernel_spmd

F32 = mybir.dt.float32
BF16 = mybir.dt.bfloat16
ALU = mybir.AluOpType
AF = mybir.ActivationFunctionType
AX = mybir.AxisListType

SAME_ENG_SYNC = True
SCHED = True
HOIST = False
HOIST_F2 = False
SCHED_XL = 0.85
MM_OVH = 0.08
SCHED_TAGS = ("M1", "M2", "F")
SAME_ENG_MIN_DIST = 0
ANNOTATE = False
D = 1024
DFF = 2816
NCH = 22
NRING = 3
EPS = 1e-6
C0 = float(np.exp(-0.5))


class Prog:
    ENGS = ("pe", "act", "dve", "pool", "sp")

    def __init__(self, nc):
        self.nc = nc
        self.ops = []
        self.writers = {}
        self.readers = {}
        self.last = {}
        self.dma_pending = []
        self.bar_nop = {}
        self.last_q = {}
        self.seg_start = 0

    @staticmethod
    def _key(a):
        if isinstance(a, (str, tuple)):
            return a
        if "DRam" in type(a.tensor).__name__:
            return None
        return a.name

    def op(self, eng, fn, r=(), w=(), dma=None, extra=(), cost=0.3, lat=0.0, tbl=None):
        idx = len(self.ops)
        rk = [k for k in (self._key(a) for a in r) if k is not None]
        wk = [k for k in (self._key(a) for a in w) if k is not None]
        deps = set(extra)
        for k in rk:
            deps.update(self.writers.get(k, ()))
            if isinstance(k, str) and k.startswith("ps"):
                deps.update(j for j in self.readers.get(k, ()) if self.ops[j]["eng"] != eng)
        for k in wk:
            deps.update(self.writers.get(k, ()))
            deps.update(self.readers.get(k, ()))
        if SCHED:
            b = self.bar_nop.get(eng)
            if b is not None:
                deps.add(b)
        self.ops.append(dict(eng=eng, fn=fn, deps=deps, dma=dma, tag=getattr(self, "tag", ""), cost=cost, lat=lat, tbl=tbl))
        if eng in ("sp", "pool"):
            self.last_q[eng] = idx
        for k in rk:
            if k in wk:
                continue
            lst = self.readers.setdefault(k, [])
            if dma is None and not SCHED:
                lst[:] = [j for j in lst if not (self.ops[j]["eng"] == eng and self.ops[j]["dma"] is None)]
            lst.append(idx)
        for k in wk:
            self.writers[k] = [idx]
            self.readers[k] = []
        if dma is None:
            self.last[eng] = idx
        else:
            self.dma_pending.append(idx)
        return idx

    def barrier(self):
        if getattr(self, "_bar_at", -1) == len(self.ops):
            return
        lasts = dict(self.last)
        pend = list(self.dma_pending)
        self.dma_pending = []
        if SCHED:
            allprev = list(range(self.seg_start, len(self.ops)))
        for e in self.ENGS:
            ex = [v for (q, v) in lasts.items()] + pend
            if SCHED:
                ex = allprev
            i_ = self.op(e, lambda eng: eng.nop(), extra=ex, cost=0.05)
            self.ops[i_]["bar"] = True
            self.bar_nop[e] = i_
        self._bar_at = len(self.ops)
        self.seg_start = len(self.ops)

    def schedule(self):
        ops = self.ops
        n = len(ops)
        succ = [[] for _ in range(n)]
        indeg = [0] * n
        lastE = {}
        sdeps = []
        for i, o in enumerate(ops):
            o["deps"] = set(j for j in o["deps"] if j != i)
            sd = set(o["deps"])
            if o["eng"] in ("sp", "pool") or not any(o["tag"].startswith(p_) for p_ in SCHED_TAGS):
                if o["eng"] in lastE:
                    sd.add(lastE[o["eng"]])
            lastE[o["eng"]] = i
            sdeps.append(sd)
            indeg[i] = len(sd)
            for j in sd:
                succ[j].append(i)
        XL = SCHED_XL
        bl = [0.0] * n
        for i in range(n - 1, -1, -1):
            o = ops[i]
            m_ = 0.0
            for s in succ[i]:
                x = bl[s] + (XL if ops[s]["eng"] != o["eng"] or o["dma"] is not None else 0.05)
                if x > m_:
                    m_ = x
            bl[i] = o["cost"] + o["lat"] + m_
        finish = [0.0] * n
        ready_t = [0.0] * n
        free = {e: 0.0 for e in self.ENGS}
        rdy = {e: [] for e in self.ENGS}
        for i in range(n):
            if indeg[i] == 0:
                rdy[ops[i]["eng"]].append(i)
        order = {e: [] for e in self.ENGS}
        done = 0
        cur_tbl = [None]
        while done < n:
            best = None
            for e in self.ENGS:
                lst = rdy[e]
                if not lst:
                    continue
                tmin = min(ready_t[i] for i in lst)
                t_e = max(free[e], tmin)
                pick = None
                pk = None
                for i in lst:
                    if ready_t[i] <= t_e + 1e-9:
                        tb = ops[i]["tbl"]
                        same = 1 if (e != "act" or tb is None or tb == cur_tbl[0]) else 0
                        key = (same, bl[i], -i)
                        if pick is None or key > pk:
                            pick, pk = i, key
                if best is None or (t_e, pick) < (best[0], best[1]):
                    best = (t_e, pick, e)
            st, i, e = best
            rdy[e].remove(i)
            o = ops[i]
            if e == "act" and o["tbl"] is not None and o["tbl"] != cur_tbl[0]:
                cur_tbl[0] = o["tbl"]
                st += 1.3
            free[e] = st + o["cost"]
            finish[i] = st + o["cost"] + o["lat"]
            order[e].append(i)
            done += 1
            for s in succ[i]:
                indeg[s] -= 1
                if i in ops[s]["deps"]:
                    x = finish[i] + (XL if ops[s]["eng"] != e or o["dma"] is not None else 0.05)
                else:
                    x = st + o["cost"]
                if x > ready_t[s]:
                    ready_t[s] = x
                if indeg[s] == 0:
                    rdy[ops[s]["eng"]].append(s)
        self.sim_time = max(finish) if n else 0.0
        return order

    def emit(self):
        nc = self.nc
        ops = self.ops
        sched_order = self.schedule() if SCHED else None

        pos = {}
        per = {e: [] for e in self.ENGS}
        if sched_order is not None:
            per = sched_order
        else:
            for i, o in enumerate(ops):
                per[o["eng"]].append(i)
        for e in self.ENGS:
            for p_, i in enumerate(per[e]):
                pos[i] = p_
                ops[i]["idx"] = i
        if SCHED:
            for o in ops:
                if o.get("bar"):
                    keep = {}
                    nd = set()
                    for j in o["deps"]:
                        pj = ops[j]
                        if pj["dma"] is not None:
                            nd.add(j)
                        elif pj["eng"] not in keep or pos[j] > pos[keep[pj["eng"]]]:
                            keep[pj["eng"]] = j
                    o["deps"] = nd | set(keep.values())

        def elide(pj, o):
            if not (pj["dma"] is None and o["dma"] is None and pj["eng"] == o["eng"]):
                return False
            if pj["eng"] == "pe" or not SAME_ENG_SYNC:
                return True
            return SAME_ENG_MIN_DIST > 0 and (pos[o["idx"]] - pos[pj["idx"]]) >= SAME_ENG_MIN_DIST

        needed = [False] * len(ops)
        for i, o in enumerate(ops):
            for j in o["deps"]:
                if not elide(ops[j], o):
                    needed[j] = True
        engsem, dmasem, dmacnt, final_dma = {}, {}, {}, {}
        cnt = {e: 0 for e in self.ENGS}
        val = [0] * len(ops)
        semof = [None] * len(ops)
        num_seq = [i for e in self.ENGS for i in per[e]]
        for i in num_seq:
            o = ops[i]
            if o["dma"] is not None:
                g = o["dma"]
                if g not in dmasem:
                    dmasem[g] = nc.alloc_semaphore(name="d%d" % len(dmasem))
                    dmacnt[g] = 0
                dmacnt[g] += 16
                val[i] = dmacnt[g]
                semof[i] = dmasem[g]
                final_dma[g] = dmacnt[g]
            elif needed[i]:
                e = o["eng"]
                if e not in engsem:
                    engsem[e] = nc.alloc_semaphore(name="e_" + e)
                cnt[e] += 1
                val[i] = cnt[e]
                semof[i] = engsem[e]
        self.n_sems = len(dmasem) + len(engsem)

        def run(engname, eng):
            waited = {}
            for i in per[engname]:
                o = ops[i]
                best = {}
                for j in o["deps"]:
                    if semof[j] is None or elide(ops[j], o):
                        continue
                    s = semof[j]
                    if val[j] > best.get(id(s), (0, None))[0]:
                        best[id(s)] = (val[j], s)
                for sid, (v, s) in best.items():
                    if waited.get(sid, 0) >= v:
                        continue
                    waited[sid] = v
                    eng.wait_ge(s, v)
                ins = o["fn"](eng)
                if ANNOTATE and o["tag"]:
                    ins.annotate(o["tag"])
                if o["dma"] is not None:
                    ins.then_inc(semof[i], 16)
                elif semof[i] is not None:
                    ins.then_inc(semof[i], 1)
            if engname == "sp":
                for g, v in final_dma.items():
                    eng.wait_ge(dmasem[g], v)

        with nc.Block() as block:
            @block.tensor
            def _(e):
                run("pe", e)

            @block.scalar
            def _(e):
                run("act", e)

            @block.vector
            def _(e):
                run("dve", e)

            @block.gpsimd
            def _(e):
                run("pool", e)

            @block.sync
            def _(e):
                run("sp", e)


def _fs(ap):
    n = 1
    for s in ap.shape[1:]:
        n *= s
    return n


def _ec(ap):
    return _fs(ap) / 900.0 + 0.15


def _eca(ap):
    return _fs(ap) / 1050.0 + 0.11


class K:
    def __init__(self, nc):
        self.nc = nc
        self.p = Prog(nc)
        self._stack = [[]]
        self.ps_banks = []
        self.ps_i = 0

    def sb(self, name, shape, dt):
        self._uid = getattr(self, "_uid", 0) + 1
        g = self.nc.sbuf_tensor("%s_%d" % (name, self._uid), list(shape), dt)
        t = g.__enter__()
        self._stack[-1].append(g)
        return t

    def psum(self, name, shape, dt):
        g = self.nc.psum_tensor(name, list(shape), dt)
        t = g.__enter__()
        self._stack[-1].append(g)
        return t

    @contextlib.contextmanager
    def scope(self):
        self.p.barrier()
        self._stack.append([])
        try:
            yield
        finally:
            self.p.barrier()
            for g in reversed(self._stack.pop()):
                g.__exit__(None, None, None)

    def bank(self, pool=None):
        if pool is None:
            b = self.ps_banks[self.ps_i % len(self.ps_banks)]
            self.ps_i += 1
            return b
        self._pi = getattr(self, "_pi", [0, 0])
        b = self.ps_banks[pool * 4 + self._pi[pool] % 4]
        self._pi[pool] += 1
        return b

    def mm(self, out, lhsT, rhs, start=True, stop=True, tp=None, rk=None):
        kw = {}
        if tp is not None:
            kw["tile_position"] = tp
        self.p.op("pe", lambda e: e.matmul(out, lhsT, rhs, start=start, stop=stop, **kw),
                  r=[lhsT, rhs] if rk is None else rk, w=[out], cost=max(_fs(rhs), 64) / 2300.0 + MM_OVH, lat=0.15)

    def tr(self, out, in_, ident):
        self.p.op("pe", lambda e: e.transpose(out, in_, ident), r=[in_, ident], w=[out], cost=0.1, lat=0.15)

    def act(self, out, in_, func, bias=None, scale=None, accum=None):
        kw = {}
        if bias is not None:
            kw["bias"] = bias
        if scale is not None:
            kw["scale"] = scale
        if accum is not None:
            kw["accum_out"] = accum
        rr = [in_] + [a for a in (bias, scale) if not isinstance(a, (int, float, type(None)))]
        ww = [out] + ([accum] if accum is not None else [])
        tbl = {AF.Sigmoid: "sig", AF.Tanh: "sig", AF.Exp: "exp", AF.Ln: "exp", AF.Silu: "silu"}.get(func)
        self.p.op("act", lambda e: e.activation(out, in_, func, **kw), r=rr, w=ww, cost=_ec(out), tbl=tbl)

    def tt(self, eng, out, in0, in1, op, wk=None):
        self.p.op(eng, lambda e: e.tensor_tensor(out, in0, in1, op), r=[in0, in1], w=[out] if wk is None else wk, cost=_ec(out))

    def ts(self, eng, out, in0, s1, s2, op0, op1=None):
        rr = [in0] + [a for a in (s1, s2) if not isinstance(a, (int, float, type(None)))]
        kw = {}
        if op1 is not None:
            kw["op1"] = op1
        self.p.op(eng, lambda e: e.tensor_scalar(out, in0, s1, s2, op0, **kw), r=rr, w=[out], cost=_ec(out))

    def stt(self, eng, out, in0, scalar, in1, op0, op1):
        rr = [in0, in1] + ([] if isinstance(scalar, (int, float)) else [scalar])
        self.p.op(eng, lambda e: e.scalar_tensor_tensor(out, in0, scalar, in1, op0, op1), r=rr, w=[out], cost=_ec(out))

    def copy(self, eng, out, in_):
        if eng == "act":
            self.p.op("act", lambda e: e.copy(out, in_), r=[in_], w=[out], cost=_ec(out))
        else:
            self.p.op(eng, lambda e: e.tensor_copy(out, in_), r=[in_], w=[out], cost=_ec(out))

    def recip(self, out, in_):
        self.p.op("dve", lambda e: e.reciprocal(out, in_), r=[in_], w=[out], cost=5 * _ec(out))

    def red(self, out, in_):
        self.p.op("dve", lambda e: e.tensor_reduce(out, in_, AX.X, ALU.add), r=[in_], w=[out], cost=_ec(in_))

    def memset(self, eng, ap, v):
        self.p.op(eng, lambda e: e.memset(ap, v), r=[], w=[ap], cost=_ec(ap))

    def dma(self, q, out, in_, grp):
        self.p.op(q, lambda e: e.dma_start(out=out, in_=in_), r=[in_], w=[out], dma=grp, cost=0.3,
                  lat=2.0 + _fs(out) * 128 * 4 / 150e3)


def _pack_consts():
    P = 128
    items = {}
    p = np.arange(P)
    col = np.arange(128)
    items["ident"] = np.eye(P, dtype=np.float32)
    s = (p % 64)[:, None]
    t = (col % 64)[None, :]
    items["mA_p"] = np.where(col[None, :] < 64, s < t, s <= t).astype(np.float32)
    items["mN_p"] = (np.arange(64)[None, :] < s).astype(np.float32)
    sb_, tb_ = (p // 4)[:, None], ((col % 64) // 4)[None, :]
    s4, t4 = (p % 4)[:, None], ((col % 64) % 4)[None, :]
    mA_s = np.where(col[None, :] < 64, s4 < t4, s4 <= t4) & (sb_ == tb_) & (p[:, None] < 64)
    items["mA_s"] = mA_s.astype(np.float32)
    c64 = np.arange(64)
    items["mN_s"] = (((c64 % 4)[None, :] < s4) & ((c64 // 4)[None, :] == sb_) & (p[:, None] < 64)).astype(np.float32)
    S_, T_ = p[:, None], col[None, :]
    same = (S_ // 64) == (T_ // 64)
    items["triI_p"] = (same & (S_ <= T_)).astype(np.float32)
    items["triX_p"] = (same & (S_ < T_)).astype(np.float32)
    items["triR_p"] = (same & (S_ > T_)).astype(np.float32)
    same = ((S_ // 4) == (T_ // 4)) & (S_ < 64) & (T_ < 64)
    items["triI_s"] = (same & (S_ <= T_)).astype(np.float32)
    items["triX_s"] = (same & (S_ < T_)).astype(np.float32)
    items["triR_s"] = (same & (S_ > T_)).astype(np.float32)
    items["sel_p"] = ((p // 64)[:, None] == np.arange(2)[None, :]).astype(np.float32)
    items["sel_s"] = (((p // 4)[:, None] == np.arange(16)[None, :]) & (p[:, None] < 64)).astype(np.float32)
    cm = ((c64 // 4)[None, :] == np.arange(16)[:, None]).astype(np.float32)
    items["colmask"] = np.broadcast_to(cm.reshape(1, 16 * 64), (P, 16 * 64)).copy()
    lg = np.log1p(-np.exp2(-5.0 - np.arange(4, dtype=np.float32))).astype(np.float32)
    scale = np.float32(128.0 ** -0.5)
    i_ = np.arange(128, dtype=np.float32)
    diff = i_[None, :] - i_[:, None]
    dm = np.zeros((P, 4, 128), np.float32)
    for h in range(4):
        dm[:, h, :] = np.where(diff >= 0, np.exp(lg[h] * np.maximum(diff, 0.0)), 0.0) * scale
    items["dm_p"] = dm.reshape(P, 512)
    dms = np.zeros((P, 4, 64), np.float32)
    jj, ii = np.arange(64)[:, None], np.arange(64)[None, :]
    d4 = (ii % 4 - jj % 4).astype(np.float32)
    okm = (jj // 4 == ii // 4) & (d4 >= 0)
    for h in range(4):
        dms[:64, h, :] = np.where(okm, np.exp(lg[h] * np.maximum(d4, 0.0)), 0.0) * scale
    items["dm_s"] = dms.reshape(P, 256)
    qd = np.zeros((P, 4, 128), np.float32)
    qs = np.zeros((P, 4, 64), np.float32)
    kd = np.zeros((P, 4), np.float32)
    ks = np.zeros((P, 4), np.float32)
    cdp = np.zeros((P, 4), np.float32)
    cds = np.zeros((P, 4), np.float32)
    for h in range(4):
        qd[:, h, :] = np.exp(lg[h] * (i_ + 1.0))[None, :]
        qs[:, h, :] = np.exp(lg[h] * ((np.arange(64) % 4).astype(np.float32) + 1.0))[None, :]
        kd[:, h] = np.exp(lg[h] * (127.0 - i_)) * scale
        ks[:, h] = np.exp(lg[h] * (3.0 - (p % 4).astype(np.float32))) * scale
        cdp[:, h] = np.exp(lg[h] * 128.0)
        cds[:, h] = np.exp(lg[h] * 4.0)
    items["qdec_p"] = qd.reshape(P, 512)
    items["qdec_s"] = qs.reshape(P, 256)
    items["kdec_p"] = kd
    items["kdec_s"] = ks
    items["cdec_p"] = cdp
    items["cdec_s"] = cds
    offs = {}
    o = 0
    for k_, v in items.items():
        offs[k_] = (o, v.shape[1])
        o += v.shape[1]
    pack = np.concatenate([items[k_] for k_ in items], axis=1).astype(np.float32)
    return pack, offs


def _rope_tables():
    half = 64
    inv = (np.float32(10000.0) ** (-np.arange(half, dtype=np.float32) / np.float32(half))).astype(np.float32)
    pos_p = np.arange(2048, dtype=np.float32)
    ang = (pos_p[:, None] * inv[None, :]).astype(np.float32)
    cs_p = np.concatenate([np.cos(ang), np.sin(ang)], axis=1).astype(np.float32)
    pos_s = (16384 + (np.arange(64) % 4)).astype(np.float32)
    ang = (pos_s[:, None] * inv[None, :]).astype(np.float32)
    cs_s = np.concatenate([np.cos(ang), np.sin(ang)], axis=1).astype(np.float32)
    return cs_p, cs_s


CPACK, COFF = _pack_consts()
NCP = CPACK.shape[1]


class _Stop(Exception):
    pass


class Builder:
    def __init__(self, debug=None):
        nc = bass.Bass("TRN2", target_bir_lowering=False)
        self.nc = nc
        self.k = K(nc)
        self.debug = debug or {}
        self.dbg_outs = {}

        def di(name, shape):
            return nc.dram_tensor(name, list(shape), F32, kind="ExternalInput").ap()

        def do(name, shape):
            return nc.dram_tensor(name, list(shape), F32, kind="ExternalOutput").ap()

        self.xp = di("xp", [2048, D])
        self.xs = di("xs", [64, D])
        self.sshift = di("sshift", [16, D])
        self.swkv = di("swkv", [16, 64, 512])
        self.sret = di("sret", [16, 128, 512])
        self.gTd = di("gT", [128, 48])
        self.normg = di("normg", [6, D])
        self.fw = {1: (di("f1gu", [NCH, 128, 2048]), None, di("f1d", [DFF, D])),
                   2: (di("f2gu", [NCH, 128, 2048]), None, di("f2d", [DFF, D]))}
        self.win_n = (512, 512, 512, 256, 512, 512, 512, 512)
        self.win = [di("win%d" % g, [128, 8 * n_]) for g, n_ in enumerate(self.win_n)]
        self.w_out = di("w_out", [D, D])
        self.mu = di("mu", [1, 1792])
        self.muTd = di("muT", [128, 2])
        self.vec = {n: di(n, [1, 512]) for n in ("w0", "a0", "k_k", "k_a", "r_k", "lnx_w", "lnx_b", "gn_w")}
        self.w2 = di("w2", [64, 512])
        self.a2 = di("a2", [64, 512])
        self.g2 = di("g2", [128, 512])
        self.cpack = di("cpack", [128, NCP])
        self.csp = di("cs_p", [2048, 128])
        self.css = di("cs_s", [64, 128])
        self.yp = do("yp", [2048, D])
        self.ys = do("ys", [64, D])
        self.shp = do("shp", [1, D])
        self.wkp = do("wkp", [64, 512])
        self.rtp = do("rtp", [128, 512])
        self.shs = do("shs", [16, D])
        self.wks = do("wks", [16, 64, 512])
        self.rts = do("rts", [16, 128, 512])

    def dbg(self, name, ap, shape):
        if name not in self.debug:
            return
        o = self.nc.dram_tensor("dbg_" + name, list(shape), F32, kind="ExternalOutput").ap()
        t = self.k.sb("dbgt_" + name, list(shape), F32)
        self.k.copy("dve", t[:], ap)
        self.k.dma("sp", o, t[:], "dbg_" + name)
        self.dbg_outs[name] = "dbg_" + name

    def c(self, name):
        o, n = COFF[name]
        return self.cb[:, o:o + n]

    def build(self):
        k = self.k
        for i in range(8):
            k.ps_banks.append(k.psum("ps%d" % i, [128, 512], F32))
        self.cb = k.sb("cb", [128, NCP], BF16)
        self.identf = k.sb("identf", [128, 128], F32)
        self.kcd = k.sb("kcd", [128, 16], F32)
        self.gT = k.sb("gTs", [128, 48], F32)
        self.muT = k.sb("muTs", [128, 2], F32)
        self.Hst = k.sb("Hst", [128, 4, 64], F32)
        self.Hb = k.sb("Hb", [128, 4, 64], BF16)
        self.Sst = k.sb("Sst", [128, 4, 128], F32)
        self.Sb = k.sb("Sb", [128, 4, 128], BF16)
        self.hprev = k.sb("hprev", [128, 8], BF16)
        self.xn = k.sb("xn", [128, D], BF16)
        self.junk = k.sb("junk", [128, D], BF16)
        self.ss = k.sb("ss", [128, 8], F32)
        self.identb = self.c("ident")
        with k.scope():
            st = k.sb("cstage", [128, NCP], F32)
            k.dma("sp", st[:], self.cpack, "c0")
            k.dma("sp", self.gT[:], self.gTd, "c1")
            k.dma("sp", self.muT[:], self.muTd, "c2")
            k.copy("dve", self.cb[:], st[:])
            o, n = COFF["ident"]
            k.copy("act", self.identf[:], st[:, o:o + n])
            o, _ = COFF["kdec_p"]
            k.copy("act", self.kcd[:], st[:, o:o + 16])
        k.memset("dve", self.Hst[:], 0.0)
        k.memset("dve", self.Hb[:], 0.0)
        k.memset("dve", self.Sst[:], 0.0)
        k.memset("dve", self.Sb[:], 0.0)
        k.memset("dve", self.hprev[:], 0.0)
        self.xs_t = k.sb("xs_t", [128, D], F32)
        blocks = self.debug.get("blocks", ["s", "p0", "p1"])
        stages = self.debug.get("stages", ("f1", "m1", "m2", "f2"))
        k.dma("sp", self.xs_t[:64, :], self.xs, "xs_l")
        stile = (self.xs_t, 64)
        for bi in range(2):
            if ("p%d" % bi) not in blocks:
                continue
            with k.scope():
                pt = [(k.sb("x%d" % i, [128, D], F32), 128) for i in range(8)]
                t_f1 = pt + ([stile] if (bi == 0 and "s" in blocks) else [])
                t_f2 = pt + ([stile] if (bi == 1 and "s" in blocks) else [])
                ncmax = 1024 + 64
                hT = k.sb("hT", [128, 8, 1 + ncmax], BF16)
                yaT = k.sb("yaT", [128, 4, 1024], BF16)
                for i in range(8):
                    r0 = (bi * 8 + i) * 128
                    k.dma("sp", pt[i][0][:, :], self.xp[r0:r0 + 128, :], "xl%d" % i)
                self.m_pre_done = False
                self.f2_pre_done = 0
                if "f1" in stages:
                    hoist = (2, 8) if ("m1" in stages and HOIST) else None
                    self.ffn(1, t_f1, hT, 0, 1, hoist=hoist)
                    self.m_pre_done = hoist is not None
                if "m1" in stages:
                    self.mixer1(pt, hT, yaT, "p", bi == 1, 1024, bi)
                if "m2" in stages:
                    self.mixer2(pt, hT, yaT, "p", bi == 1, 1024, bi)
                def store(i, bi=bi, pt=pt):
                    if i < 8:
                        r0 = (bi * 8 + i) * 128
                        k.dma("sp", self.yp[r0:r0 + 128, :], pt[i][0][:, :], "xs%d" % i)
                if "f2" in stages:
                    self.ffn(2, t_f2, hT, 4, 5, pre_done=self.f2_pre_done, after_tile=store)
                else:
                    for i in range(8):
                        store(i)
            if bi == 0 and "s" in blocks:
                with k.scope():
                    hT = k.sb("hTs_blk", [128, 8, 1 + 64], BF16)
                    yaT = k.sb("yaTs_blk", [128, 4, 64], BF16)
                    if "m1" in stages:
                        self.mixer1([stile], hT, yaT, "s", False, 64, 0)
                    if "m2" in stages:
                        self.mixer2([stile], hT, yaT, "s", False, 64, 0)
        if "s" in blocks:
            if "p1" not in blocks:
                with k.scope():
                    hT = k.sb("hTs_blk2", [128, 8, 1 + 64], BF16)
                    if "p0" not in blocks:
                        self.ffn(1, [stile], hT, 0, 1)
                        yaT = k.sb("yaTs_blk2", [128, 4, 64], BF16)
                        self.mixer1([stile], hT, yaT, "s", False, 64, 0)
                        self.mixer2([stile], hT, yaT, "s", False, 64, 0)
                    self.ffn(2, [stile], hT, 4, 5)
            k.dma("sp", self.ys, self.xs_t[:64, :], "xs_s")
        k.p.emit()
        return self.nc

    def rstd_from(self, nt, src, dst, n):
        k = self.k
        k.act(dst, src, AF.Ln, scale=1.0 / n, bias=EPS)
        k.act(dst, dst, AF.Exp, scale=-0.5)

    def prenorm(self, xt, nt, gidx, dst, sample_dst=False, wk=None, pool=None):
        k = self.k
        k.act(self.junk[:nt, :], xt[:nt, :], AF.Square, accum=self.ss[:nt, 0:1])
        self.rstd_from(nt, self.ss[:nt, 0:1], self.ss[:nt, 1:2], D)
        k.ts("dve", self.xn[:nt, :], xt[:nt, :], self.ss[:nt, 1:2], None, ALU.mult)
        bk = k.bank(pool)
        psb = bk[:].bitcast(BF16)
        for kk in range(8):
            k.tr(psb[:, kk * 128:kk * 128 + nt], self.xn[:nt, kk * 128:(kk + 1) * 128], self.identb[:nt, :nt])
        src = psb.rearrange("p (k t) -> p k t", k=8)[:, :, :nt]
        gs = self.gT[:, gidx * 8:(gidx + 1) * 8]
        if sample_dst:
            src = src.rearrange("p k (b t) -> p k b t", t=4)
            g = gs.unsqueeze(2).unsqueeze(3).to_broadcast([128, 8, 16, 4])
        else:
            g = gs.unsqueeze(2).to_broadcast([128, 8, nt])
        k.tt("dve", dst, src, g, ALU.mult, wk=wk)

    def postnorm_residual(self, Y, xt, nt, gpb, factor, tY):
        k = self.k
        ss = self.ss
        k.act(self.junk[:nt, 0:512], Y[0][:nt, :], AF.Square, accum=ss[:nt, 2:3])
        k.act(self.junk[:nt, 512:1024], Y[1][:nt, :], AF.Square, accum=ss[:nt, 3:4])
        k.tt("dve", ss[:nt, 4:5], ss[:nt, 2:3], ss[:nt, 3:4], ALU.add)
        self.rstd_from(nt, ss[:nt, 4:5], ss[:nt, 5:6], D)
        for j in range(2):
            t = tY[j]
            k.stt("dve", t[:nt, :], Y[j][:nt, :], ss[:nt, 5:6], gpb[:nt, j * 512:(j + 1) * 512], ALU.mult, ALU.mult)
            k.stt("dve", xt[:nt, j * 512:(j + 1) * 512], t[:nt, :], float(factor), xt[:nt, j * 512:(j + 1) * 512],
                  ALU.mult, ALU.add)

    def ffn(self, which, tiles, hT, gpre, gpost, pre_done=0, hoist=None, after_tile=None):
        k = self.k
        wg, wu, wd = self.fw[which]
        ncols = sum(t[1] for t in tiles)
        k.p.tag = "F%d" % which
        with k.scope():
            aT = k.sb("aT", [128, NCH, ncols], BF16)
            wdp = [k.sb("wdp%d" % i, [128, 2, D], BF16) for i in range(NCH // 2)]
            ring = [k.sb("wr%d" % i, [128, 2, 8, 128], BF16) for i in range(NRING)]
            gpb = k.sb("gpb", [128, D], F32)
            tE = [k.sb("tE%d" % i, [128, 512], F32) for i in range(2)]
            tY = [k.sb("tY%d" % i, [128, 512], F32) for i in range(2)]
            k.dma("sp", gpb[:], self.normg[gpost:gpost + 1, :].partition_broadcast(128), "gpb")
            wdv = wd.rearrange("(c p) d -> p c d", p=128)
            col = 1
            hkey = lambda c_: ("hTg", hT[:].name, (c_ - 1) // 512)
            for i, (xt, nt) in enumerate(tiles):
                if i >= pre_done:
                    self.prenorm(xt, nt, gpre, hT[:, :, col:col + nt], wk=[hkey(col)])
                col += nt
            groups = [(c0, min(512, ncols - c0)) for c0 in range(0, ncols, 512)]
            for c in range(NCH):
                slot = ring[c % NRING]
                k.dma("pool", slot[:].rearrange("p a k f -> p (a k f)"), wg[c], "wr%d" % (c % NRING))
                if c % 2 == 0:
                    k.dma("pool", wdp[c // 2][:], wdv[:, c:c + 2, :], "wdp%d" % (c // 2))
                for gi, (c0, n) in enumerate(groups):
                    G = k.bank()
                    U = k.bank()
                    for kk in range(8):
                        k.mm(G[:, :n], slot[:, 0, kk, :], hT[:, kk, 1 + c0:1 + c0 + n], start=kk == 0, stop=kk == 7,
                             rk=[slot[:, 0, kk, :], hkey(1 + c0)])
                    for kk in range(8):
                        k.mm(U[:, :n], slot[:, 1, kk, :], hT[:, kk, 1 + c0:1 + c0 + n], start=kk == 0, stop=kk == 7,
                             rk=[slot[:, 0, kk, :], hkey(1 + c0)])
                    e = tE[gi % 2]
                    k.act(e[:, :n], G[:, :n], AF.Silu)
                    k.tt("dve", aT[:, c, c0:c0 + n], U[:, :n], e[:, :n], ALU.mult)
            col = 0
            pend = None

            def do_hoist(p_):
                xt0, nt0, c0_ = p_
                self.prenorm(xt0, nt0, hoist[0], hT[:, :, 1 + c0_:1 + c0_ + nt0], wk=[hkey(1 + c0_), hT[:]])
            for i, (xt, nt) in enumerate(tiles):
                Y = [k.bank(), k.bank()]
                for j in range(2):
                    for c in range(NCH):
                        k.mm(Y[j][:nt, :], aT[:, c, col:col + nt], wdp[c // 2][:, c % 2, j * 512:(j + 1) * 512],
                             start=c == 0, stop=c == NCH - 1)
                if pend is not None:
                    do_hoist(pend)
                    pend = None
                self.postnorm_residual(Y, xt, nt, gpb, 0.5, tY)
                if after_tile is not None:
                    after_tile(i)
                if hoist is not None and i < hoist[1]:
                    pend = (xt, nt, col)
                col += nt
            if pend is not None:
                do_hoist(pend)

    def stop(self, n):
        if self.debug.get("m1lvl", 99) <= n:
            raise _Stop()

    def mixer1(self, *a):
        try:
            self._mixer1(*a)
        except _Stop:
            pass

    def _mixer1(self, tiles, hT, yaT, kind, last, ncols, bi):
        k = self.k
        smp = kind == "s"
        sfx = "_s" if smp else "_p"
        ns = 16 if smp else 2
        nsteps = 2 if smp else 6
        k.p.tag = "M1%s_pre" % sfx
        with k.scope():
            Wsg = []
            for g_ in range(4):
                t_ = k.sb("Ws%d" % g_, [128, 8, self.win_n[g_]], BF16)
                k.dma("pool", t_[:].rearrange("p k n -> p (k n)"), self.win[g_], "wsg%d" % g_)
                Wsg.append(t_)
            mu_b = k.sb("mu_b", [128, 1536], F32)
            k.dma("sp", mu_b[:], self.mu[0:1, 0:1536].partition_broadcast(128), "mub")
            bc = {}
            for n in ("w0", "a0", "k_k", "k_a", "r_k", "lnx_w", "lnx_b"):
                bc[n] = k.sb("bc_" + n, [128, 512], F32)
                k.dma("sp", bc[n][:], self.vec[n].partition_broadcast(128), "bc_" + n)
            w2a2 = k.sb("w2a2", [128, 512], BF16)
            k.dma("pool", w2a2[0:64, :], self.w2, "w2a2")
            k.dma("pool", w2a2[64:128, :], self.a2, "w2a2")
            g2b = k.sb("g2b", [128, 512], BF16)
            k.dma("pool", g2b[:], self.g2, "g2b")
            scr = k.sb("scr", [128, D], F32)
            hfull = scr
            f32t = lambda n, w=512: k.sb(n, [128, w], F32)
            b16t = lambda n, w=512: k.sb(n, [128, w], BF16)
            rkv = f32t("rkv", 1536)
            g2f = rkv[:, 0:D]
            k.dma("sp", g2f, self.normg[2:3, :].partition_broadcast(128), "g2f")
            dT = k.sb("dT", [128, 8, 128], BF16)
            lor = f32t("lor", 128)
            lwa = b16t("lwa", 128)
            lg = b16t("lg", 128)
            sg = f32t("sg")
            sghi = b16t("sghi")
            sglo = b16t("sglo")
            alr = f32t("alr")
            gg2 = [f32t("gg0"), f32t("gg1")]
            E1 = f32t("E1")
            dtmp = E1
            E2 = f32t("E2")
            kkt = sg
            bvec = kkt
            t1 = f32t("t1")
            kkn = f32t("kkn")
            kmod = f32t("kmod")
            st8 = k.sb("st8", [128, 64], F32)
            Rt2 = [b16t("Rt0"), b16t("Rt1")]
            At2 = [b16t("At0"), b16t("At1")]
            Bt2 = [b16t("Bt0"), b16t("Bt1")]
            Kt2 = [b16t("Kt0"), b16t("Kt1")]
            Bh2 = [b16t("Bh0"), b16t("Bh1")]
            Kh2 = [b16t("Kh0"), b16t("Kh1")]
            Vb2 = [b16t("Vb0"), b16t("Vb1")]
            ART_2 = [k.sb("ART_%d" % i, [128, 4, 2, 2, 64], BF16) for i in range(2)]
            BT = k.sb("BT", [128, 4, 128], BF16)
            KT = k.sb("KT", [128, 4, 128], BF16)
            A1_2 = [k.sb("A1_%d" % i, [128, 8, 128], BF16) for i in range(2)]
            A2_2 = [k.sb("A2_%d" % i, [128, 8, 128], BF16) for i in range(2)]
            Pt = [k.sb("Pt%d" % i, [128, 8, 64], BF16) for i in range(3)]
            PTt = [k.sb("PTt%d" % i, [128, 8, 64], BF16) for i in range(2)]
            W2s_2 = [k.sb("W2s_%d" % i, [128, 8, 64], F32) for i in range(2)]
            Xb = k.sb("Xb", [128, 8, 128], BF16)
            W1T = k.sb("W1T", [128, 4, 128], BF16)
            Ub = k.sb("Ub", [128, 8, 64], BF16)
            gC2 = [k.sb("gC%d" % i, [128, 4, 16], F32) for i in range(2)]
            stA2 = [k.sb("stA%d" % i, [128, 8], F32) for i in range(2)]
            ysb = scr[:, 0:512]
            yc = scr[:, 512:1024]
            ya = b16t("ya")
            mA = self.c("mA" + sfx)
            mN = self.c("mN" + sfx)
            triI, triX, triR = self.c("triI" + sfx), self.c("triX" + sfx), self.c("triR" + sfx)
            sel = self.c("sel" + sfx)
            if smp:
                hTs = k.sb("hTs", [128, 8, 16, 5], BF16)
                hTc = k.sb("hTc", [128, 8, 64], BF16)
                hTp = k.sb("hTp", [128, 8, 64], BF16)
                shs_t = k.sb("shs_t", [16, D], F32)
                shs_b = k.sb("shs_b", [16, D], BF16)
                Hs = [k.sb("Hs%d" % b, [128, 4, 64], F32) for b in range(16)]
                Hsb = k.sb("Hsb", [128, 16, 4, 64], BF16)
                Snat = [k.sb("Snat%d" % i, [64, 512], F32) for i in range(2)]
                W1Tm = k.sb("W1Tm", [128, 4, 16, 64], BF16)
                RTm = k.sb("RTm", [128, 4, 16, 64], BF16)
                Bhm = [Rt2[1], At2[1]]
                Khm = [Bt2[1], Kt2[1]]
                colmask = self.c("colmask").rearrange("p (b t) -> p b t", b=16)
                k.dma("sp", shs_t[:], self.sshift, "shs_t")
                k.copy("dve", shs_b[:], shs_t[:])
                bk = k.bank()
                psb = bk[:].bitcast(BF16)
                for kk in range(8):
                    k.tr(psb[:, kk * 16:(kk + 1) * 16], shs_b[:16, kk * 128:(kk + 1) * 128], self.identb[:16, :16])
                k.copy("dve", hTs[:, :, :, 0], psb[:, 0:128].rearrange("p (k b) -> p k b", k=8))

            def state_gen():
                for b in range(16):
                    k.p.tag = "M1_s_state"
                    sn = Snat[b % 2]
                    k.dma("sp", sn[:, :], self.swkv[b],
                          "snat%d" % (b % 2))
                    bk = k.bank(1)
                    for hp in range(4):
                        k.tr(bk[:, hp * 64:(hp + 1) * 64], sn[:64, hp * 128:(hp + 1) * 128], self.identf[:64, :64])
                    k.copy("act", Hs[b][:], bk[:, 0:256].rearrange("p (h v) -> p h v", h=4))
                    k.copy("dve", Hsb[:, b, :, :], Hs[b][:])
                    yield "s"

            self.stop(1)
            if not smp:
                k.copy("dve", hT[:, :, 0], self.hprev[:])
            col = 1
            for i, (xt_, nt) in enumerate(tiles):
                if smp:
                    self.prenorm(xt_, nt, 2, hTs[:, :, :, 1:5], sample_dst=True)
                    k.stt("dve", hfull[:nt, :], xt_[:nt, :], self.ss[:nt, 1:2], g2f[:nt, :], ALU.mult, ALU.mult)
                    for b in range(16):
                        k.dma("sp", self.shs[b:b + 1, :], hfull[4 * b + 3:4 * b + 4, :], "shs_o")
                    k.copy("dve", hTc[:].rearrange("p k (b t) -> p k b t", t=4), hTs[:, :, :, 1:5])
                    k.copy("dve", hTp[:].rearrange("p k (b t) -> p k b t", t=4), hTs[:, :, :, 0:4])
                    k.copy("dve", hT[:, :, 1:65], hTc[:])
                else:
                    if not (getattr(self, "m_pre_done", False) and not (last and i == len(tiles) - 1)):
                        self.prenorm(xt_, nt, 2, hT[:, :, col:col + nt])
                    if last and i == len(tiles) - 1:
                        k.stt("dve", hfull[:nt, :], xt_[:nt, :], self.ss[:nt, 1:2], g2f[:nt, :], ALU.mult, ALU.mult)
                        k.dma("sp", self.shp, hfull[127:128, :], "shp_o")
                col += nt
            if not smp:
                k.copy("dve", self.hprev[:], hT[:, :, ncols])

            self.stop(2)
            def tile_gen(ti, xt_, nt, col):
                k.p.tag = "M1%s_t%d" % (sfx, ti)
                pb_ = ti % 2
                Rt, At, Bt, Kt, Bh, Kh, Vb = Rt2[pb_], At2[pb_], Bt2[pb_], Kt2[pb_], Bh2[pb_], Kh2[pb_], Vb2[pb_]
                gg, gC, stA = gg2[pb_], gC2[pb_], stA2[pb_]
                A1, A2, ART, W2s = A1_2[pb_], A2_2[pb_], ART_2[pb_], W2s_2[pb_]
                if smp:
                    cur_ap = lambda kk: hTc[:, kk, :]
                    k.tt("dve", dT[:, :, :nt], hTp[:, :, :], hTc[:, :, :], ALU.subtract)
                else:
                    cur_ap = lambda kk, col=col: hT[:, kk, col:col + nt]
                    k.tt("dve", dT[:, :, :nt], hT[:, :, col - 1:col - 1 + nt], hT[:, :, col:col + nt], ALU.subtract)
                prv_ap = lambda kk: dT[:, kk, :nt]
                for gi in range(3):
                    cs_ = slice(gi * 512, (gi + 1) * 512)
                    cu, pv = k.bank(0), k.bank(0)
                    for kk in range(8):
                        k.mm(cu[:nt, :], cur_ap(kk), Wsg[gi][:, kk, :], start=kk == 0, stop=kk == 7)
                    for kk in range(8):
                        k.mm(pv[:nt, :], prv_ap(kk), Wsg[gi][:, kk, :], start=kk == 0, stop=kk == 7)
                    k.tt("dve", dtmp[:nt, :], pv[:nt, :], mu_b[:nt, cs_], ALU.mult)
                    k.tt("dve", rkv[:nt, cs_], cu[:nt, :], dtmp[:nt, :], ALU.add)
                    yield "a"
                    k.p.tag = "M1%s_t%d" % (sfx, ti)
                for fc in range(2):
                    cs_ = slice(1536 + fc * 128, 1536 + (fc + 1) * 128)
                    cu, pv = k.bank(0), k.bank(0)
                    for kk in range(8):
                        k.mm(cu[:, :nt], Wsg[3][:, kk, fc * 128:(fc + 1) * 128], cur_ap(kk), start=kk == 0, stop=kk == 7)
                    for kk in range(8):
                        k.mm(pv[:, :nt], Wsg[3][:, kk, fc * 128:(fc + 1) * 128], prv_ap(kk), start=kk == 0, stop=kk == 7)
                    k.copy("act", lor[:, :nt], cu[:, :nt])
                    k.stt("dve", lor[:, :nt], pv[:, :nt], self.muT[:, fc:fc + 1], lor[:, :nt], ALU.mult, ALU.add)
                    if fc == 0:
                        k.act(lwa[0:64, :nt], lor[0:64, :nt], AF.Tanh)
                        k.copy("act", lwa[64:128, :nt], lor[64:128, :nt])
                    else:
                        k.act(lg[:, :nt], lor[:, :nt], AF.Sigmoid)
                    yield "a"
                    k.p.tag = "M1%s_t%d" % (sfx, ti)
                def sigm(dst, ps, bias_t):
                    k.tt("dve", dst[:nt, :], ps[:nt, :], bias_t[:nt, :], ALU.add)
                    k.act(dst[:nt, :], dst[:nt, :], AF.Sigmoid)
                Lw = k.bank(0)
                k.mm(Lw[:nt, :], lwa[0:64, :nt], w2a2[0:64, :])
                sigm(sg, Lw, bc["w0"])
                La = k.bank(0)
                k.mm(La[:nt, :], lwa[64:128, :nt], w2a2[64:128, :], tp=(64, 0))
                sigm(alr, La, bc["a0"])
                Gp = k.bank(0)
                k.mm(Gp[:nt, :], lg[:, :nt], g2b[:, :])
                k.copy("act", gg[:nt, :], Gp[:nt, :])
                k.copy("act", sghi[:nt, :], sg[:nt, :])
                k.tt("dve", sglo[:nt, :], sg[:nt, :], sghi[:nt, :], ALU.subtract)
                r_ = rkv[:nt, 0:512]
                k_ = rkv[:nt, 512:1024]
                v_ = rkv[:nt, 1024:1536]
                h8 = lambda ap: ap.rearrange("p (h j) -> p h j", h=8)
                bc8 = lambda ap: ap.unsqueeze(2).to_broadcast([nt, 8, 64])
                yield "a"
                k.p.tag = "M1%s_t%d" % (sfx, ti)
                k.tt("dve", kkt[:nt, :], k_, bc["k_k"][:nt, :], ALU.mult)
                k.tt("dve", t1[:nt, :], kkt[:nt, :], kkt[:nt, :], ALU.mult)
                k.red(st8[:nt, 0:8], h8(t1[:nt, :]))
                k.ts("dve", st8[:nt, 0:8], st8[:nt, 0:8], 1e-24, None, ALU.max)
                k.act(st8[:nt, 8:16], st8[:nt, 0:8], AF.Ln)
                k.act(st8[:nt, 8:16], st8[:nt, 8:16], AF.Exp, scale=-0.5)
                k.tt("dve", h8(kkn[:nt, :]), h8(kkt[:nt, :]), bc8(st8[:nt, 8:16]), ALU.mult)
                k.stt("dve", t1[:nt, :], alr[:nt, :], -1.0, bc["k_a"][:nt, :], ALU.add, ALU.mult)
                k.ts("dve", t1[:nt, :], t1[:nt, :], 1.0, None, ALU.add)
                k.tt("dve", kmod[:nt, :], k_, t1[:nt, :], ALU.mult)
                k.tt("dve", bvec[:nt, :], kkn[:nt, :], alr[:nt, :], ALU.mult)
                k.tt("dve", t1[:nt, :], r_, kmod[:nt, :], ALU.mult)
                k.tt("dve", t1[:nt, :], t1[:nt, :], bc["r_k"][:nt, :], ALU.mult)
                k.red(stA[:nt, 0:8], h8(t1[:nt, :]))
                k.copy("act", Vb[:nt, :], v_)
                yield "a"
                k.p.tag = "M1%s_t%d" % (sfx, ti)

                def cums(tri):
                    pb = k.bank(0)
                    k.mm(pb[:nt, :], tri[:nt, :nt], sghi[:nt, :], start=True, stop=False)
                    k.mm(pb[:nt, :], tri[:nt, :nt], sglo[:nt, :], start=False, stop=True)
                    return pb
                csI = cums(triI)
                k.act(E1[:nt, :], csI[:nt, :], AF.Exp, scale=-C0)
                k.tt("dve", Rt[:nt, :], r_, E1[:nt, :], ALU.mult)
                k.act(E2[:nt, :], csI[:nt, :], AF.Exp, scale=C0)
                k.tt("dve", Bt[:nt, :], bvec[:nt, :], E2[:nt, :], ALU.mult)
                k.tt("dve", Kt[:nt, :], kmod[:nt, :], E2[:nt, :], ALU.mult)
                csX = cums(triX)
                k.act(E1[:nt, :], csX[:nt, :], AF.Exp, scale=-C0)
                k.stt("dve", At[:nt, :], kkn[:nt, :], -1.0, E1[:nt, :], ALU.mult, ALU.mult)
                csR = cums(triR)
                k.act(E2[:nt, :], csR[:nt, :], AF.Exp, scale=-C0)
                k.tt("dve", Bh[:nt, :], bvec[:nt, :], E2[:nt, :], ALU.mult)
                k.tt("dve", Kh[:nt, :], kmod[:nt, :], E2[:nt, :], ALU.mult)
                gcp = k.bank(0)
                for hp in range(4):
                    k.mm(gcp[:, hp * 16:hp * 16 + ns], sghi[:nt, hp * 128:(hp + 1) * 128], sel[:nt, :ns], start=True, stop=False)
                    k.mm(gcp[:, hp * 16:hp * 16 + ns], sglo[:nt, hp * 128:(hp + 1) * 128], sel[:nt, :ns], start=False, stop=True)
                k.act(gC[:, :, :ns], gcp[:, 0:64].rearrange("p (h s) -> p h s", h=4)[:, :, :ns], AF.Exp, scale=-C0)
                yield "A_done"
                self.stop(3)
                k.p.tag = "M1%s_t%dB" % (sfx, ti)
                nch = 1 if smp else 2
                for (src, which) in ((At, 0), (Rt, 1), (Bt, 2), (Kt, 3)):
                    bk = k.bank(1)
                    psb = bk[:].bitcast(BF16)
                    for hp in range(4):
                        k.tr(psb[:, hp * 128:hp * 128 + nt], src[:nt, hp * 128:(hp + 1) * 128], self.identb[:nt, :nt])
                    pv4 = psb[:, 0:512].rearrange("p (h t) -> p h t", h=4)
                    if which < 2:
                        k.copy("act" if which == 0 else "dve", ART[:, :, 0:nch, which, :],
                               pv4[:, :, :nt].rearrange("p h (c t) -> p h c t", c=nch))
                    else:
                        k.copy("act" if which == 2 else "dve", (BT if which == 2 else KT)[:, :, :nt], pv4[:, :, :nt])
                self.stop(3.2)
                mA4 = mA[:nt, :].unsqueeze(1).to_broadcast([nt, 4, 128])
                hp2 = lambda t_: t_.rearrange("p (hp par) t -> p hp par t", par=2)
                for (LT, dstA) in ((BT, A1), (KT, A2)):
                    oo = [k.bank(1), k.bank(1)]
                    for c2 in range(nch):
                        rows = c2 * 64
                        for hp in range(4):
                            for par in range(2):
                                fp = par * 64
                                rhsAR = ART[fp:fp + 64, hp, c2, :, :].rearrange("p a t -> p (a t)")
                                k.mm(oo[par][rows:rows + 64, hp * 128:(hp + 1) * 128], LT[fp:fp + 64, hp, rows:rows + 64],
                                     rhsAR, tp=(fp, rows))
                    for par in range(2):
                        k.tt("dve", hp2(dstA[:nt, :, :])[:, :, par, :], oo[par][:nt, :].rearrange("p (h t) -> p h t", h=4), mA4,
                             ALU.mult)
                oN = [k.bank(1), k.bank(1)]
                for c2 in range(nch):
                    rows = c2 * 64
                    for hp in range(4):
                        for par in range(2):
                            fp = par * 64
                            k.mm(oN[par][rows:rows + 64, hp * 64:(hp + 1) * 64], ART[fp:fp + 64, hp, c2, 0, :],
                                 BT[fp:fp + 64, hp, rows:rows + 64], tp=(fp, rows))
                for par in range(2):
                    k.tt("dve", hp2(Pt[0][:nt, :, :])[:, :, par, :], oN[par][:nt, 0:256].rearrange("p (h t) -> p h t", h=4),
                         mN[:nt, :].unsqueeze(1).to_broadcast([nt, 4, 64]), ALU.mult)
                self.stop(3.6)
                Xps = [k.bank(1), k.bank(1)]
                pnb, ptnb = k.bank(1), k.bank(1)
                seen = set()
                for h in range(8):
                    for c2 in range(nch):
                        rows = c2 * 64
                        bkx = Xps[h // 4]
                        s0 = (h % 4) * 128
                        first = (h // 4, c2) not in seen
                        seen.add((h // 4, c2))
                        k.mm(bkx[rows:rows + 64, s0:s0 + 64], self.identb[rows:rows + 64, rows:rows + 64],
                             At[rows:rows + 64, h * 64:(h + 1) * 64], start=first, stop=True, tp=(rows, rows))
                        k.mm(bkx[rows:rows + 64, s0 + 64:s0 + 128], A2[rows:rows + 64, h, 0:64],
                             Vb[rows:rows + 64, h * 64:(h + 1) * 64], start=False, stop=True, tp=(rows, rows))
                for q in range(2):
                    k.copy("act" if q == 0 else "dve", Xb[:nt, q * 4:(q + 1) * 4, :],
                           Xps[q][:nt, :].rearrange("p (h t) -> p h t", h=4))
                self.stop(4)
                k.p.tag = "M1%s_t%dD" % (sfx, ti)
                Pc = Pt[0]
                PTc = None
                for step in range(nsteps):
                    lastst = step == nsteps - 1
                    if not lastst:
                        for h in range(8):
                            for c2 in range(nch):
                                rows = c2 * 64
                                lhsPT = A1[rows:rows + 64, h, 0:64] if PTc is None else PTc[rows:rows + 64, h, :]
                                k.mm(pnb[rows:rows + 64, h * 64:(h + 1) * 64], lhsPT, Pc[rows:rows + 64, h, :], tp=(rows, rows))
                        for h in range(8):
                            for c2 in range(nch):
                                rows = c2 * 64
                                rhsPT = A1[rows:rows + 64, h, 0:64] if PTc is None else PTc[rows:rows + 64, h, :]
                                k.mm(ptnb[rows:rows + 64, h * 64:(h + 1) * 64], Pc[rows:rows + 64, h, :], rhsPT, tp=(rows, rows))
                    for h in range(8):
                        for c2 in range(nch):
                            rows = c2 * 64
                            lhs = A1[rows:rows + 64, h, 0:64] if PTc is None else PTc[rows:rows + 64, h, :]
                            k.mm(Xps[h // 4][rows:rows + 64, (h % 4) * 128:(h % 4 + 1) * 128], lhs, Xb[rows:rows + 64, h, :],
                                 start=False, stop=True, tp=(rows, rows))
                    if not lastst:
                        Pn = Pt[1 + step % 2]
                        PTn = PTt[step % 2]
                        k.copy("act", Pn[:nt, :, :], pnb[:nt, :].rearrange("p (h t) -> p h t", h=8))
                        k.copy("act", PTn[:nt, :, :], ptnb[:nt, :].rearrange("p (h t) -> p h t", h=8))
                        Pc, PTc = Pn, PTn
                    k.copy("act", Xb[:nt, 0:4, :], Xps[0][:nt, :].rearrange("p (h t) -> p h t", h=4))
                    k.copy("dve", Xb[:nt, 4:8, :], Xps[1][:nt, :].rearrange("p (h t) -> p h t", h=4))
                    yield "d"
                    k.p.tag = "M1%s_t%dD" % (sfx, ti)
                self.stop(4.9)
                for q in range(2):
                    k.copy("act", W2s[:nt, q * 4:(q + 1) * 4, :],
                           Xps[q][:nt, :].rearrange("p (h t) -> p h t", h=4)[:, :, 64:128])
                self.stop(4.95)
                yield "D_done"
                k.p.tag = "M1%s_t%dC" % (sfx, ti)
                bk = k.bank(0)
                psb = bk[:].bitcast(BF16)
                k.copy("dve", ya[:nt, :].rearrange("p (h j) -> p h j", h=8), Xb[:nt, :, 0:64])
                for hp in range(4):
                    k.tr(psb[:, hp * 128:hp * 128 + nt], ya[:nt, hp * 128:(hp + 1) * 128], self.identb[:nt, :nt])
                k.copy("act", W1T[:, :, :nt], psb[:, 0:512].rearrange("p (h t) -> p h t", h=4)[:, :, :nt])

                self.stop(5)
                yield "c"
                k.p.tag = "M1%s_t%dC" % (sfx, ti)
                hpv = lambda t_: t_.rearrange("p (hp par) v -> p hp par v", par=2)

                def evac_y(r0, r1, Yp, YA):
                    k.copy("act", ysb[r0:r1, :], Yp[r0:r1, :])
                    for par in range(2):
                        yv = hpv(ysb[r0:r1, :].rearrange("p (h v) -> p h v", h=8))[:, :, par, :]
                        k.tt("dve", yv, yv, YA[par][r0:r1, 0:256].rearrange("p (h v) -> p h v", h=4), ALU.add)
                if not smp:
                    for c2 in range(2):
                        rows = c2 * 64
                        Up = [k.bank(0), k.bank(0)]
                        YA = [k.bank(0), k.bank(0)]
                        for hp in range(4):
                            for par in range(2):
                                fp = par * 64
                                k.mm(Up[par][rows:rows + 64, hp * 64:(hp + 1) * 64], W1T[fp:fp + 64, hp, rows:rows + 64],
                                     self.Hb[fp:fp + 64, hp, :], tp=(fp, rows))
                        for hp in range(4):
                            for par in range(2):
                                fp = par * 64
                                k.mm(YA[par][rows:rows + 64, hp * 64:(hp + 1) * 64], ART[fp:fp + 64, hp, c2, 1, :],
                                     self.Hb[fp:fp + 64, hp, :], tp=(fp, rows))
                        for par in range(2):
                            k.tt("dve", hpv(Ub[rows:rows + 64, :, :])[:, :, par, :],
                                 Up[par][rows:rows + 64, 0:256].rearrange("p (h v) -> p h v", h=4),
                                 hpv(W2s[rows:rows + 64, :, :])[:, :, par, :], ALU.add)
                        Hn = k.bank(0)
                        for h in range(8):
                            hp, fp = h // 2, (h % 2) * 64
                            ho = Hn[fp:fp + 64, hp * 64:(hp + 1) * 64]
                            k.mm(ho, Bh[rows:rows + 64, h * 64:(h + 1) * 64], Ub[rows:rows + 64, h, :], start=True, stop=False, tp=(rows, fp))
                            k.mm(ho, Kh[rows:rows + 64, h * 64:(h + 1) * 64], Vb[rows:rows + 64, h * 64:(h + 1) * 64], start=False,
                                 stop=True, tp=(rows, fp))
                        k.tt("dve", self.Hst[:], self.Hst[:], gC[:, :, c2:c2 + 1].to_broadcast([128, 4, 64]), ALU.mult)
                        k.tt("dve", self.Hst[:], self.Hst[:], Hn[:, 0:256].rearrange("p (h v) -> p h v", h=4), ALU.add)
                        k.copy("act", self.Hb[:], self.Hst[:])
                        Yp = k.bank(0)
                        for h in range(8):
                            yo = Yp[rows:rows + 64, h * 64:(h + 1) * 64]
                            k.mm(yo, A1[rows:rows + 64, h, 64:128], Ub[rows:rows + 64, h, :], start=True, stop=False, tp=(rows, rows))
                            k.mm(yo, A2[rows:rows + 64, h, 64:128], Vb[rows:rows + 64, h * 64:(h + 1) * 64], start=False, stop=True,
                                 tp=(rows, rows))
                        evac_y(rows, rows + 64, Yp, YA)
                        if c2 == 0:
                            yield "c"
                            k.p.tag = "M1%s_t%dC" % (sfx, ti)
                else:
                    k.tt("dve", W1Tm[:], W1T[:, :, 0:64].unsqueeze(2).to_broadcast([128, 4, 16, 64]),
                         colmask.unsqueeze(1).to_broadcast([128, 4, 16, 64]), ALU.mult)
                    k.tt("dve", RTm[:], ART[:, :, 0, 1, :].unsqueeze(2).to_broadcast([128, 4, 16, 64]),
                         colmask.unsqueeze(1).to_broadcast([128, 4, 16, 64]), ALU.mult)
                    Up = [k.bank(0), k.bank(0)]
                    YA = [k.bank(0), k.bank(0)]
                    for par in range(2):
                        fp = par * 64
                        for hp in range(4):
                            for b in range(16):
                                k.mm(Up[par][0:64, hp * 64:(hp + 1) * 64], W1Tm[fp:fp + 64, hp, b, :], Hsb[fp:fp + 64, b, hp, :],
                                     start=b == 0, stop=b == 15, tp=(fp, 0))
                        for hp in range(4):
                            for b in range(16):
                                k.mm(YA[par][0:64, hp * 64:(hp + 1) * 64], RTm[fp:fp + 64, hp, b, :], Hsb[fp:fp + 64, b, hp, :],
                                     start=b == 0, stop=b == 15, tp=(fp, 0))
                    for par in range(2):
                        k.tt("dve", hpv(Ub[0:64, :, :])[:, :, par, :], Up[par][0:64, 0:256].rearrange("p (h v) -> p h v", h=4),
                             hpv(W2s[0:64, :, :])[:, :, par, :], ALU.add)
                    Yp = k.bank(0)
                    for h in range(8):
                        yo = Yp[0:64, h * 64:(h + 1) * 64]
                        k.mm(yo, A1[0:64, h, 64:128], Ub[0:64, h, :], start=True, stop=False)
                        k.mm(yo, A2[0:64, h, 64:128], Vb[0:64, h * 64:(h + 1) * 64], start=False, stop=True)
                if smp:
                    evac_y(0, 64, Yp, YA)
                    for b in range(16):
                        bm, km = Bhm[b % 2], Khm[b % 2]
                        k.ts("dve", bm[0:64, :], Bh[0:64, :], sel[0:64, b:b + 1], None, ALU.mult)
                        k.ts("dve", km[0:64, :], Kh[0:64, :], sel[0:64, b:b + 1], None, ALU.mult)
                        Hn = k.bank(0)
                        for h in range(8):
                            hp, fp = h // 2, (h % 2) * 64
                            ho = Hn[fp:fp + 64, hp * 64:(hp + 1) * 64]
                            k.mm(ho, bm[0:64, h * 64:(h + 1) * 64], Ub[0:64, h, :], start=True, stop=False, tp=(0, fp))
                            k.mm(ho, km[0:64, h * 64:(h + 1) * 64], Vb[0:64, h * 64:(h + 1) * 64], start=False, stop=True, tp=(0, fp))
                        k.tt("dve", Hs[b][:], Hs[b][:], gC[:, :, b:b + 1].to_broadcast([128, 4, 64]), ALU.mult)
                        k.tt("dve", Hs[b][:], Hs[b][:], Hn[:, 0:256].rearrange("p (h v) -> p h v", h=4), ALU.add)
                        bk = k.bank(0)
                        for hp in range(4):
                            k.tr(bk[0:64, hp * 128:(hp + 1) * 128], Hs[b][:, hp, :], self.identf[:, :])
                        sn = Snat[b % 2]
                        k.copy("act", sn[:, :], bk[0:64, :])
                        k.dma("sp", self.wks[b], sn[:, :], "snat%d" % (b % 2))

                self.stop(6)
                yield "C_done"
                k.p.tag = "M1%s_t%dO" % (sfx, ti)
                k.red(st8[:nt, 24:32], h8(ysb[:nt, :]))
                k.ts("dve", st8[:nt, 24:32], st8[:nt, 24:32], -1.0 / 64, None, ALU.mult)
                k.tt("dve", h8(yc[:nt, :]), h8(ysb[:nt, :]), bc8(st8[:nt, 24:32]), ALU.add)
                k.tt("dve", ysb[:nt, :], yc[:nt, :], yc[:nt, :], ALU.mult)
                k.red(st8[:nt, 32:40], h8(ysb[:nt, :]))
                k.act(st8[:nt, 40:48], st8[:nt, 32:40], AF.Ln, scale=1.0 / 64, bias=64e-5)
                k.act(st8[:nt, 40:48], st8[:nt, 40:48], AF.Exp, scale=-0.5)
                yield "o"
                k.p.tag = "M1%s_t%dO" % (sfx, ti)
                k.tt("dve", h8(yc[:nt, :]), h8(yc[:nt, :]), bc8(st8[:nt, 40:48]), ALU.mult)
                k.tt("dve", yc[:nt, :], yc[:nt, :], bc["lnx_w"][:nt, :], ALU.mult)
                k.tt("dve", yc[:nt, :], yc[:nt, :], bc["lnx_b"][:nt, :], ALU.add)
                k.tt("dve", h8(ysb[:nt, :]), h8(Vb[:nt, :]), bc8(stA[:nt, 0:8]), ALU.mult)
                k.tt("dve", yc[:nt, :], yc[:nt, :], ysb[:nt, :], ALU.add)
                k.tt("dve", ya[:nt, :], yc[:nt, :], gg[:nt, :], ALU.mult)
                yield "o"
                k.p.tag = "M1%s_t%dO" % (sfx, ti)
                bk = k.bank(0)
                psb = bk[:].bitcast(BF16)
                for m in range(4):
                    k.tr(psb[:, m * 128:m * 128 + nt], ya[:nt, m * 128:(m + 1) * 128], self.identb[:nt, :nt])
                k.copy("act", yaT[:, :, col - 1:col - 1 + nt], psb[:, 0:512].rearrange("p (h t) -> p h t", h=4)[:, :, :nt])

            def adv(g, until):
                for tok in g:
                    if tok in until:
                        return tok
                return None
            gens = []
            col = 1
            for ti, (xt_, nt) in enumerate(tiles):
                gens.append(tile_gen(ti, xt_, nt, col))
                col += nt
            if smp:
                sg_ = state_gen()
                a_done = False
                s_done = False
                while not (a_done and s_done):
                    if not s_done:
                        for _ in range(2):
                            if adv(sg_, ("s",)) is None:
                                s_done = True
                                break
                    if not a_done:
                        if adv(gens[0], ("a", "A_done")) == "A_done":
                            a_done = True
            else:
                adv(gens[0], ("A_done",))
            pending_out = None
            for ti in range(len(gens)):
                g = gens[ti]
                nx = gens[ti + 1] if ti + 1 < len(gens) else None
                nx_done = nx is None
                nd = 0
                while True:
                    tok = adv(g, ("d", "D_done", "C_done", "c"))
                    if tok is None:
                        break
                    if tok == "c":
                        continue
                    if tok == "d":
                        nd += 1
                        if pending_out is not None and nd >= 2:
                            if adv(pending_out, ("o",)) is None:
                                pending_out = None
                        for _rep in range(1):
                            if not nx_done:
                                if adv(nx, ("a", "A_done")) == "A_done":
                                    nx_done = True
                    if tok == "D_done":
                        if pending_out is not None:
                            adv(pending_out, ())
                            pending_out = None
                    if tok == "C_done":
                        pending_out = g
                        break
                if not nx_done:
                    adv(nx, ("A_done",))
            if pending_out is not None:
                adv(pending_out, ())
            if (not smp) and last:
                bk = k.bank(1)
                for hp in range(4):
                    k.tr(bk[0:64, hp * 128:(hp + 1) * 128], self.Hst[:, hp, :], self.identf[:, :])
                k.copy("act", scr[0:64, 0:512], bk[0:64, :])
                k.dma("sp", self.wkp, scr[0:64, 0:512], "wkp_o")

    def mixer2(self, tiles, hT, yaT, kind, last, ncols, bi):
        k = self.k
        smp = kind == "s"
        sfx = "_s" if smp else "_p"
        k.p.tag = "M2%s_pre" % sfx
        hoist_f2 = (not smp) and HOIST_F2 and ("f2" in self.debug.get("stages", ("f1", "m1", "m2", "f2")))
        if hoist_f2:
            self.f2_pre_done = len(tiles)
        with k.scope():
            Wrg = []
            for g_ in range(4):
                t_ = k.sb("Wr%d" % g_, [128, 8, 512], BF16)
                k.dma("pool", t_[:].rearrange("p k n -> p (k n)"), self.win[4 + g_], "wrg%d" % g_)
                Wrg.append(t_)
            Wo = k.sb("Wo", [128, 8, D], BF16)
            k.dma("pool", Wo[:], self.w_out.rearrange("(k p) n -> p k n", p=128), "wo")
            gpb = k.sb("gpb2", [128, D], F32)
            k.dma("sp", gpb[:], self.normg[3:4, :].partition_broadcast(128), "gpb2")
            gnw = k.sb("gnw", [128, 512], F32)
            k.dma("sp", gnw[:], self.vec["gn_w"].partition_broadcast(128), "gnw")
            nbuf = 1 if smp else 2
            cst = [k.sb("cst%d" % i, [128, 128], F32) for i in range(2)]
            B2 = []
            for pb in range(nbuf):
                d = {}
                d["qkvg"] = k.sb("qkvg%d" % pb, [128, 2048], F32)
                d["ta"] = k.sb("ta%d" % pb, [128, 256], F32)
                d["tb"] = k.sb("tb%d" % pb, [128, 256], F32)
                for n in ("qrot", "krot", "kd", "vb", "yr"):
                    d[n] = k.sb("%s%d" % (n, pb), [128, 512], BF16)
                for n in ("inm", "qT", "kT", "qdT", "yrT"):
                    d[n] = k.sb("%s%d" % (n, pb), [128, 4, 128], BF16)
                for n in ("ysb", "yc", "eg"):
                    d[n] = k.sb("%s2_%d" % (n, pb), [128, 512], F32)
                d["st4"] = k.sb("st4_%d" % pb, [128, 32], F32)
                d["tY"] = [k.sb("tY2%d_%d" % (i, pb), [128, 512], F32) for i in range(2)]
                B2.append(d)
            dm = self.c("dm" + sfx)
            qdec = self.c("qdec" + sfx)
            kdec = self.kcd[:, 4:8] if smp else self.kcd[:, 0:4]
            cdec = self.kcd[:, 12:16] if smp else self.kcd[:, 8:12]
            sel = self.c("sel_s")
            if smp:
                Ss = [k.sb("Ss%d" % b, [128, 4, 128], F32) for b in range(16)]
                Ssb = k.sb("Ssb", [128, 16, 4, 128], BF16)
                qdTm = k.sb("qdTm", [128, 4, 16, 64], BF16)
                kdm = [k.sb("kdm%d" % i, [64, 512], BF16) for i in range(2)]
                colmask = self.c("colmask").rearrange("p (b t) -> p b t", b=16)
                for b in range(16):
                    k.dma("sp", Ss[b][:].rearrange("p h e -> p (h e)"), self.sret[b], "ssld%d" % b)
                    k.copy("act" if b % 2 == 0 else "dve", Ssb[:, b, :, :], Ss[b][:])
            def tile_gen(ti, xt_, nt, col):
                k.p.tag = "M2%s_t%d" % (sfx, ti)
                pl = ti % 2
                d = B2[ti % nbuf]
                qkvg, ta, tb, qrot, krot, kd, vb, yr = (d[n] for n in ("qkvg", "ta", "tb", "qrot", "krot", "kd", "vb", "yr"))
                inm, qT, kT, qdT, yrT = (d[n] for n in ("inm", "qT", "kT", "qdT", "yrT"))
                ysb, yc, eg, st4, tY = d["ysb"], d["yc"], d["eg"], d["st4"], d["tY"]
                ct = cst[ti % 2]
                if smp:
                    k.dma("sp", ct[:nt, :], self.css, "cst%d" % (ti % 2))
                else:
                    r0 = (bi * 8 + ti) * 128
                    k.dma("sp", ct[:nt, :], self.csp[r0:r0 + 128, :], "cst%d" % (ti % 2))
                cosb = ct[:nt, 0:64].unsqueeze(1).to_broadcast([nt, 4, 64])
                sinb = ct[:nt, 64:128].unsqueeze(1).to_broadcast([nt, 4, 64])
                tav = ta[:nt, :].rearrange("p (h d) -> p h d", h=4)
                tbv = tb[:nt, :].rearrange("p (h d) -> p h d", h=4)
                h4 = lambda ap: ap.rearrange("p (h e) -> p h e", h=4)
                bc4 = lambda ap: ap.unsqueeze(2).to_broadcast([nt, 4, 128])

                def rope(off, dst):
                    xv = qkvg[:nt, off:off + 512].rearrange("p (h a d) -> p h a d", h=4, a=2)
                    dv = dst[:nt, :].rearrange("p (h a d) -> p h a d", h=4, a=2)
                    k.tt("dve", tav, xv[:, :, 0, :], cosb, ALU.mult)
                    k.tt("dve", tbv, xv[:, :, 1, :], sinb, ALU.mult)
                    k.tt("dve", dv[:, :, 0, :], tav, tbv, ALU.subtract)
                    k.tt("dve", tav, xv[:, :, 0, :], sinb, ALU.mult)
                    k.tt("dve", tbv, xv[:, :, 1, :], cosb, ALU.mult)
                    k.tt("dve", dv[:, :, 1, :], tav, tbv, ALU.add)

                def transp(src, dstT):
                    bk = k.bank(pl)
                    psb = bk[:].bitcast(BF16)
                    for h in range(4):
                        k.tr(psb[:, h * 128:h * 128 + nt], src[:nt, h * 128:(h + 1) * 128], self.identb[:nt, :nt])
                    k.copy("act", dstT[:, :, :nt], psb[:, 0:512].rearrange("p (h t) -> p h t", h=4)[:, :, :nt])
                for gi in range(4):
                    bk = k.bank(pl)
                    for kk in range(8):
                        k.mm(bk[:nt, :], hT[:, kk, col:col + nt], Wrg[gi][:, kk, :], start=kk == 0, stop=kk == 7,
                             rk=[("hTt", ti), Wrg[gi][:]])
                    k.copy("act", qkvg[:nt, gi * 512:(gi + 1) * 512], bk[:nt, :])
                    if gi == 1:
                        rope(0, qrot)
                    if gi == 2:
                        rope(512, krot)
                        k.tt("dve", h4(kd[:nt, :]), h4(krot[:nt, :]), bc4(kdec[:nt, :]), ALU.mult)
                        transp(qrot, qT)
                    if gi == 3:
                        k.copy("act", vb[:nt, :], qkvg[:nt, 1024:1536])
                        transp(krot, kT)
                    yield "y"
                    k.p.tag = "M2%s_t%d" % (sfx, ti)

                yield "y"
                k.p.tag = "M2%s_t%d" % (sfx, ti)
                qdv = qdec.rearrange("p (h t) -> p h t", h=4)
                k.tt("dve", qdT[:, :, :nt], qT[:, :, :nt], qdv, ALU.mult)
                bk = k.bank(pl)
                for h in range(4):
                    k.mm(bk[:nt, h * 128:h * 128 + nt], kT[:, h, :nt], qT[:, h, :nt])
                k.tt("dve", inm[:nt, :, :nt], bk[:nt, :].rearrange("p (h t) -> p h t", h=4)[:, :, :nt],
                     dm[:nt, :].rearrange("p (h t) -> p h t", h=4), ALU.mult)
                yield "S0"
                k.p.tag = "M2%s_t%d" % (sfx, ti)
                Yb = k.bank(pl)
                if smp:
                    k.tt("dve", qdTm[:], qdT[:, :, 0:64].unsqueeze(2).to_broadcast([128, 4, 16, 64]),
                         colmask.unsqueeze(1).to_broadcast([128, 4, 16, 64]), ALU.mult)
                for h in range(4):
                    yo = Yb[:nt, h * 128:(h + 1) * 128]
                    k.mm(yo, inm[:nt, h, :nt], vb[:nt, h * 128:(h + 1) * 128], start=True, stop=False)
                    if smp:
                        for b in range(16):
                            k.mm(yo, qdTm[:, h, b, :], Ssb[:, b, h, :], start=False, stop=b == 15)
                    else:
                        k.mm(yo, qdT[:, h, :nt], self.Sb[:, h, :], start=False, stop=True)
                k.copy("act", ysb[:nt, :], Yb[:nt, :])
                cdb = cdec.unsqueeze(2).to_broadcast([128, 4, 128])
                if smp:
                    for b in range(16):
                        km = kdm[b % 2]
                        k.ts("dve", km[:, :], kd[0:64, :], sel[0:64, b:b + 1], None, ALU.mult)
                        Sn = k.bank(pl)
                        for h in range(4):
                            k.mm(Sn[:, h * 128:(h + 1) * 128], km[0:64, h * 128:(h + 1) * 128], vb[0:64, h * 128:(h + 1) * 128])
                        k.tt("dve", Ss[b][:], Ss[b][:], cdb, ALU.mult)
                        k.tt("dve", Ss[b][:], Ss[b][:], Sn[:, :].rearrange("p (h e) -> p h e", h=4), ALU.add)
                        k.dma("sp", self.rts[b], Ss[b][:].rearrange("p h e -> p (h e)"), "rts_o%d" % (b % 4))
                else:
                    Sn = k.bank(pl)
                    for h in range(4):
                        k.mm(Sn[:, h * 128:(h + 1) * 128], kd[:nt, h * 128:(h + 1) * 128], vb[:nt, h * 128:(h + 1) * 128])
                    k.tt("dve", self.Sst[:], self.Sst[:], cdb, ALU.mult)
                    k.tt("dve", self.Sst[:], self.Sst[:], Sn[:, :].rearrange("p (h e) -> p h e", h=4), ALU.add)
                    k.copy("act", self.Sb[:], self.Sst[:])
                yield "S1"
                k.p.tag = "M2%s_t%d" % (sfx, ti)
                k.red(st4[:nt, 0:4], h4(ysb[:nt, :]))
                k.ts("dve", st4[:nt, 0:4], st4[:nt, 0:4], -1.0 / 128, None, ALU.mult)
                k.tt("dve", h4(yc[:nt, :]), h4(ysb[:nt, :]), bc4(st4[:nt, 0:4]), ALU.add)
                k.tt("dve", ysb[:nt, :], yc[:nt, :], yc[:nt, :], ALU.mult)
                k.red(st4[:nt, 4:8], h4(ysb[:nt, :]))
                k.act(st4[:nt, 8:12], st4[:nt, 4:8], AF.Ln, scale=1.0 / 128, bias=1e-5)
                k.act(st4[:nt, 8:12], st4[:nt, 8:12], AF.Exp, scale=-0.5)
                k.tt("dve", h4(yc[:nt, :]), h4(yc[:nt, :]), bc4(st4[:nt, 8:12]), ALU.mult)
                k.tt("dve", yc[:nt, :], yc[:nt, :], gnw[:nt, :], ALU.mult)

                yield "y"
                k.p.tag = "M2%s_t%d" % (sfx, ti)
                g_ = qkvg[:nt, 1536:2048]
                k.act(eg[:nt, :], g_, AF.Silu)
                k.tt("dve", yr[:nt, :], eg[:nt, :], yc[:nt, :], ALU.mult)
                bk = k.bank(pl)
                psb = bk[:].bitcast(BF16)
                for m in range(4):
                    k.tr(psb[:, m * 128:m * 128 + nt], yr[:nt, m * 128:(m + 1) * 128], self.identb[:nt, :nt])
                k.copy("act", yrT[:, :, :nt], psb[:, 0:512].rearrange("p (h t) -> p h t", h=4)[:, :, :nt])

                yield "y"
                k.p.tag = "M2%s_t%d" % (sfx, ti)
                Y = [k.bank(pl), k.bank(pl)]
                for j in range(2):
                    for m in range(8):
                        lhs = yaT[:, m, col - 1:col - 1 + nt] if m < 4 else yrT[:, m - 4, :nt]
                        k.mm(Y[j][:nt, :], lhs, Wo[:, m, j * 512:(j + 1) * 512], start=m == 0, stop=m == 7)
                self.postnorm_residual(Y, xt_, nt, gpb, 1.0, tY)
                if hoist_f2:
                    self.prenorm(xt_, nt, 4, hT[:, :, col:col + nt], wk=[("hTt", ti)], pool=pl)

            def adv(g, until):
                for tok in g:
                    if tok in until:
                        return tok
                return None
            gens = []
            col = 1
            for ti, (xt_, nt) in enumerate(tiles):
                gens.append(tile_gen(ti, xt_, nt, col))
                col += nt
            adv(gens[0], ("S0",))
            for ti in range(len(gens)):
                cur = gens[ti]
                nxt = gens[ti + 1] if ti + 1 < len(gens) else None
                adv(cur, ("S1",))
                cur_done = False
                nxt_ready = nxt is None
                while not (cur_done and nxt_ready):
                    if not nxt_ready:
                        if adv(nxt, ("y", "S0")) == "S0":
                            nxt_ready = True
                    if not cur_done:
                        if adv(cur, ("y",)) is None:
                            cur_done = True
            if (not smp) and last:
                k.dma("sp", self.rtp, self.Sst[:].rearrange("p h e -> p (h e)"), "rtp_o")


_CACHE = {}


def _get_nc(debug=None):
    key = repr(sorted((debug or {}).items()))
    if key not in _CACHE:
        b = Builder(debug)
        b.build()
        _CACHE[key] = b
    return _CACHE[key]


def _in_maps(inp):
    f = lambda a: np.ascontiguousarray(np.asarray(a, dtype=np.float32))
    ct = lambda a: f(f(a)[0].reshape(8, 128, NCH, 128).transpose(2, 1, 0, 3).reshape(NCH, 128, 1024))
    gu = lambda a, b: f(np.stack([ct(a), ct(b)], axis=2).reshape(NCH, 128, 2048))
    cs_p, cs_s = _rope_tables()
    ng = f(inp["norm_g"])[0]
    shared = {
        "gT": f(ng.reshape(6, 8, 128).transpose(2, 0, 1).reshape(128, 48)),
        "normg": ng,
        "f1gu": gu(inp["ffn1_wg"], inp["ffn1_wu"]), "f1d": f(inp["ffn1_wd"])[0],
        "f2gu": gu(inp["ffn2_wg"], inp["ffn2_wu"]), "f2d": f(inp["ffn2_wd"])[0],
        "w_out": f(inp["w_out"])[0],
        "mu": f(inp["mu_shift"]),
        "muT": f(f(inp["mu_shift"])[0, 1536:1792].reshape(2, 128).T),
        "w0": f(inp["w0"]), "a0": f(inp["a0"]), "k_k": f(inp["k_k"]), "k_a": f(inp["k_a"]),
        "r_k": f(inp["r_k"]).reshape(1, 512), "lnx_w": f(inp["lnx_w"]), "lnx_b": f(inp["lnx_b"]),
        "gn_w": f(inp["ret_gn_w"]),
        "w2": f(inp["w2"])[0], "a2": f(inp["a2"])[0], "g2": f(inp["g2"])[0],
        "cpack": CPACK, "cs_p": cs_p, "cs_s": cs_s,
    }
    wi = f(inp["w_in"])[0]
    a_ = 0
    for g_, n_ in enumerate((512, 512, 512, 256, 512, 512, 512, 512)):
        shared["win%d" % g_] = f(wi[:, a_:a_ + n_].reshape(8, 128, n_).transpose(1, 0, 2).reshape(128, 8 * n_))
        a_ += n_
    xp = f(inp["x_prompt"])
    xs = f(inp["x_sample"])
    ssh = f(inp["state_shift"])[0]
    swk = f(inp["state_wkv"])[0]
    srt = f(inp["state_ret"])[0]
    maps = []
    for c in range(8):
        m = dict(shared)
        m["xp"] = xp[c]
        m["xs"] = f(xs[16 * c:16 * (c + 1)].reshape(64, D))
        m["sshift"] = f(ssh[16 * c:16 * (c + 1)])
        m["swkv"] = f(swk[16 * c:16 * (c + 1)].transpose(0, 2, 1, 3).reshape(16, 64, 512))
        m["sret"] = f(srt[16 * c:16 * (c + 1)].transpose(0, 2, 1, 3).reshape(16, 128, 512))
        maps.append(m)
    return maps


def kernel(**inputs):
    b = _get_nc()
    maps = _in_maps(inputs)
    res = run_bass_kernel_spmd(b.nc, maps, core_ids=list(range(8)))
    R = res.results
    yp = np.stack([R[c]["yp"] for c in range(8)]).astype(np.float32)
    ys = np.concatenate([R[c]["ys"].reshape(16, 4, D) for c in range(8)]).astype(np.float32)
    shp = np.stack([R[c]["shp"].reshape(D) for c in range(8)])[None].astype(np.float32)
    wkp = np.stack([R[c]["wkp"].reshape(64, 8, 64).transpose(1, 0, 2) for c in range(8)])[None].astype(np.float32)
    rtp = np.stack([R[c]["rtp"].reshape(128, 4, 128).transpose(1, 0, 2) for c in range(8)])[None].astype(np.float32)
    shs = np.concatenate([R[c]["shs"] for c in range(8)])[None].astype(np.float32)
    wks = np.concatenate([R[c]["wks"].reshape(16, 64, 8, 64).transpose(0, 2, 1, 3) for c in range(8)])[None].astype(np.float32)
    rts = np.concatenate([R[c]["rts"].reshape(16, 128, 4, 128).transpose(0, 2, 1, 3) for c in range(8)])[None].astype(np.float32)
    return (yp, ys, shp, wkp, rtp, shs, wks, rts)
```

```python
import contextlib
import numpy as np
import concourse.bass as bass
import concourse.mybir as mybir
from concourse.bass_utils import run_bass_kernel_spmd

F32 = mybir.dt.float32
BF16 = mybir.dt.bfloat16
ALU = mybir.AluOpType
AF = mybir.ActivationFunctionType
AX = mybir.AxisListType

SAME_ENG_SYNC = True
SCHED = True
HOIST = False
HOIST_F2 = False
SCHED_XL = 0.85
MM_OVH = 0.08
SCHED_TAGS = ("M1", "M2", "F")
SAME_ENG_MIN_DIST = 0
ANNOTATE = False
D = 1024
DFF = 2816
NCH = 22
NRING = 3
EPS = 1e-6
C0 = float(np.exp(-0.5))


class Prog:
    ENGS = ("pe", "act", "dve", "pool", "sp")

    def __init__(self, nc):
        self.nc = nc
        self.ops = []
        self.writers = {}
        self.readers = {}
        self.last = {}
        self.dma_pending = []
        self.bar_nop = {}
        self.last_q = {}
        self.seg_start = 0

    @staticmethod
    def _key(a):
        if isinstance(a, (str, tuple)):
            return a
        if "DRam" in type(a.tensor).__name__:
            return None
        return a.name

    def op(self, eng, fn, r=(), w=(), dma=None, extra=(), cost=0.3, lat=0.0, tbl=None):
        idx = len(self.ops)
        rk = [k for k in (self._key(a) for a in r) if k is not None]
        wk = [k for k in (self._key(a) for a in w) if k is not None]
        deps = set(extra)
        for k in rk:
            deps.update(self.writers.get(k, ()))
            if isinstance(k, str) and k.startswith("ps"):
                deps.update(j for j in self.readers.get(k, ()) if self.ops[j]["eng"] != eng)
        for k in wk:
            deps.update(self.writers.get(k, ()))
            deps.update(self.readers.get(k, ()))
        if SCHED:
            b = self.bar_nop.get(eng)
            if b is not None:
                deps.add(b)
        self.ops.append(dict(eng=eng, fn=fn, deps=deps, dma=dma, tag=getattr(self, "tag", ""), cost=cost, lat=lat, tbl=tbl))
        if eng in ("sp", "pool"):
            self.last_q[eng] = idx
        for k in rk:
            if k in wk:
                continue
            lst = self.readers.setdefault(k, [])
            if dma is None and not SCHED:
                lst[:] = [j for j in lst if not (self.ops[j]["eng"] == eng and self.ops[j]["dma"] is None)]
            lst.append(idx)
        for k in wk:
            self.writers[k] = [idx]
            self.readers[k] = []
        if dma is None:
            self.last[eng] = idx
        else:
            self.dma_pending.append(idx)
        return idx

    def barrier(self):
        if getattr(self, "_bar_at", -1) == len(self.ops):
            return
        lasts = dict(self.last)
        pend = list(self.dma_pending)
        self.dma_pending = []
        if SCHED:
            allprev = list(range(self.seg_start, len(self.ops)))
        for e in self.ENGS:
            ex = [v for (q, v) in lasts.items()] + pend
            if SCHED:
                ex = allprev
            i_ = self.op(e, lambda eng: eng.nop(), extra=ex, cost=0.05)
            self.ops[i_]["bar"] = True
            self.bar_nop[e] = i_
        self._bar_at = len(self.ops)
        self.seg_start = len(self.ops)

    def schedule(self):
        ops = self.ops
        n = len(ops)
        succ = [[] for _ in range(n)]
        indeg = [0] * n
        lastE = {}
        sdeps = []
        for i, o in enumerate(ops):
            o["deps"] = set(j for j in o["deps"] if j != i)
            sd = set(o["deps"])
            if o["eng"] in ("sp", "pool") or not any(o["tag"].startswith(p_) for p_ in SCHED_TAGS):
                if o["eng"] in lastE:
                    sd.add(lastE[o["eng"]])
            lastE[o["eng"]] = i
            sdeps.append(sd)
            indeg[i] = len(sd)
            for j in sd:
                succ[j].append(i)
        XL = SCHED_XL
        bl = [0.0] * n
        for i in range(n - 1, -1, -1):
            o = ops[i]
            m_ = 0.0
            for s in succ[i]:
                x = bl[s] + (XL if ops[s]["eng"] != o["eng"] or o["dma"] is not None else 0.05)
                if x > m_:
                    m_ = x
            bl[i] = o["cost"] + o["lat"] + m_
        finish = [0.0] * n
        ready_t = [0.0] * n
        free = {e: 0.0 for e in self.ENGS}
        rdy = {e: [] for e in self.ENGS}
        for i in range(n):
            if indeg[i] == 0:
                rdy[ops[i]["eng"]].append(i)
        order = {e: [] for e in self.ENGS}
        done = 0
        cur_tbl = [None]
        while done < n:
            best = None
            for e in self.ENGS:
                lst = rdy[e]
                if not lst:
                    continue
                tmin = min(ready_t[i] for i in lst)
                t_e = max(free[e], tmin)
                pick = None
                pk = None
                for i in lst:
                    if ready_t[i] <= t_e + 1e-9:
                        tb = ops[i]["tbl"]
                        same = 1 if (e != "act" or tb is None or tb == cur_tbl[0]) else 0
                        key = (same, bl[i], -i)
                        if pick is None or key > pk:
                            pick, pk = i, key
                if best is None or (t_e, pick) < (best[0], best[1]):
                    best = (t_e, pick, e)
            st, i, e = best
            rdy[e].remove(i)
            o = ops[i]
            if e == "act" and o["tbl"] is not None and o["tbl"] != cur_tbl[0]:
                cur_tbl[0] = o["tbl"]
                st += 1.3
            free[e] = st + o["cost"]
            finish[i] = st + o["cost"] + o["lat"]
            order[e].append(i)
            done += 1
            for s in succ[i]:
                indeg[s] -= 1
                if i in ops[s]["deps"]:
                    x = finish[i] + (XL if ops[s]["eng"] != e or o["dma"] is not None else 0.05)
                else:
                    x = st + o["cost"]
                if x > ready_t[s]:
                    ready_t[s] = x
                if indeg[s] == 0:
                    rdy[ops[s]["eng"]].append(s)
        self.sim_time = max(finish) if n else 0.0
        return order

    def emit(self):
        nc = self.nc
        ops = self.ops
        sched_order = self.schedule() if SCHED else None

        pos = {}
        per = {e: [] for e in self.ENGS}
        if sched_order is not None:
            per = sched_order
        else:
            for i, o in enumerate(ops):
                per[o["eng"]].append(i)
        for e in self.ENGS:
            for p_, i in enumerate(per[e]):
                pos[i] = p_
                ops[i]["idx"] = i
        if SCHED:
            for o in ops:
                if o.get("bar"):
                    keep = {}
                    nd = set()
                    for j in o["deps"]:
                        pj = ops[j]
                        if pj["dma"] is not None:
                            nd.add(j)
                        elif pj["eng"] not in keep or pos[j] > pos[keep[pj["eng"]]]:
                            keep[pj["eng"]] = j
                    o["deps"] = nd | set(keep.values())

        def elide(pj, o):
            if not (pj["dma"] is None and o["dma"] is None and pj["eng"] == o["eng"]):
                return False
            if pj["eng"] == "pe" or not SAME_ENG_SYNC:
                return True
            return SAME_ENG_MIN_DIST > 0 and (pos[o["idx"]] - pos[pj["idx"]]) >= SAME_ENG_MIN_DIST

        needed = [False] * len(ops)
        for i, o in enumerate(ops):
            for j in o["deps"]:
                if not elide(ops[j], o):
                    needed[j] = True
        engsem, dmasem, dmacnt, final_dma = {}, {}, {}, {}
        cnt = {e: 0 for e in self.ENGS}
        val = [0] * len(ops)
        semof = [None] * len(ops)
        num_seq = [i for e in self.ENGS for i in per[e]]
        for i in num_seq:
            o = ops[i]
            if o["dma"] is not None:
                g = o["dma"]
                if g not in dmasem:
                    dmasem[g] = nc.alloc_semaphore(name="d%d" % len(dmasem))
                    dmacnt[g] = 0
                dmacnt[g] += 16
                val[i] = dmacnt[g]
                semof[i] = dmasem[g]
                final_dma[g] = dmacnt[g]
            elif needed[i]:
                e = o["eng"]
                if e not in engsem:
                    engsem[e] = nc.alloc_semaphore(name="e_" + e)
                cnt[e] += 1
                val[i] = cnt[e]
                semof[i] = engsem[e]
        self.n_sems = len(dmasem) + len(engsem)

        def run(engname, eng):
            waited = {}
            for i in per[engname]:
                o = ops[i]
                best = {}
                for j in o["deps"]:
                    if semof[j] is None or elide(ops[j], o):
                        continue
                    s = semof[j]
                    if val[j] > best.get(id(s), (0, None))[0]:
                        best[id(s)] = (val[j], s)
                for sid, (v, s) in best.items():
                    if waited.get(sid, 0) >= v:
                        continue
                    waited[sid] = v
                    eng.wait_ge(s, v)
                ins = o["fn"](eng)
                if ANNOTATE and o["tag"]:
                    ins.annotate(o["tag"])
                if o["dma"] is not None:
                    ins.then_inc(semof[i], 16)
                elif semof[i] is not None:
                    ins.then_inc(semof[i], 1)
            if engname == "sp":
                for g, v in final_dma.items():
                    eng.wait_ge(dmasem[g], v)

        with nc.Block() as block:
            @block.tensor
            def _(e):
                run("pe", e)

            @block.scalar
            def _(e):
                run("act", e)

            @block.vector
            def _(e):
                run("dve", e)

            @block.gpsimd
            def _(e):
                run("pool", e)

            @block.sync
            def _(e):
                run("sp", e)


def _fs(ap):
    n = 1
    for s in ap.shape[1:]:
        n *= s
    return n


def _ec(ap):
    return _fs(ap) / 900.0 + 0.15


def _eca(ap):
    return _fs(ap) / 1050.0 + 0.11


class K:
    def __init__(self, nc):
        self.nc = nc
        self.p = Prog(nc)
        self._stack = [[]]
        self.ps_banks = []
        self.ps_i = 0

    def sb(self, name, shape, dt):
        self._uid = getattr(self, "_uid", 0) + 1
        g = self.nc.sbuf_tensor("%s_%d" % (name, self._uid), list(shape), dt)
        t = g.__enter__()
        self._stack[-1].append(g)
        return t

    def psum(self, name, shape, dt):
        g = self.nc.psum_tensor(name, list(shape), dt)
        t = g.__enter__()
        self._stack[-1].append(g)
        return t

    @contextlib.contextmanager
    def scope(self):
        self.p.barrier()
        self._stack.append([])
        try:
            yield
        finally:
            self.p.barrier()
            for g in reversed(self._stack.pop()):
                g.__exit__(None, None, None)

    def bank(self, pool=None):
        if pool is None:
            b = self.ps_banks[self.ps_i % len(self.ps_banks)]
            self.ps_i += 1
            return b
        self._pi = getattr(self, "_pi", [0, 0])
        b = self.ps_banks[pool * 4 + self._pi[pool] % 4]
        self._pi[pool] += 1
        return b

    def mm(self, out, lhsT, rhs, start=True, stop=True, tp=None, rk=None):
        kw = {}
        if tp is not None:
            kw["tile_position"] = tp
        self.p.op("pe", lambda e: e.matmul(out, lhsT, rhs, start=start, stop=stop, **kw),
                  r=[lhsT, rhs] if rk is None else rk, w=[out], cost=max(_fs(rhs), 64) / 2300.0 + MM_OVH, lat=0.15)

    def tr(self, out, in_, ident):
        self.p.op("pe", lambda e: e.transpose(out, in_, ident), r=[in_, ident], w=[out], cost=0.1, lat=0.15)

    def act(self, out, in_, func, bias=None, scale=None, accum=None):
        kw = {}
        if bias is not None:
            kw["bias"] = bias
        if scale is not None:
            kw["scale"] = scale
        if accum is not None:
            kw["accum_out"] = accum
        rr = [in_] + [a for a in (bias, scale) if not isinstance(a, (int, float, type(None)))]
        ww = [out] + ([accum] if accum is not None else [])
        tbl = {AF.Sigmoid: "sig", AF.Tanh: "sig", AF.Exp: "exp", AF.Ln: "exp", AF.Silu: "silu"}.get(func)
        self.p.op("act", lambda e: e.activation(out, in_, func, **kw), r=rr, w=ww, cost=_ec(out), tbl=tbl)

    def tt(self, eng, out, in0, in1, op, wk=None):
        self.p.op(eng, lambda e: e.tensor_tensor(out, in0, in1, op), r=[in0, in1], w=[out] if wk is None else wk, cost=_ec(out))

    def ts(self, eng, out, in0, s1, s2, op0, op1=None):
        rr = [in0] + [a for a in (s1, s2) if not isinstance(a, (int, float, type(None)))]
        kw = {}
        if op1 is not None:
            kw["op1"] = op1
        self.p.op(eng, lambda e: e.tensor_scalar(out, in0, s1, s2, op0, **kw), r=rr, w=[out], cost=_ec(out))

    def stt(self, eng, out, in0, scalar, in1, op0, op1):
        rr = [in0, in1] + ([] if isinstance(scalar, (int, float)) else [scalar])
        self.p.op(eng, lambda e: e.scalar_tensor_tensor(out, in0, scalar, in1, op0, op1), r=rr, w=[out], cost=_ec(out))

    def copy(self, eng, out, in_):
        if eng == "act":
            self.p.op("act", lambda e: e.copy(out, in_), r=[in_], w=[out], cost=_ec(out))
        else:
            self.p.op(eng, lambda e: e.tensor_copy(out, in_), r=[in_], w=[out], cost=_ec(out))

    def recip(self, out, in_):
        self.p.op("dve", lambda e: e.reciprocal(out, in_), r=[in_], w=[out], cost=5 * _ec(out))

    def red(self, out, in_):
        self.p.op("dve", lambda e: e.tensor_reduce(out, in_, AX.X, ALU.add), r=[in_], w=[out], cost=_ec(in_))

    def memset(self, eng, ap, v):
        self.p.op(eng, lambda e: e.memset(ap, v), r=[], w=[ap], cost=_ec(ap))

    def dma(self, q, out, in_, grp):
        self.p.op(q, lambda e: e.dma_start(out=out, in_=in_), r=[in_], w=[out], dma=grp, cost=0.3,
                  lat=2.0 + _fs(out) * 128 * 4 / 150e3)


def _pack_consts():
    P = 128
    items = {}
    p = np.arange(P)
    col = np.arange(128)
    items["ident"] = np.eye(P, dtype=np.float32)
    s = (p % 64)[:, None]
    t = (col % 64)[None, :]
    items["mA_p"] = np.where(col[None, :] < 64, s < t, s <= t).astype(np.float32)
    items["mN_p"] = (np.arange(64)[None, :] < s).astype(np.float32)
    sb_, tb_ = (p // 4)[:, None], ((col % 64) // 4)[None, :]
    s4, t4 = (p % 4)[:, None], ((col % 64) % 4)[None, :]
    mA_s = np.where(col[None, :] < 64, s4 < t4, s4 <= t4) & (sb_ == tb_) & (p[:, None] < 64)
    items["mA_s"] = mA_s.astype(np.float32)
    c64 = np.arange(64)
    items["mN_s"] = (((c64 % 4)[None, :] < s4) & ((c64 // 4)[None, :] == sb_) & (p[:, None] < 64)).astype(np.float32)
    S_, T_ = p[:, None], col[None, :]
    same = (S_ // 64) == (T_ // 64)
    items["triI_p"] = (same & (S_ <= T_)).astype(np.float32)
    items["triX_p"] = (same & (S_ < T_)).astype(np.float32)
    items["triR_p"] = (same & (S_ > T_)).astype(np.float32)
    same = ((S_ // 4) == (T_ // 4)) & (S_ < 64) & (T_ < 64)
    items["triI_s"] = (same & (S_ <= T_)).astype(np.float32)
    items["triX_s"] = (same & (S_ < T_)).astype(np.float32)
    items["triR_s"] = (same & (S_ > T_)).astype(np.float32)
    items["sel_p"] = ((p // 64)[:, None] == np.arange(2)[None, :]).astype(np.float32)
    items["sel_s"] = (((p // 4)[:, None] == np.arange(16)[None, :]) & (p[:, None] < 64)).astype(np.float32)
    cm = ((c64 // 4)[None, :] == np.arange(16)[:, None]).astype(np.float32)
    items["colmask"] = np.broadcast_to(cm.reshape(1, 16 * 64), (P, 16 * 64)).copy()
    lg = np.log1p(-np.exp2(-5.0 - np.arange(4, dtype=np.float32))).astype(np.float32)
    scale = np.float32(128.0 ** -0.5)
    i_ = np.arange(128, dtype=np.float32)
    diff = i_[None, :] - i_[:, None]
    dm = np.zeros((P, 4, 128), np.float32)
    for h in range(4):
        dm[:, h, :] = np.where(diff >= 0, np.exp(lg[h] * np.maximum(diff, 0.0)), 0.0) * scale
    items["dm_p"] = dm.reshape(P, 512)
    dms = np.zeros((P, 4, 64), np.float32)
    jj, ii = np.arange(64)[:, None], np.arange(64)[None, :]
    d4 = (ii % 4 - jj % 4).astype(np.float32)
    okm = (jj // 4 == ii // 4) & (d4 >= 0)
    for h in range(4):
        dms[:64, h, :] = np.where(okm, np.exp(lg[h] * np.maximum(d4, 0.0)), 0.0) * scale
    items["dm_s"] = dms.reshape(P, 256)
    qd = np.zeros((P, 4, 128), np.float32)
    qs = np.zeros((P, 4, 64), np.float32)
    kd = np.zeros((P, 4), np.float32)
    ks = np.zeros((P, 4), np.float32)
    cdp = np.zeros((P, 4), np.float32)
    cds = np.zeros((P, 4), np.float32)
    for h in range(4):
        qd[:, h, :] = np.exp(lg[h] * (i_ + 1.0))[None, :]
        qs[:, h, :] = np.exp(lg[h] * ((np.arange(64) % 4).astype(np.float32) + 1.0))[None, :]
        kd[:, h] = np.exp(lg[h] * (127.0 - i_)) * scale
        ks[:, h] = np.exp(lg[h] * (3.0 - (p % 4).astype(np.float32))) * scale
        cdp[:, h] = np.exp(lg[h] * 128.0)
        cds[:, h] = np.exp(lg[h] * 4.0)
    items["qdec_p"] = qd.reshape(P, 512)
    items["qdec_s"] = qs.reshape(P, 256)
    items["kdec_p"] = kd
    items["kdec_s"] = ks
    items["cdec_p"] = cdp
    items["cdec_s"] = cds
    offs = {}
    o = 0
    for k_, v in items.items():
        offs[k_] = (o, v.shape[1])
        o += v.shape[1]
    pack = np.concatenate([items[k_] for k_ in items], axis=1).astype(np.float32)
    return pack, offs


def _rope_tables():
    half = 64
    inv = (np.float32(10000.0) ** (-np.arange(half, dtype=np.float32) / np.float32(half))).astype(np.float32)
    pos_p = np.arange(2048, dtype=np.float32)
    ang = (pos_p[:, None] * inv[None, :]).astype(np.float32)
    cs_p = np.concatenate([np.cos(ang), np.sin(ang)], axis=1).astype(np.float32)
    pos_s = (16384 + (np.arange(64) % 4)).astype(np.float32)
    ang = (pos_s[:, None] * inv[None, :]).astype(np.float32)
    cs_s = np.concatenate([np.cos(ang), np.sin(ang)], axis=1).astype(np.float32)
    return cs_p, cs_s


CPACK, COFF = _pack_consts()
NCP = CPACK.shape[1]


class _Stop(Exception):
    pass


class Builder:
    def __init__(self, debug=None):
        nc = bass.Bass("TRN2", target_bir_lowering=False)
        self.nc = nc
        self.k = K(nc)
        self.debug = debug or {}
        self.dbg_outs = {}

        def di(name, shape):
            return nc.dram_tensor(name, list(shape), F32, kind="ExternalInput").ap()

        def do(name, shape):
            return nc.dram_tensor(name, list(shape), F32, kind="ExternalOutput").ap()

        self.xp = di("xp", [2048, D])
        self.xs = di("xs", [64, D])
        self.sshift = di("sshift", [16, D])
        self.swkv = di("swkv", [16, 64, 512])
        self.sret = di("sret", [16, 128, 512])
        self.gTd = di("gT", [128, 48])
        self.normg = di("normg", [6, D])
        self.fw = {1: (di("f1gu", [NCH, 128, 2048]), None, di("f1d", [NCH // 2, 128, 2 * D])),
                   2: (di("f2gu", [NCH, 128, 2048]), None, di("f2d", [NCH // 2, 128, 2 * D]))}
        self.win_n = (512, 512, 512, 256, 512, 512, 512, 512)
        self.win = [di("win%d" % g, [128, 8 * n_]) for g, n_ in enumerate(self.win_n)]
        self.w_out = di("w_out", [D, D])
        self.mu = di("mu", [1, 1792])
        self.muTd = di("muT", [128, 2])
        self.vec = {n: di(n, [1, 512]) for n in ("w0", "a0", "k_k", "k_a", "r_k", "lnx_w", "lnx_b", "gn_w")}
        self.w2 = di("w2", [64, 512])
        self.a2 = di("a2", [64, 512])
        self.g2 = di("g2", [128, 512])
        self.cpack = di("cpack", [128, NCP])
        self.csp = di("cs_p", [2048, 128])
        self.css = di("cs_s", [64, 128])
        self.yp = do("yp", [2048, D])
        self.ys = do("ys", [64, D])
        self.shp = do("shp", [1, D])
        self.wkp = do("wkp", [64, 512])
        self.rtp = do("rtp", [128, 512])
        self.shs = do("shs", [16, D])
        self.wks = do("wks", [16, 64, 512])
        self.rts = do("rts", [16, 128, 512])

    def dbg(self, name, ap, shape):
        if name not in self.debug:
            return
        o = self.nc.dram_tensor("dbg_" + name, list(shape), F32, kind="ExternalOutput").ap()
        t = self.k.sb("dbgt_" + name, list(shape), F32)
        self.k.copy("dve", t[:], ap)
        self.k.dma("sp", o, t[:], "dbg_" + name)
        self.dbg_outs[name] = "dbg_" + name

    def c(self, name):
        o, n = COFF[name]
        return self.cb[:, o:o + n]

    def build(self):
        k = self.k
        for i in range(8):
            k.ps_banks.append(k.psum("ps%d" % i, [128, 512], F32))
        self.cb = k.sb("cb", [128, NCP], BF16)
        self.identf = k.sb("identf", [128, 128], F32)
        self.kcd = k.sb("kcd", [128, 16], F32)
        self.gT = k.sb("gTs", [128, 48], F32)
        self.muT = k.sb("muTs", [128, 2], F32)
        self.Hst = k.sb("Hst", [128, 4, 64], F32)
        self.Hb = k.sb("Hb", [128, 4, 64], BF16)
        self.Sst = k.sb("Sst", [128, 4, 128], F32)
        self.Sb = k.sb("Sb", [128, 4, 128], BF16)
        self.hprev = k.sb("hprev", [128, 8], BF16)
        self.xn = k.sb("xn", [128, D], BF16)
        self.junk = k.sb("junk", [128, D], BF16)
        self.ss = k.sb("ss", [128, 8], F32)
        self.identb = self.c("ident")
        with k.scope():
            st = k.sb("cstage", [128, NCP], F32)
            k.dma("sp", st[:], self.cpack, "c0")
            k.dma("sp", self.gT[:], self.gTd, "c1")
            k.dma("sp", self.muT[:], self.muTd, "c2")
            k.copy("dve", self.cb[:], st[:])
            o, n = COFF["ident"]
            k.copy("act", self.identf[:], st[:, o:o + n])
            o, _ = COFF["kdec_p"]
            k.copy("act", self.kcd[:], st[:, o:o + 16])
        k.memset("dve", self.Hst[:], 0.0)
        k.memset("dve", self.Hb[:], 0.0)
        k.memset("dve", self.Sst[:], 0.0)
        k.memset("dve", self.Sb[:], 0.0)
        k.memset("dve", self.hprev[:], 0.0)
        self.xs_t = k.sb("xs_t", [128, D], F32)
        blocks = self.debug.get("blocks", ["s", "p0", "p1"])
        stages = self.debug.get("stages", ("f1", "m1", "m2", "f2"))
        k.dma("sp", self.xs_t[:64, :], self.xs, "xs_l")
        stile = (self.xs_t, 64)
        for bi in range(2):
            if ("p%d" % bi) not in blocks:
                continue
            with k.scope():
                pt = [(k.sb("x%d" % i, [128, D], F32), 128) for i in range(8)]
                t_f1 = pt + ([stile] if (bi == 0 and "s" in blocks) else [])
                t_f2 = pt + ([stile] if (bi == 1 and "s" in blocks) else [])
                ncmax = 1024 + 64
                hT = k.sb("hT", [128, 8, 1 + ncmax], BF16)
                yaT = k.sb("yaT", [128, 4, 1024], BF16)
                for i in range(8):
                    r0 = (bi * 8 + i) * 128
                    k.dma("sp", pt[i][0][:, :], self.xp[r0:r0 + 128, :], "xl%d" % i)
                self.m_pre_done = False
                self.f2_pre_done = 0
                if "f1" in stages:
                    hoist = (2, 8) if ("m1" in stages and HOIST) else None
                    self.ffn(1, t_f1, hT, 0, 1, hoist=hoist)
                    self.m_pre_done = hoist is not None
                if "m1" in stages:
                    self.mixer1(pt, hT, yaT, "p", bi == 1, 1024, bi)
                if "m2" in stages:
                    self.mixer2(pt, hT, yaT, "p", bi == 1, 1024, bi)
                def store(i, bi=bi, pt=pt):
                    if i < 8:
                        r0 = (bi * 8 + i) * 128
                        k.dma("sp", self.yp[r0:r0 + 128, :], pt[i][0][:, :], "xs%d" % i)
                if "f2" in stages:
                    self.ffn(2, t_f2, hT, 4, 5, pre_done=self.f2_pre_done, after_tile=store)
                else:
                    for i in range(8):
                        store(i)
            if bi == 0 and "s" in blocks:
                with k.scope():
                    hT = k.sb("hTs_blk", [128, 8, 1 + 64], BF16)
                    yaT = k.sb("yaTs_blk", [128, 4, 64], BF16)
                    if "m1" in stages:
                        self.mixer1([stile], hT, yaT, "s", False, 64, 0)
                    if "m2" in stages:
                        self.mixer2([stile], hT, yaT, "s", False, 64, 0)
        if "s" in blocks:
            if "p1" not in blocks:
                with k.scope():
                    hT = k.sb("hTs_blk2", [128, 8, 1 + 64], BF16)
                    if "p0" not in blocks:
                        self.ffn(1, [stile], hT, 0, 1)
                        yaT = k.sb("yaTs_blk2", [128, 4, 64], BF16)
                        self.mixer1([stile], hT, yaT, "s", False, 64, 0)
                        self.mixer2([stile], hT, yaT, "s", False, 64, 0)
                    self.ffn(2, [stile], hT, 4, 5)
            k.dma("sp", self.ys, self.xs_t[:64, :], "xs_s")
        k.p.emit()
        return self.nc

    def rstd_from(self, nt, src, dst, n):
        k = self.k
        k.act(dst, src, AF.Ln, scale=1.0 / n, bias=EPS)
        k.act(dst, dst, AF.Exp, scale=-0.5)

    def prenorm(self, xt, nt, gidx, dst, sample_dst=False, wk=None, pool=None):
        k = self.k
        k.act(self.junk[:nt, :], xt[:nt, :], AF.Square, accum=self.ss[:nt, 0:1])
        self.rstd_from(nt, self.ss[:nt, 0:1], self.ss[:nt, 1:2], D)
        k.ts("dve", self.xn[:nt, :], xt[:nt, :], self.ss[:nt, 1:2], None, ALU.mult)
        bk = k.bank(pool)
        psb = bk[:].bitcast(BF16)
        for kk in range(8):
            k.tr(psb[:, kk * 128:kk * 128 + nt], self.xn[:nt, kk * 128:(kk + 1) * 128], self.identb[:nt, :nt])
        src = psb.rearrange("p (k t) -> p k t", k=8)[:, :, :nt]
        gs = self.gT[:, gidx * 8:(gidx + 1) * 8]
        if sample_dst:
            src = src.rearrange("p k (b t) -> p k b t", t=4)
            g = gs.unsqueeze(2).unsqueeze(3).to_broadcast([128, 8, 16, 4])
        else:
            g = gs.unsqueeze(2).to_broadcast([128, 8, nt])
        k.tt("dve", dst, src, g, ALU.mult, wk=wk)

    def postnorm_residual(self, Y, xt, nt, gpb, factor, tY):
        k = self.k
        ss = self.ss
        k.act(self.junk[:nt, 0:512], Y[0][:nt, :], AF.Square, accum=ss[:nt, 2:3])
        k.act(self.junk[:nt, 512:1024], Y[1][:nt, :], AF.Square, accum=ss[:nt, 3:4])
        k.tt("dve", ss[:nt, 4:5], ss[:nt, 2:3], ss[:nt, 3:4], ALU.add)
        self.rstd_from(nt, ss[:nt, 4:5], ss[:nt, 5:6], D)
        for j in range(2):
            t = tY[j]
            k.stt("dve", t[:nt, :], Y[j][:nt, :], ss[:nt, 5:6], gpb[:nt, j * 512:(j + 1) * 512], ALU.mult, ALU.mult)
            k.stt("dve", xt[:nt, j * 512:(j + 1) * 512], t[:nt, :], float(factor), xt[:nt, j * 512:(j + 1) * 512],
                  ALU.mult, ALU.add)

    def ffn(self, which, tiles, hT, gpre, gpost, pre_done=0, hoist=None, after_tile=None):
        k = self.k
        wg, wu, wd = self.fw[which]
        ncols = sum(t[1] for t in tiles)
        k.p.tag = "F%d" % which
        with k.scope():
            aT = k.sb("aT", [128, NCH, ncols], BF16)
            wdp = [k.sb("wdp%d" % i, [128, 2, D], BF16) for i in range(NCH // 2)]
            ring = [k.sb("wr%d" % i, [128, 2, 8, 128], BF16) for i in range(NRING)]
            gpb = k.sb("gpb", [128, D], F32)
            tE = [k.sb("tE%d" % i, [128, 512], F32) for i in range(2)]
            tY = [k.sb("tY%d" % i, [128, 512], F32) for i in range(2)]
            k.dma("sp", gpb[:], self.normg[gpost:gpost + 1, :].partition_broadcast(128), "gpb")
            col = 1
            hkey = lambda c_: ("hTg", hT[:].name, (c_ - 1) // 512)
            for i, (xt, nt) in enumerate(tiles):
                if i >= pre_done:
                    self.prenorm(xt, nt, gpre, hT[:, :, col:col + nt], wk=[hkey(col)])
                col += nt
            groups = [(c0, min(512, ncols - c0)) for c0 in range(0, ncols, 512)]
            for c in range(NCH):
                slot = ring[c % NRING]
                k.dma("pool", slot[:].rearrange("p a k f -> p (a k f)"), wg[c], "wr%d" % (c % NRING))
                if c % 2 == 0:
                    k.dma("pool", wdp[c // 2][:].rearrange("p a d -> p (a d)"), wd[c // 2], "wdp%d" % (c // 2))
                for gi, (c0, n) in enumerate(groups):
                    G = k.bank()
                    U = k.bank()
                    for kk in range(8):
                        k.mm(G[:, :n], slot[:, 0, kk, :], hT[:, kk, 1 + c0:1 + c0 + n], start=kk == 0, stop=kk == 7,
                             rk=[slot[:, 0, kk, :], hkey(1 + c0)])
                    for kk in range(8):
                        k.mm(U[:, :n], slot[:, 1, kk, :], hT[:, kk, 1 + c0:1 + c0 + n], start=kk == 0, stop=kk == 7,
                             rk=[slot[:, 0, kk, :], hkey(1 + c0)])
                    e = tE[gi % 2]
                    k.act(e[:, :n], G[:, :n], AF.Silu)
                    k.tt("dve", aT[:, c, c0:c0 + n], U[:, :n], e[:, :n], ALU.mult)
            col = 0
            pend = None

            def do_hoist(p_):
                xt0, nt0, c0_ = p_
                self.prenorm(xt0, nt0, hoist[0], hT[:, :, 1 + c0_:1 + c0_ + nt0], wk=[hkey(1 + c0_), hT[:]])
            for i, (xt, nt) in enumerate(tiles):
                Y = [k.bank(), k.bank()]
                for j in range(2):
                    for c in range(NCH):
                        k.mm(Y[j][:nt, :], aT[:, c, col:col + nt], wdp[c // 2][:, c % 2, j * 512:(j + 1) * 512],
                             start=c == 0, stop=c == NCH - 1)
                if pend is not None:
                    do_hoist(pend)
                    pend = None
                self.postnorm_residual(Y, xt, nt, gpb, 0.5, tY)
                if after_tile is not None:
                    after_tile(i)
                if hoist is not None and i < hoist[1]:
                    pend = (xt, nt, col)
                col += nt
            if pend is not None:
                do_hoist(pend)

    def stop(self, n):
        if self.debug.get("m1lvl", 99) <= n:
            raise _Stop()

    def mixer1(self, *a):
        try:
            self._mixer1(*a)
        except _Stop:
            pass

    def _mixer1(self, tiles, hT, yaT, kind, last, ncols, bi):
        k = self.k
        smp = kind == "s"
        sfx = "_s" if smp else "_p"
        ns = 16 if smp else 2
        nsteps = 2 if smp else 6
        k.p.tag = "M1%s_pre" % sfx
        with k.scope():
            Wsg = []
            for g_ in range(4):
                t_ = k.sb("Ws%d" % g_, [128, 8, self.win_n[g_]], BF16)
                k.dma("pool", t_[:].rearrange("p k n -> p (k n)"), self.win[g_], "wsg%d" % g_)
                Wsg.append(t_)
            mu_b = k.sb("mu_b", [128, 1536], F32)
            k.dma("sp", mu_b[:], self.mu[0:1, 0:1536].partition_broadcast(128), "mub")
            bc = {}
            for n in ("w0", "a0", "k_k", "k_a", "r_k", "lnx_w", "lnx_b"):
                bc[n] = k.sb("bc_" + n, [128, 512], F32)
                k.dma("sp", bc[n][:], self.vec[n].partition_broadcast(128), "bc_" + n)
            w2a2 = k.sb("w2a2", [128, 512], BF16)
            k.dma("pool", w2a2[0:64, :], self.w2, "w2a2")
            k.dma("pool", w2a2[64:128, :], self.a2, "w2a2")
            g2b = k.sb("g2b", [128, 512], BF16)
            k.dma("pool", g2b[:], self.g2, "g2b")
            scr = k.sb("scr", [128, D], F32)
            hfull = scr
            f32t = lambda n, w=512: k.sb(n, [128, w], F32)
            b16t = lambda n, w=512: k.sb(n, [128, w], BF16)
            rkv = f32t("rkv", 1536)
            g2f = rkv[:, 0:D]
            k.dma("sp", g2f, self.normg[2:3, :].partition_broadcast(128), "g2f")
            dT = k.sb("dT", [128, 8, 128], BF16)
            lor = f32t("lor", 128)
            lwa = b16t("lwa", 128)
            lg = b16t("lg", 128)
            sg = f32t("sg")
            sghi = b16t("sghi")
            sglo = b16t("sglo")
            alr = f32t("alr")
            gg2 = [f32t("gg0"), f32t("gg1")]
            E1 = f32t("E1")
            dtmp = E1
            E2 = f32t("E2")
            kkt = sg
            bvec = kkt
            t1 = f32t("t1")
            kkn = f32t("kkn")
            kmod = f32t("kmod")
            st8 = k.sb("st8", [128, 64], F32)
            Rt2 = [b16t("Rt0"), b16t("Rt1")]
            At2 = [b16t("At0"), b16t("At1")]
            Bt2 = [b16t("Bt0"), b16t("Bt1")]
            Kt2 = [b16t("Kt0"), b16t("Kt1")]
            Bh2 = [b16t("Bh0"), b16t("Bh1")]
            Kh2 = [b16t("Kh0"), b16t("Kh1")]
            Vb2 = [b16t("Vb0"), b16t("Vb1")]
            ART_2 = [k.sb("ART_%d" % i, [128, 4, 2, 2, 64], BF16) for i in range(2)]
            BT = k.sb("BT", [128, 4, 128], BF16)
            KT = k.sb("KT", [128, 4, 128], BF16)
            A1_2 = [k.sb("A1_%d" % i, [128, 8, 128], BF16) for i in range(2)]
            A2_2 = [k.sb("A2_%d" % i, [128, 8, 128], BF16) for i in range(2)]
            Pt = [k.sb("Pt%d" % i, [128, 8, 64], BF16) for i in range(3)]
            PTt = [k.sb("PTt%d" % i, [128, 8, 64], BF16) for i in range(2)]
            W2s_2 = [k.sb("W2s_%d" % i, [128, 8, 64], F32) for i in range(2)]
            Xb = k.sb("Xb", [128, 8, 128], BF16)
            W1T = k.sb("W1T", [128, 4, 128], BF16)
            Ub = k.sb("Ub", [128, 8, 64], BF16)
            gC2 = [k.sb("gC%d" % i, [128, 4, 16], F32) for i in range(2)]
            stA2 = [k.sb("stA%d" % i, [128, 8], F32) for i in range(2)]
            ysb = scr[:, 0:512]
            yc = scr[:, 512:1024]
            ya = b16t("ya")
            mA = self.c("mA" + sfx)
            mN = self.c("mN" + sfx)
            triI, triX, triR = self.c("triI" + sfx), self.c("triX" + sfx), self.c("triR" + sfx)
            sel = self.c("sel" + sfx)
            if smp:
                hTs = k.sb("hTs", [128, 8, 16, 5], BF16)
                hTc = k.sb("hTc", [128, 8, 64], BF16)
                hTp = k.sb("hTp", [128, 8, 64], BF16)
                shs_t = k.sb("shs_t", [16, D], F32)
                shs_b = k.sb("shs_b", [16, D], BF16)
                Hs = [k.sb("Hs%d" % b, [128, 4, 64], F32) for b in range(16)]
                Hsb = k.sb("Hsb", [128, 16, 4, 64], BF16)
                Snat = [k.sb("Snat%d" % i, [64, 512], F32) for i in range(2)]
                W1Tm = k.sb("W1Tm", [128, 4, 16, 64], BF16)
                RTm = k.sb("RTm", [128, 4, 16, 64], BF16)
                Bhm = [Rt2[1], At2[1]]
                Khm = [Bt2[1], Kt2[1]]
                colmask = self.c("colmask").rearrange("p (b t) -> p b t", b=16)
                k.dma("sp", shs_t[:], self.sshift, "shs_t")
                k.copy("dve", shs_b[:], shs_t[:])
                bk = k.bank()
                psb = bk[:].bitcast(BF16)
                for kk in range(8):
                    k.tr(psb[:, kk * 16:(kk + 1) * 16], shs_b[:16, kk * 128:(kk + 1) * 128], self.identb[:16, :16])
                k.copy("dve", hTs[:, :, :, 0], psb[:, 0:128].rearrange("p (k b) -> p k b", k=8))

            def state_gen():
                for b in range(16):
                    k.p.tag = "M1_s_state"
                    sn = Snat[b % 2]
                    k.dma("sp", sn[:, :], self.swkv[b],
                          "snat%d" % (b % 2))
                    bk = k.bank(1)
                    for hp in range(4):
                        k.tr(bk[:, hp * 64:(hp + 1) * 64], sn[:64, hp * 128:(hp + 1) * 128], self.identf[:64, :64])
                    k.copy("act", Hs[b][:], bk[:, 0:256].rearrange("p (h v) -> p h v", h=4))
                    k.copy("dve", Hsb[:, b, :, :], Hs[b][:])
                    yield "s"

            self.stop(1)
            if not smp:
                k.copy("dve", hT[:, :, 0], self.hprev[:])
            col = 1
            for i, (xt_, nt) in enumerate(tiles):
                if smp:
                    self.prenorm(xt_, nt, 2, hTs[:, :, :, 1:5], sample_dst=True)
                    k.stt("dve", hfull[:nt, :], xt_[:nt, :], self.ss[:nt, 1:2], g2f[:nt, :], ALU.mult, ALU.mult)
                    for b in range(16):
                        k.dma("sp", self.shs[b:b + 1, :], hfull[4 * b + 3:4 * b + 4, :], "shs_o")
                    k.copy("dve", hTc[:].rearrange("p k (b t) -> p k b t", t=4), hTs[:, :, :, 1:5])
                    k.copy("dve", hTp[:].rearrange("p k (b t) -> p k b t", t=4), hTs[:, :, :, 0:4])
                    k.copy("dve", hT[:, :, 1:65], hTc[:])
                else:
                    if not (getattr(self, "m_pre_done", False) and not (last and i == len(tiles) - 1)):
                        self.prenorm(xt_, nt, 2, hT[:, :, col:col + nt])
                    if last and i == len(tiles) - 1:
                        k.stt("dve", hfull[:nt, :], xt_[:nt, :], self.ss[:nt, 1:2], g2f[:nt, :], ALU.mult, ALU.mult)
                        k.dma("sp", self.shp, hfull[127:128, :], "shp_o")
                col += nt
            if not smp:
                k.copy("dve", self.hprev[:], hT[:, :, ncols])

            self.stop(2)
            def tile_gen(ti, xt_, nt, col):
                k.p.tag = "M1%s_t%d" % (sfx, ti)
                pb_ = ti % 2
                Rt, At, Bt, Kt, Bh, Kh, Vb = Rt2[pb_], At2[pb_], Bt2[pb_], Kt2[pb_], Bh2[pb_], Kh2[pb_], Vb2[pb_]
                gg, gC, stA = gg2[pb_], gC2[pb_], stA2[pb_]
                A1, A2, ART, W2s = A1_2[pb_], A2_2[pb_], ART_2[pb_], W2s_2[pb_]
                if smp:
                    cur_ap = lambda kk: hTc[:, kk, :]
                    k.tt("dve", dT[:, :, :nt], hTp[:, :, :], hTc[:, :, :], ALU.subtract)
                else:
                    cur_ap = lambda kk, col=col: hT[:, kk, col:col + nt]
                    k.tt("dve", dT[:, :, :nt], hT[:, :, col - 1:col - 1 + nt], hT[:, :, col:col + nt], ALU.subtract)
                prv_ap = lambda kk: dT[:, kk, :nt]
                for gi in range(3):
                    cs_ = slice(gi * 512, (gi + 1) * 512)
                    cu, pv = k.bank(0), k.bank(0)
                    for kk in range(8):
                        k.mm(cu[:nt, :], cur_ap(kk), Wsg[gi][:, kk, :], start=kk == 0, stop=kk == 7)
                    for kk in range(8):
                        k.mm(pv[:nt, :], prv_ap(kk), Wsg[gi][:, kk, :], start=kk == 0, stop=kk == 7)
                    k.tt("dve", dtmp[:nt, :], pv[:nt, :], mu_b[:nt, cs_], ALU.mult)
                    k.tt("dve", rkv[:nt, cs_], cu[:nt, :], dtmp[:nt, :], ALU.add)
                    yield "a"
                    k.p.tag = "M1%s_t%d" % (sfx, ti)
                for fc in range(2):
                    cs_ = slice(1536 + fc * 128, 1536 + (fc + 1) * 128)
                    cu, pv = k.bank(0), k.bank(0)
                    for kk in range(8):
                        k.mm(cu[:, :nt], Wsg[3][:, kk, fc * 128:(fc + 1) * 128], cur_ap(kk), start=kk == 0, stop=kk == 7)
                    for kk in range(8):
                        k.mm(pv[:, :nt], Wsg[3][:, kk, fc * 128:(fc + 1) * 128], prv_ap(kk), start=kk == 0, stop=kk == 7)
                    k.copy("act", lor[:, :nt], cu[:, :nt])
                    k.stt("dve", lor[:, :nt], pv[:, :nt], self.muT[:, fc:fc + 1], lor[:, :nt], ALU.mult, ALU.add)
                    if fc == 0:
                        k.act(lwa[0:64, :nt], lor[0:64, :nt], AF.Tanh)
                        k.copy("act", lwa[64:128, :nt], lor[64:128, :nt])
                    else:
                        k.act(lg[:, :nt], lor[:, :nt], AF.Sigmoid)
                    yield "a"
                    k.p.tag = "M1%s_t%d" % (sfx, ti)
                def sigm(dst, ps, bias_t):
                    k.tt("dve", dst[:nt, :], ps[:nt, :], bias_t[:nt, :], ALU.add)
                    k.act(dst[:nt, :], dst[:nt, :], AF.Sigmoid)
                Lw = k.bank(0)
                k.mm(Lw[:nt, :], lwa[0:64, :nt], w2a2[0:64, :])
                sigm(sg, Lw, bc["w0"])
                La = k.bank(0)
                k.mm(La[:nt, :], lwa[64:128, :nt], w2a2[64:128, :], tp=(64, 0))
                sigm(alr, La, bc["a0"])
                Gp = k.bank(0)
                k.mm(Gp[:nt, :], lg[:, :nt], g2b[:, :])
                k.copy("act", gg[:nt, :], Gp[:nt, :])
                k.copy("act", sghi[:nt, :], sg[:nt, :])
                k.tt("dve", sglo[:nt, :], sg[:nt, :], sghi[:nt, :], ALU.subtract)
                r_ = rkv[:nt, 0:512]
                k_ = rkv[:nt, 512:1024]
                v_ = rkv[:nt, 1024:1536]
                h8 = lambda ap: ap.rearrange("p (h j) -> p h j", h=8)
                bc8 = lambda ap: ap.unsqueeze(2).to_broadcast([nt, 8, 64])
                yield "a"
                k.p.tag = "M1%s_t%d" % (sfx, ti)
                k.tt("dve", kkt[:nt, :], k_, bc["k_k"][:nt, :], ALU.mult)
                k.tt("dve", t1[:nt, :], kkt[:nt, :], kkt[:nt, :], ALU.mult)
                k.red(st8[:nt, 0:8], h8(t1[:nt, :]))
                k.ts("dve", st8[:nt, 0:8], st8[:nt, 0:8], 1e-24, None, ALU.max)
                k.act(st8[:nt, 8:16], st8[:nt, 0:8], AF.Ln)
                k.act(st8[:nt, 8:16], st8[:nt, 8:16], AF.Exp, scale=-0.5)
                k.tt("dve", h8(kkn[:nt, :]), h8(kkt[:nt, :]), bc8(st8[:nt, 8:16]), ALU.mult)
                k.stt("dve", t1[:nt, :], alr[:nt, :], -1.0, bc["k_a"][:nt, :], ALU.add, ALU.mult)
                k.ts("dve", t1[:nt, :], t1[:nt, :], 1.0, None, ALU.add)
                k.tt("dve", kmod[:nt, :], k_, t1[:nt, :], ALU.mult)
                k.tt("dve", bvec[:nt, :], kkn[:nt, :], alr[:nt, :], ALU.mult)
                k.tt("dve", t1[:nt, :], r_, kmod[:nt, :], ALU.mult)
                k.tt("dve", t1[:nt, :], t1[:nt, :], bc["r_k"][:nt, :], ALU.mult)
                k.red(stA[:nt, 0:8], h8(t1[:nt, :]))
                k.copy("act", Vb[:nt, :], v_)
                yield "a"
                k.p.tag = "M1%s_t%d" % (sfx, ti)

                def cums(tri):
                    pb = k.bank(0)
                    k.mm(pb[:nt, :], tri[:nt, :nt], sghi[:nt, :], start=True, stop=False)
                    k.mm(pb[:nt, :], tri[:nt, :nt], sglo[:nt, :], start=False, stop=True)
                    return pb
                csI = cums(triI)
                k.act(E1[:nt, :], csI[:nt, :], AF.Exp, scale=-C0)
                k.tt("dve", Rt[:nt, :], r_, E1[:nt, :], ALU.mult)
                k.act(E2[:nt, :], csI[:nt, :], AF.Exp, scale=C0)
                k.tt("dve", Bt[:nt, :], bvec[:nt, :], E2[:nt, :], ALU.mult)
                k.tt("dve", Kt[:nt, :], kmod[:nt, :], E2[:nt, :], ALU.mult)
                csX = cums(triX)
                k.act(E1[:nt, :], csX[:nt, :], AF.Exp, scale=-C0)
                k.stt("dve", At[:nt, :], kkn[:nt, :], -1.0, E1[:nt, :], ALU.mult, ALU.mult)
                csR = cums(triR)
                k.act(E2[:nt, :], csR[:nt, :], AF.Exp, scale=-C0)
                k.tt("dve", Bh[:nt, :], bvec[:nt, :], E2[:nt, :], ALU.mult)
                k.tt("dve", Kh[:nt, :], kmod[:nt, :], E2[:nt, :], ALU.mult)
                gcp = k.bank(0)
                for hp in range(4):
                    k.mm(gcp[:, hp * 16:hp * 16 + ns], sghi[:nt, hp * 128:(hp + 1) * 128], sel[:nt, :ns], start=True, stop=False)
                    k.mm(gcp[:, hp * 16:hp * 16 + ns], sglo[:nt, hp * 128:(hp + 1) * 128], sel[:nt, :ns], start=False, stop=True)
                k.act(gC[:, :, :ns], gcp[:, 0:64].rearrange("p (h s) -> p h s", h=4)[:, :, :ns], AF.Exp, scale=-C0)
                yield "A_done"
                self.stop(3)
                k.p.tag = "M1%s_t%dB" % (sfx, ti)
                nch = 1 if smp else 2
                for (src, which) in ((At, 0), (Rt, 1), (Bt, 2), (Kt, 3)):
                    bk = k.bank(1)
                    psb = bk[:].bitcast(BF16)
                    for hp in range(4):
                        k.tr(psb[:, hp * 128:hp * 128 + nt], src[:nt, hp * 128:(hp + 1) * 128], self.identb[:nt, :nt])
                    pv4 = psb[:, 0:512].rearrange("p (h t) -> p h t", h=4)
                    if which < 2:
                        k.copy("act" if which == 0 else "dve", ART[:, :, 0:nch, which, :],
                               pv4[:, :, :nt].rearrange("p h (c t) -> p h c t", c=nch))
                    else:
                        k.copy("act" if which == 2 else "dve", (BT if which == 2 else KT)[:, :, :nt], pv4[:, :, :nt])
                self.stop(3.2)
                mA4 = mA[:nt, :].unsqueeze(1).to_broadcast([nt, 4, 128])
                hp2 = lambda t_: t_.rearrange("p (hp par) t -> p hp par t", par=2)
                for (LT, dstA) in ((BT, A1), (KT, A2)):
                    oo = [k.bank(1), k.bank(1)]
                    for c2 in range(nch):
                        rows = c2 * 64
                        for hp in range(4):
                            for par in range(2):
                                fp = par * 64
                                rhsAR = ART[fp:fp + 64, hp, c2, :, :].rearrange("p a t -> p (a t)")
                                k.mm(oo[par][rows:rows + 64, hp * 128:(hp + 1) * 128], LT[fp:fp + 64, hp, rows:rows + 64],
                                     rhsAR, tp=(fp, rows))
                    for par in range(2):
                        k.tt("dve", hp2(dstA[:nt, :, :])[:, :, par, :], oo[par][:nt, :].rearrange("p (h t) -> p h t", h=4), mA4,
                             ALU.mult)
                oN = [k.bank(1), k.bank(1)]
                for c2 in range(nch):
                    rows = c2 * 64
                    for hp in range(4):
                        for par in range(2):
                            fp = par * 64
                            k.mm(oN[par][rows:rows + 64, hp * 64:(hp + 1) * 64], ART[fp:fp + 64, hp, c2, 0, :],
                                 BT[fp:fp + 64, hp, rows:rows + 64], tp=(fp, rows))
                for par in range(2):
                    k.tt("dve", hp2(Pt[0][:nt, :, :])[:, :, par, :], oN[par][:nt, 0:256].rearrange("p (h t) -> p h t", h=4),
                         mN[:nt, :].unsqueeze(1).to_broadcast([nt, 4, 64]), ALU.mult)
                self.stop(3.6)
                Xps = [k.bank(1), k.bank(1)]
                pnb, ptnb = k.bank(1), k.bank(1)
                seen = set()
                for h in range(8):
                    for c2 in range(nch):
                        rows = c2 * 64
                        bkx = Xps[h // 4]
                        s0 = (h % 4) * 128
                        first = (h // 4, c2) not in seen
                        seen.add((h // 4, c2))
                        k.mm(bkx[rows:rows + 64, s0:s0 + 64], self.identb[rows:rows + 64, rows:rows + 64],
                             At[rows:rows + 64, h * 64:(h + 1) * 64], start=first, stop=True, tp=(rows, rows))
                        k.mm(bkx[rows:rows + 64, s0 + 64:s0 + 128], A2[rows:rows + 64, h, 0:64],
                             Vb[rows:rows + 64, h * 64:(h + 1) * 64], start=False, stop=True, tp=(rows, rows))
                for q in range(2):
                    k.copy("act" if q == 0 else "dve", Xb[:nt, q * 4:(q + 1) * 4, :],
                           Xps[q][:nt, :].rearrange("p (h t) -> p h t", h=4))
                self.stop(4)
                k.p.tag = "M1%s_t%dD" % (sfx, ti)
                Pc = Pt[0]
                PTc = None
                for step in range(nsteps):
                    lastst = step == nsteps - 1
                    if not lastst:
                        for h in range(8):
                            for c2 in range(nch):
                                rows = c2 * 64
                                lhsPT = A1[rows:rows + 64, h, 0:64] if PTc is None else PTc[rows:rows + 64, h, :]
                                k.mm(pnb[rows:rows + 64, h * 64:(h + 1) * 64], lhsPT, Pc[rows:rows + 64, h, :], tp=(rows, rows))
                        for h in range(8):
                            for c2 in range(nch):
                                rows = c2 * 64
                                rhsPT = A1[rows:rows + 64, h, 0:64] if PTc is None else PTc[rows:rows + 64, h, :]
                                k.mm(ptnb[rows:rows + 64, h * 64:(h + 1) * 64], Pc[rows:rows + 64, h, :], rhsPT, tp=(rows, rows))
                    for h in range(8):
                        for c2 in range(nch):
                            rows = c2 * 64
                            lhs = A1[rows:rows + 64, h, 0:64] if PTc is None else PTc[rows:rows + 64, h, :]
                            k.mm(Xps[h // 4][rows:rows + 64, (h % 4) * 128:(h % 4 + 1) * 128], lhs, Xb[rows:rows + 64, h, :],
                                 start=False, stop=True, tp=(rows, rows))
                    if not lastst:
                        Pn = Pt[1 + step % 2]
                        PTn = PTt[step % 2]
                        k.copy("act", Pn[:nt, :, :], pnb[:nt, :].rearrange("p (h t) -> p h t", h=8))
                        k.copy("act", PTn[:nt, :, :], ptnb[:nt, :].rearrange("p (h t) -> p h t", h=8))
                        Pc, PTc = Pn, PTn
                    k.copy("act", Xb[:nt, 0:4, :], Xps[0][:nt, :].rearrange("p (h t) -> p h t", h=4))
                    k.copy("dve", Xb[:nt, 4:8, :], Xps[1][:nt, :].rearrange("p (h t) -> p h t", h=4))
                    yield "d"
                    k.p.tag = "M1%s_t%dD" % (sfx, ti)
                self.stop(4.9)
                for q in range(2):
                    k.copy("act", W2s[:nt, q * 4:(q + 1) * 4, :],
                           Xps[q][:nt, :].rearrange("p (h t) -> p h t", h=4)[:, :, 64:128])
                self.stop(4.95)
                yield "D_done"
                k.p.tag = "M1%s_t%dC" % (sfx, ti)
                bk = k.bank(0)
                psb = bk[:].bitcast(BF16)
                k.copy("dve", ya[:nt, :].rearrange("p (h j) -> p h j", h=8), Xb[:nt, :, 0:64])
                for hp in range(4):
                    k.tr(psb[:, hp * 128:hp * 128 + nt], ya[:nt, hp * 128:(hp + 1) * 128], self.identb[:nt, :nt])
                k.copy("act", W1T[:, :, :nt], psb[:, 0:512].rearrange("p (h t) -> p h t", h=4)[:, :, :nt])

                self.stop(5)
                yield "c"
                k.p.tag = "M1%s_t%dC" % (sfx, ti)
                hpv = lambda t_: t_.rearrange("p (hp par) v -> p hp par v", par=2)

                def evac_y(r0, r1, Yp, YA):
                    k.copy("act", ysb[r0:r1, :], Yp[r0:r1, :])
                    for par in range(2):
                        yv = hpv(ysb[r0:r1, :].rearrange("p (h v) -> p h v", h=8))[:, :, par, :]
                        k.tt("dve", yv, yv, YA[par][r0:r1, 0:256].rearrange("p (h v) -> p h v", h=4), ALU.add)
                if not smp:
                    for c2 in range(2):
                        rows = c2 * 64
                        Up = [k.bank(0), k.bank(0)]
                        YA = [k.bank(0), k.bank(0)]
                        for hp in range(4):
                            for par in range(2):
                                fp = par * 64
                                k.mm(Up[par][rows:rows + 64, hp * 64:(hp + 1) * 64], W1T[fp:fp + 64, hp, rows:rows + 64],
                                     self.Hb[fp:fp + 64, hp, :], tp=(fp, rows))
                        for hp in range(4):
                            for par in range(2):
                                fp = par * 64
                                k.mm(YA[par][rows:rows + 64, hp * 64:(hp + 1) * 64], ART[fp:fp + 64, hp, c2, 1, :],
                                     self.Hb[fp:fp + 64, hp, :], tp=(fp, rows))
                        for par in range(2):
                            k.tt("dve", hpv(Ub[rows:rows + 64, :, :])[:, :, par, :],
                                 Up[par][rows:rows + 64, 0:256].rearrange("p (h v) -> p h v", h=4),
                                 hpv(W2s[rows:rows + 64, :, :])[:, :, par, :], ALU.add)
                        Hn = k.bank(0)
                        for h in range(8):
                            hp, fp = h // 2, (h % 2) * 64
                            ho = Hn[fp:fp + 64, hp * 64:(hp + 1) * 64]
                            k.mm(ho, Bh[rows:rows + 64, h * 64:(h + 1) * 64], Ub[rows:rows + 64, h, :], start=True, stop=False, tp=(rows, fp))
                            k.mm(ho, Kh[rows:rows + 64, h * 64:(h + 1) * 64], Vb[rows:rows + 64, h * 64:(h + 1) * 64], start=False,
                                 stop=True, tp=(rows, fp))
                        k.tt("dve", self.Hst[:], self.Hst[:], gC[:, :, c2:c2 + 1].to_broadcast([128, 4, 64]), ALU.mult)
                        k.tt("dve", self.Hst[:], self.Hst[:], Hn[:, 0:256].rearrange("p (h v) -> p h v", h=4), ALU.add)
                        k.copy("act", self.Hb[:], self.Hst[:])
                        Yp = k.bank(0)
                        for h in range(8):
                            yo = Yp[rows:rows + 64, h * 64:(h + 1) * 64]
                            k.mm(yo, A1[rows:rows + 64, h, 64:128], Ub[rows:rows + 64, h, :], start=True, stop=False, tp=(rows, rows))
                            k.mm(yo, A2[rows:rows + 64, h, 64:128], Vb[rows:rows + 64, h * 64:(h + 1) * 64], start=False, stop=True,
                                 tp=(rows, rows))
                        evac_y(rows, rows + 64, Yp, YA)
                        if c2 == 0:
                            yield "c"
                            k.p.tag = "M1%s_t%dC" % (sfx, ti)
                else:
                    k.tt("dve", W1Tm[:], W1T[:, :, 0:64].unsqueeze(2).to_broadcast([128, 4, 16, 64]),
                         colmask.unsqueeze(1).to_broadcast([128, 4, 16, 64]), ALU.mult)
                    k.tt("dve", RTm[:], ART[:, :, 0, 1, :].unsqueeze(2).to_broadcast([128, 4, 16, 64]),
                         colmask.unsqueeze(1).to_broadcast([128, 4, 16, 64]), ALU.mult)
                    Up = [k.bank(0), k.bank(0)]
                    YA = [k.bank(0), k.bank(0)]
                    for par in range(2):
                        fp = par * 64
                        for hp in range(4):
                            for b in range(16):
                                k.mm(Up[par][0:64, hp * 64:(hp + 1) * 64], W1Tm[fp:fp + 64, hp, b, :], Hsb[fp:fp + 64, b, hp, :],
                                     start=b == 0, stop=b == 15, tp=(fp, 0))
                        for hp in range(4):
                            for b in range(16):
                                k.mm(YA[par][0:64, hp * 64:(hp + 1) * 64], RTm[fp:fp + 64, hp, b, :], Hsb[fp:fp + 64, b, hp, :],
                                     start=b == 0, stop=b == 15, tp=(fp, 0))
                    for par in range(2):
                        k.tt("dve", hpv(Ub[0:64, :, :])[:, :, par, :], Up[par][0:64, 0:256].rearrange("p (h v) -> p h v", h=4),
                             hpv(W2s[0:64, :, :])[:, :, par, :], ALU.add)
                    Yp = k.bank(0)
                    for h in range(8):
                        yo = Yp[0:64, h * 64:(h + 1) * 64]
                        k.mm(yo, A1[0:64, h, 64:128], Ub[0:64, h, :], start=True, stop=False)
                        k.mm(yo, A2[0:64, h, 64:128], Vb[0:64, h * 64:(h + 1) * 64], start=False, stop=True)
                if smp:
                    evac_y(0, 64, Yp, YA)
                    for b in range(16):
                        bm, km = Bhm[b % 2], Khm[b % 2]
                        k.ts("dve", bm[0:64, :], Bh[0:64, :], sel[0:64, b:b + 1], None, ALU.mult)
                        k.ts("dve", km[0:64, :], Kh[0:64, :], sel[0:64, b:b + 1], None, ALU.mult)
                        Hn = k.bank(0)
                        for h in range(8):
                            hp, fp = h // 2, (h % 2) * 64
                            ho = Hn[fp:fp + 64, hp * 64:(hp + 1) * 64]
                            k.mm(ho, bm[0:64, h * 64:(h + 1) * 64], Ub[0:64, h, :], start=True, stop=False, tp=(0, fp))
                            k.mm(ho, km[0:64, h * 64:(h + 1) * 64], Vb[0:64, h * 64:(h + 1) * 64], start=False, stop=True, tp=(0, fp))
                        k.tt("dve", Hs[b][:], Hs[b][:], gC[:, :, b:b + 1].to_broadcast([128, 4, 64]), ALU.mult)
                        k.tt("dve", Hs[b][:], Hs[b][:], Hn[:, 0:256].rearrange("p (h v) -> p h v", h=4), ALU.add)
                        bk = k.bank(0)
                        for hp in range(4):
                            k.tr(bk[0:64, hp * 128:(hp + 1) * 128], Hs[b][:, hp, :], self.identf[:, :])
                        sn = Snat[b % 2]
                        k.copy("act", sn[:, :], bk[0:64, :])
                        k.dma("sp", self.wks[b], sn[:, :], "snat%d" % (b % 2))

                self.stop(6)
                yield "C_done"
                k.p.tag = "M1%s_t%dO" % (sfx, ti)
                k.red(st8[:nt, 24:32], h8(ysb[:nt, :]))
                k.ts("dve", st8[:nt, 24:32], st8[:nt, 24:32], -1.0 / 64, None, ALU.mult)
                k.tt("dve", h8(yc[:nt, :]), h8(ysb[:nt, :]), bc8(st8[:nt, 24:32]), ALU.add)
                k.tt("dve", ysb[:nt, :], yc[:nt, :], yc[:nt, :], ALU.mult)
                k.red(st8[:nt, 32:40], h8(ysb[:nt, :]))
                k.act(st8[:nt, 40:48], st8[:nt, 32:40], AF.Ln, scale=1.0 / 64, bias=64e-5)
                k.act(st8[:nt, 40:48], st8[:nt, 40:48], AF.Exp, scale=-0.5)
                yield "o"
                k.p.tag = "M1%s_t%dO" % (sfx, ti)
                k.tt("dve", h8(yc[:nt, :]), h8(yc[:nt, :]), bc8(st8[:nt, 40:48]), ALU.mult)
                k.tt("dve", yc[:nt, :], yc[:nt, :], bc["lnx_w"][:nt, :], ALU.mult)
                k.tt("dve", yc[:nt, :], yc[:nt, :], bc["lnx_b"][:nt, :], ALU.add)
                k.tt("dve", h8(ysb[:nt, :]), h8(Vb[:nt, :]), bc8(stA[:nt, 0:8]), ALU.mult)
                k.tt("dve", yc[:nt, :], yc[:nt, :], ysb[:nt, :], ALU.add)
                k.tt("dve", ya[:nt, :], yc[:nt, :], gg[:nt, :], ALU.mult)
                yield "o"
                k.p.tag = "M1%s_t%dO" % (sfx, ti)
                bk = k.bank(0)
                psb = bk[:].bitcast(BF16)
                for m in range(4):
                    k.tr(psb[:, m * 128:m * 128 + nt], ya[:nt, m * 128:(m + 1) * 128], self.identb[:nt, :nt])
                k.copy("act", yaT[:, :, col - 1:col - 1 + nt], psb[:, 0:512].rearrange("p (h t) -> p h t", h=4)[:, :, :nt])

            def adv(g, until):
                for tok in g:
                    if tok in until:
                        return tok
                return None
            gens = []
            col = 1
            for ti, (xt_, nt) in enumerate(tiles):
                gens.append(tile_gen(ti, xt_, nt, col))
                col += nt
            if smp:
                sg_ = state_gen()
                a_done = False
                s_done = False
                while not (a_done and s_done):
                    if not s_done:
                        for _ in range(2):
                            if adv(sg_, ("s",)) is None:
                                s_done = True
                                break
                    if not a_done:
                        if adv(gens[0], ("a", "A_done")) == "A_done":
                            a_done = True
            else:
                adv(gens[0], ("A_done",))
            pending_out = None
            for ti in range(len(gens)):
                g = gens[ti]
                nx = gens[ti + 1] if ti + 1 < len(gens) else None
                nx_done = nx is None
                nd = 0
                while True:
                    tok = adv(g, ("d", "D_done", "C_done", "c"))
                    if tok is None:
                        break
                    if tok == "c":
                        continue
                    if tok == "d":
                        nd += 1
                        if pending_out is not None and nd >= 2:
                            if adv(pending_out, ("o",)) is None:
                                pending_out = None
                        for _rep in range(1):
                            if not nx_done:
                                if adv(nx, ("a", "A_done")) == "A_done":
                                    nx_done = True
                    if tok == "D_done":
                        if pending_out is not None:
                            adv(pending_out, ())
                            pending_out = None
                    if tok == "C_done":
                        pending_out = g
                        break
                if not nx_done:
                    adv(nx, ("A_done",))
            if pending_out is not None:
                adv(pending_out, ())
            if (not smp) and last:
                bk = k.bank(1)
                for hp in range(4):
                    k.tr(bk[0:64, hp * 128:(hp + 1) * 128], self.Hst[:, hp, :], self.identf[:, :])
                k.copy("act", scr[0:64, 0:512], bk[0:64, :])
                k.dma("sp", self.wkp, scr[0:64, 0:512], "wkp_o")

    def mixer2(self, tiles, hT, yaT, kind, last, ncols, bi):
        k = self.k
        smp = kind == "s"
        sfx = "_s" if smp else "_p"
        k.p.tag = "M2%s_pre" % sfx
        hoist_f2 = (not smp) and HOIST_F2 and ("f2" in self.debug.get("stages", ("f1", "m1", "m2", "f2")))
        if hoist_f2:
            self.f2_pre_done = len(tiles)
        with k.scope():
            Wrg = []
            for g_ in range(4):
                t_ = k.sb("Wr%d" % g_, [128, 8, 512], BF16)
                k.dma("pool", t_[:].rearrange("p k n -> p (k n)"), self.win[4 + g_], "wrg%d" % g_)
                Wrg.append(t_)
            Wo = k.sb("Wo", [128, 8, D], BF16)
            k.dma("pool", Wo[:], self.w_out.rearrange("(k p) n -> p k n", p=128), "wo")
            gpb = k.sb("gpb2", [128, D], F32)
            k.dma("sp", gpb[:], self.normg[3:4, :].partition_broadcast(128), "gpb2")
            gnw = k.sb("gnw", [128, 512], F32)
            k.dma("sp", gnw[:], self.vec["gn_w"].partition_broadcast(128), "gnw")
            nbuf = 1 if smp else 2
            cst = [k.sb("cst%d" % i, [128, 128], F32) for i in range(2)]
            B2 = []
            for pb in range(nbuf):
                d = {}
                d["qkvg"] = k.sb("qkvg%d" % pb, [128, 2048], F32)
                d["ta"] = k.sb("ta%d" % pb, [128, 256], F32)
                d["tb"] = k.sb("tb%d" % pb, [128, 256], F32)
                for n in ("qrot", "krot", "kd", "vb", "yr"):
                    d[n] = k.sb("%s%d" % (n, pb), [128, 512], BF16)
                for n in ("inm", "qT", "kT", "qdT", "yrT"):
                    d[n] = k.sb("%s%d" % (n, pb), [128, 4, 128], BF16)
                for n in ("ysb", "yc", "eg"):
                    d[n] = k.sb("%s2_%d" % (n, pb), [128, 512], F32)
                d["st4"] = k.sb("st4_%d" % pb, [128, 32], F32)
                d["tY"] = [k.sb("tY2%d_%d" % (i, pb), [128, 512], F32) for i in range(2)]
                B2.append(d)
            dm = self.c("dm" + sfx)
            qdec = self.c("qdec" + sfx)
            kdec = self.kcd[:, 4:8] if smp else self.kcd[:, 0:4]
            cdec = self.kcd[:, 12:16] if smp else self.kcd[:, 8:12]
            sel = self.c("sel_s")
            if smp:
                Ss = [k.sb("Ss%d" % b, [128, 4, 128], F32) for b in range(16)]
                Ssb = k.sb("Ssb", [128, 16, 4, 128], BF16)
                qdTm = k.sb("qdTm", [128, 4, 16, 64], BF16)
                kdm = [k.sb("kdm%d" % i, [64, 512], BF16) for i in range(2)]
                colmask = self.c("colmask").rearrange("p (b t) -> p b t", b=16)
                for b in range(16):
                    k.dma("sp", Ss[b][:].rearrange("p h e -> p (h e)"), self.sret[b], "ssld%d" % b)
                    k.copy("act" if b % 2 == 0 else "dve", Ssb[:, b, :, :], Ss[b][:])
            def tile_gen(ti, xt_, nt, col):
                k.p.tag = "M2%s_t%d" % (sfx, ti)
                pl = ti % 2
                d = B2[ti % nbuf]
                qkvg, ta, tb, qrot, krot, kd, vb, yr = (d[n] for n in ("qkvg", "ta", "tb", "qrot", "krot", "kd", "vb", "yr"))
                inm, qT, kT, qdT, yrT = (d[n] for n in ("inm", "qT", "kT", "qdT", "yrT"))
                ysb, yc, eg, st4, tY = d["ysb"], d["yc"], d["eg"], d["st4"], d["tY"]
                ct = cst[ti % 2]
                if smp:
                    k.dma("sp", ct[:nt, :], self.css, "cst%d" % (ti % 2))
                else:
                    r0 = (bi * 8 + ti) * 128
                    k.dma("sp", ct[:nt, :], self.csp[r0:r0 + 128, :], "cst%d" % (ti % 2))
                cosb = ct[:nt, 0:64].unsqueeze(1).to_broadcast([nt, 4, 64])
                sinb = ct[:nt, 64:128].unsqueeze(1).to_broadcast([nt, 4, 64])
                tav = ta[:nt, :].rearrange("p (h d) -> p h d", h=4)
                tbv = tb[:nt, :].rearrange("p (h d) -> p h d", h=4)
                h4 = lambda ap: ap.rearrange("p (h e) -> p h e", h=4)
                bc4 = lambda ap: ap.unsqueeze(2).to_broadcast([nt, 4, 128])

                def rope(off, dst):
                    xv = qkvg[:nt, off:off + 512].rearrange("p (h a d) -> p h a d", h=4, a=2)
                    dv = dst[:nt, :].rearrange("p (h a d) -> p h a d", h=4, a=2)
                    k.tt("dve", tav, xv[:, :, 0, :], cosb, ALU.mult)
                    k.tt("dve", tbv, xv[:, :, 1, :], sinb, ALU.mult)
                    k.tt("dve", dv[:, :, 0, :], tav, tbv, ALU.subtract)
                    k.tt("dve", tav, xv[:, :, 0, :], sinb, ALU.mult)
                    k.tt("dve", tbv, xv[:, :, 1, :], cosb, ALU.mult)
                    k.tt("dve", dv[:, :, 1, :], tav, tbv, ALU.add)

                def transp(src, dstT):
                    bk = k.bank(pl)
                    psb = bk[:].bitcast(BF16)
                    for h in range(4):
                        k.tr(psb[:, h * 128:h * 128 + nt], src[:nt, h * 128:(h + 1) * 128], self.identb[:nt, :nt])
                    k.copy("act", dstT[:, :, :nt], psb[:, 0:512].rearrange("p (h t) -> p h t", h=4)[:, :, :nt])
                for gi in range(4):
                    bk = k.bank(pl)
                    for kk in range(8):
                        k.mm(bk[:nt, :], hT[:, kk, col:col + nt], Wrg[gi][:, kk, :], start=kk == 0, stop=kk == 7,
                             rk=[("hTt", ti), Wrg[gi][:]])
                    k.copy("act", qkvg[:nt, gi * 512:(gi + 1) * 512], bk[:nt, :])
                    if gi == 1:
                        rope(0, qrot)
                    if gi == 2:
                        rope(512, krot)
                        k.tt("dve", h4(kd[:nt, :]), h4(krot[:nt, :]), bc4(kdec[:nt, :]), ALU.mult)
                        transp(qrot, qT)
                    if gi == 3:
                        k.copy("act", vb[:nt, :], qkvg[:nt, 1024:1536])
                        transp(krot, kT)
                    yield "y"
                    k.p.tag = "M2%s_t%d" % (sfx, ti)

                yield "y"
                k.p.tag = "M2%s_t%d" % (sfx, ti)
                qdv = qdec.rearrange("p (h t) -> p h t", h=4)
                k.tt("dve", qdT[:, :, :nt], qT[:, :, :nt], qdv, ALU.mult)
                bk = k.bank(pl)
                for h in range(4):
                    k.mm(bk[:nt, h * 128:h * 128 + nt], kT[:, h, :nt], qT[:, h, :nt])
                k.tt("dve", inm[:nt, :, :nt], bk[:nt, :].rearrange("p (h t) -> p h t", h=4)[:, :, :nt],
                     dm[:nt, :].rearrange("p (h t) -> p h t", h=4), ALU.mult)
                yield "S0"
                k.p.tag = "M2%s_t%d" % (sfx, ti)
                Yb = k.bank(pl)
                if smp:
                    k.tt("dve", qdTm[:], qdT[:, :, 0:64].unsqueeze(2).to_broadcast([128, 4, 16, 64]),
                         colmask.unsqueeze(1).to_broadcast([128, 4, 16, 64]), ALU.mult)
                for h in range(4):
                    yo = Yb[:nt, h * 128:(h + 1) * 128]
                    k.mm(yo, inm[:nt, h, :nt], vb[:nt, h * 128:(h + 1) * 128], start=True, stop=False)
                    if smp:
                        for b in range(16):
                            k.mm(yo, qdTm[:, h, b, :], Ssb[:, b, h, :], start=False, stop=b == 15)
                    else:
                        k.mm(yo, qdT[:, h, :nt], self.Sb[:, h, :], start=False, stop=True)
                k.copy("act", ysb[:nt, :], Yb[:nt, :])
                cdb = cdec.unsqueeze(2).to_broadcast([128, 4, 128])
                if smp:
                    for b in range(16):
                        km = kdm[b % 2]
                        k.ts("dve", km[:, :], kd[0:64, :], sel[0:64, b:b + 1], None, ALU.mult)
                        Sn = k.bank(pl)
                        for h in range(4):
                            k.mm(Sn[:, h * 128:(h + 1) * 128], km[0:64, h * 128:(h + 1) * 128], vb[0:64, h * 128:(h + 1) * 128])
                        k.tt("dve", Ss[b][:], Ss[b][:], cdb, ALU.mult)
                        k.tt("dve", Ss[b][:], Ss[b][:], Sn[:, :].rearrange("p (h e) -> p h e", h=4), ALU.add)
                        k.dma("sp", self.rts[b], Ss[b][:].rearrange("p h e -> p (h e)"), "rts_o%d" % (b % 4))
                else:
                    Sn = k.bank(pl)
                    for h in range(4):
                        k.mm(Sn[:, h * 128:(h + 1) * 128], kd[:nt, h * 128:(h + 1) * 128], vb[:nt, h * 128:(h + 1) * 128])
                    k.tt("dve", self.Sst[:], self.Sst[:], cdb, ALU.mult)
                    k.tt("dve", self.Sst[:], self.Sst[:], Sn[:, :].rearrange("p (h e) -> p h e", h=4), ALU.add)
                    k.copy("act", self.Sb[:], self.Sst[:])
                yield "S1"
                k.p.tag = "M2%s_t%d" % (sfx, ti)
                k.red(st4[:nt, 0:4], h4(ysb[:nt, :]))
                k.ts("dve", st4[:nt, 0:4], st4[:nt, 0:4], -1.0 / 128, None, ALU.mult)
                k.tt("dve", h4(yc[:nt, :]), h4(ysb[:nt, :]), bc4(st4[:nt, 0:4]), ALU.add)
                k.tt("dve", ysb[:nt, :], yc[:nt, :], yc[:nt, :], ALU.mult)
                k.red(st4[:nt, 4:8], h4(ysb[:nt, :]))
                k.act(st4[:nt, 8:12], st4[:nt, 4:8], AF.Ln, scale=1.0 / 128, bias=1e-5)
                k.act(st4[:nt, 8:12], st4[:nt, 8:12], AF.Exp, scale=-0.5)
                k.tt("dve", h4(yc[:nt, :]), h4(yc[:nt, :]), bc4(st4[:nt, 8:12]), ALU.mult)
                k.tt("dve", yc[:nt, :], yc[:nt, :], gnw[:nt, :], ALU.mult)

                yield "y"
                k.p.tag = "M2%s_t%d" % (sfx, ti)
                g_ = qkvg[:nt, 1536:2048]
                k.act(eg[:nt, :], g_, AF.Silu)
                k.tt("dve", yr[:nt, :], eg[:nt, :], yc[:nt, :], ALU.mult)
                bk = k.bank(pl)
                psb = bk[:].bitcast(BF16)
                for m in range(4):
                    k.tr(psb[:, m * 128:m * 128 + nt], yr[:nt, m * 128:(m + 1) * 128], self.identb[:nt, :nt])
                k.copy("act", yrT[:, :, :nt], psb[:, 0:512].rearrange("p (h t) -> p h t", h=4)[:, :, :nt])

                yield "y"
                k.p.tag = "M2%s_t%d" % (sfx, ti)
                Y = [k.bank(pl), k.bank(pl)]
                for j in range(2):
                    for m in range(8):
                        lhs = yaT[:, m, col - 1:col - 1 + nt] if m < 4 else yrT[:, m - 4, :nt]
                        k.mm(Y[j][:nt, :], lhs, Wo[:, m, j * 512:(j + 1) * 512], start=m == 0, stop=m == 7)
                self.postnorm_residual(Y, xt_, nt, gpb, 1.0, tY)
                if hoist_f2:
                    self.prenorm(xt_, nt, 4, hT[:, :, col:col + nt], wk=[("hTt", ti)], pool=pl)

            def adv(g, until):
                for tok in g:
                    if tok in until:
                        return tok
                return None
            gens = []
            col = 1
            for ti, (xt_, nt) in enumerate(tiles):
                gens.append(tile_gen(ti, xt_, nt, col))
                col += nt
            adv(gens[0], ("S0",))
            for ti in range(len(gens)):
                cur = gens[ti]
                nxt = gens[ti + 1] if ti + 1 < len(gens) else None
                adv(cur, ("S1",))
                cur_done = False
                nxt_ready = nxt is None
                while not (cur_done and nxt_ready):
                    if not nxt_ready:
                        if adv(nxt, ("y", "S0")) == "S0":
                            nxt_ready = True
                    if not cur_done:
                        if adv(cur, ("y",)) is None:
                            cur_done = True
            if (not smp) and last:
                k.dma("sp", self.rtp, self.Sst[:].rearrange("p h e -> p (h e)"), "rtp_o")


_CACHE = {}


def _get_nc(debug=None):
    key = repr(sorted((debug or {}).items()))
    if key not in _CACHE:
        b = Builder(debug)
        b.build()
        _CACHE[key] = b
    return _CACHE[key]


def _in_maps(inp):
    f = lambda a: np.ascontiguousarray(np.asarray(a, dtype=np.float32))
    ct = lambda a: f(f(a)[0].reshape(8, 128, NCH, 128).transpose(2, 1, 0, 3).reshape(NCH, 128, 1024))
    gu = lambda a, b: f(np.stack([ct(a), ct(b)], axis=2).reshape(NCH, 128, 2048))
    dl = lambda a: f(f(a)[0].reshape(NCH // 2, 2, 128, D).transpose(0, 2, 1, 3).reshape(NCH // 2, 128, 2 * D))
    cs_p, cs_s = _rope_tables()
    ng = f(inp["norm_g"])[0]
    shared = {
        "gT": f(ng.reshape(6, 8, 128).transpose(2, 0, 1).reshape(128, 48)),
        "normg": ng,
        "f1gu": gu(inp["ffn1_wg"], inp["ffn1_wu"]), "f1d": dl(inp["ffn1_wd"]),
        "f2gu": gu(inp["ffn2_wg"], inp["ffn2_wu"]), "f2d": dl(inp["ffn2_wd"]),
        "w_out": f(inp["w_out"])[0],
        "mu": f(inp["mu_shift"]),
        "muT": f(f(inp["mu_shift"])[0, 1536:1792].reshape(2, 128).T),
        "w0": f(inp["w0"]), "a0": f(inp["a0"]), "k_k": f(inp["k_k"]), "k_a": f(inp["k_a"]),
        "r_k": f(inp["r_k"]).reshape(1, 512), "lnx_w": f(inp["lnx_w"]), "lnx_b": f(inp["lnx_b"]),
        "gn_w": f(inp["ret_gn_w"]),
        "w2": f(inp["w2"])[0], "a2": f(inp["a2"])[0], "g2": f(inp["g2"])[0],
        "cpack": CPACK, "cs_p": cs_p, "cs_s": cs_s,
    }
    wi = f(inp["w_in"])[0]
    a_ = 0
    for g_, n_ in enumerate((512, 512, 512, 256, 512, 512, 512, 512)):
        shared["win%d" % g_] = f(wi[:, a_:a_ + n_].reshape(8, 128, n_).transpose(1, 0, 2).reshape(128, 8 * n_))
        a_ += n_
    xp = f(inp["x_prompt"])
    xs = f(inp["x_sample"])
    ssh = f(inp["state_shift"])[0]
    swk = f(inp["state_wkv"])[0]
    srt = f(inp["state_ret"])[0]
    maps = []
    for c in range(8):
        m = dict(shared)
        m["xp"] = xp[c]
        m["xs"] = f(xs[16 * c:16 * (c + 1)].reshape(64, D))
        m["sshift"] = f(ssh[16 * c:16 * (c + 1)])
        m["swkv"] = f(swk[16 * c:16 * (c + 1)].transpose(0, 2, 1, 3).reshape(16, 64, 512))
        m["sret"] = f(srt[16 * c:16 * (c + 1)].transpose(0, 2, 1, 3).reshape(16, 128, 512))
        maps.append(m)
    return maps


def kernel(**inputs):
    b = _get_nc()
    maps = _in_maps(inputs)
    res = run_bass_kernel_spmd(b.nc, maps, core_ids=list(range(8)))
    R = res.results
    yp = np.stack([R[c]["yp"] for c in range(8)]).astype(np.float32)
    ys = np.concatenate([R[c]["ys"].reshape(16, 4, D) for c in range(8)]).astype(np.float32)
    shp = np.stack([R[c]["shp"].reshape(D) for c in range(8)])[None].astype(np.float32)
    wkp = np.stack([R[c]["wkp"].reshape(64, 8, 64).transpose(1, 0, 2) for c in range(8)])[None].astype(np.float32)
    rtp = np.stack([R[c]["rtp"].reshape(128, 4, 128).transpose(1, 0, 2) for c in range(8)])[None].astype(np.float32)
    shs = np.concatenate([R[c]["shs"] for c in range(8)])[None].astype(np.float32)
    wks = np.concatenate([R[c]["wks"].reshape(16, 64, 8, 64).transpose(0, 2, 1, 3) for c in range(8)])[None].astype(np.float32)
    rts = np.concatenate([R[c]["rts"].reshape(16, 128, 4, 128).transpose(0, 2, 1, 3) for c in range(8)])[None].astype(np.float32)
    return (yp, ys, shp, wkp, rtp, shs, wks, rts)
```
